# Optimizing a Trainium2 kernel written in Bass

```python
import jax, jax.numpy as jnp
from jax import lax
import numpy as np

D_MODEL = 1024
BATCH = 16
SEQ = 256
DEPTH = 2
DEC_BATCH = 2
DEC_SEQ = 2048
PAST_LEN = 256

GRID_W = 64
N_HEADS = 8
QK_NOPE = 64
QK_ROPE = 32
V_HEAD = 64
Q_LORA = 384
KV_LORA = 256
AXIS_ROPE = QK_ROPE // 2
ROPE_THETA = 10000.0
Q_BLOCK = 128
W_CONF = D_MODEL // 4
CONF_K = 31
W_SC = D_MODEL // 4
SC_K = 3
W_FN = D_MODEL // 4
FN_GROUPS = 4
FN_GROUP_W = W_FN // FN_GROUPS
N_BRANCH = 4
FF_HIDDEN = ((8 * D_MODEL // 3 + 255) // 256) * 256
EPS = 1e-6
OFF_QA = 0
OFF_KVA = OFF_QA + Q_LORA
OFF_CONF = OFF_KVA + KV_LORA + QK_ROPE
OFF_SC = OFF_CONF + 2 * W_CONF
OFF_FN = OFF_SC + 3 * W_SC
OFF_GATE = OFF_FN + W_FN
IN_COLS = OFF_GATE + N_BRANCH * D_MODEL

kernel_name = 'hybrid_mla_conv_fourier_diffusion_step'


def _rmsnorm(x, g):
    xf = x.astype(jnp.float32)
    y = xf * lax.rsqrt(jnp.mean(xf * xf, axis=-1, keepdims=True) + EPS)
    return (y * g.astype(jnp.float32)).astype(x.dtype)


def _layernorm(x, g, b):
    xf = x.astype(jnp.float32)
    mu = jnp.mean(xf, axis=-1, keepdims=True)
    var = jnp.mean(jnp.square(xf - mu), axis=-1, keepdims=True)
    y = (xf - mu) * lax.rsqrt(var + EPS)
    return (y * g.astype(jnp.float32) + b.astype(jnp.float32)).astype(x.dtype)


def _dwconv(x, w):
    k = w.shape[0]
    return lax.conv_general_dilated(
        x, w[:, None, :].astype(x.dtype), window_strides=(1,),
        padding=[(k // 2, k // 2)], dimension_numbers=('NWC', 'WIO', 'NWC'),
        feature_group_count=x.shape[-1])


def _axial_rope_tables(n_tokens):
    rows = n_tokens // GRID_W
    row_pos = jnp.repeat(jnp.arange(rows, dtype=jnp.float32), GRID_W)
    col_pos = jnp.tile(jnp.arange(GRID_W, dtype=jnp.float32), rows)
    inv = ROPE_THETA ** (-jnp.arange(0, AXIS_ROPE, 2, dtype=jnp.float32) / AXIS_ROPE)
    ang_r = row_pos[:, None] * inv
    ang_c = col_pos[:, None] * inv
    return (jnp.cos(ang_r), jnp.sin(ang_r), jnp.cos(ang_c), jnp.sin(ang_c))


def _rotate(x, cos, sin):
    half = x.shape[-1] // 2
    x1, x2 = x[..., :half], x[..., half:]
    cos = cos.astype(x.dtype)
    sin = sin.astype(x.dtype)
    return jnp.concatenate([x1 * cos - x2 * sin, x2 * cos + x1 * sin], axis=-1)


def _apply_axial_rope(x, tabs):
    cr, sr, cc, sc = tabs
    shape = (x.shape[1],) + (1,) * (x.ndim - 3) + (AXIS_ROPE // 2,)
    xr = _rotate(x[..., :AXIS_ROPE], cr.reshape(shape), sr.reshape(shape))
    xc = _rotate(x[..., AXIS_ROPE:], cc.reshape(shape), sc.reshape(shape))
    return jnp.concatenate([xr, xc], axis=-1)


def _attend(q, k, v):
    b, lq, h, dqk = q.shape
    scale = dqk ** -0.5
    nb = lq // Q_BLOCK
    qb = q.reshape(b, nb, Q_BLOCK, h, dqk).transpose(1, 0, 2, 3, 4)

    def one_block(qi):
        s = jnp.einsum('bqhd,bkhd->bhqk', qi, k, preferred_element_type=jnp.float32) * scale
        p = jax.nn.softmax(s, axis=-1).astype(v.dtype)
        return jnp.einsum('bhqk,bkhd->bqhd', p, v)

    o = lax.map(one_block, qb)
    return o.transpose(1, 0, 2, 3, 4).reshape(b, lq, h, v.shape[-1])


def _mla_keys(c_kv, k_rope, w_kvb):
    b, l, _ = c_kv.shape
    kv = (c_kv @ w_kvb).reshape(b, l, N_HEADS, QK_NOPE + V_HEAD)
    k = jnp.concatenate(
        [kv[..., :QK_NOPE], jnp.broadcast_to(k_rope[:, :, None, :], (b, l, N_HEADS, QK_ROPE))], axis=-1)
    return k, kv[..., QK_NOPE:]


def _fourier(u):
    b, l, _ = u.shape
    ug = u.reshape(b, l, FN_GROUPS, FN_GROUP_W).astype(jnp.float32)
    f = jnp.fft.fft2(ug, axes=(1, 3), norm='ortho').real
    return f.reshape(b, l, W_FN).astype(u.dtype)


def _token_mixers(h, lp, rope, ctx):
    b, l, _ = h.shape
    proj = h @ lp['w_in']
    q_a = proj[..., OFF_QA:OFF_KVA]
    c_kv = _rmsnorm(proj[..., OFF_KVA:OFF_KVA + KV_LORA], lp['g_kva'])
    k_rope = proj[..., OFF_KVA + KV_LORA:OFF_CONF]
    conf_in = proj[..., OFF_CONF:OFF_SC]
    sc_in = proj[..., OFF_SC:OFF_FN]
    fn_in = proj[..., OFF_FN:OFF_GATE]
    gates = jax.nn.sigmoid(proj[..., OFF_GATE:].reshape(b, l, N_BRANCH, D_MODEL))

    q = (_rmsnorm(q_a, lp['g_qa']) @ lp['w_qb']).reshape(b, l, N_HEADS, QK_NOPE + QK_ROPE)
    q_nope, q_rope = q[..., :QK_NOPE], q[..., QK_NOPE:]
    k_rope_pos = k_rope
    if rope is not None:
        q_rope = _apply_axial_rope(q_rope, rope)
        k_rope_pos = _apply_axial_rope(k_rope, rope)
    k, v = _mla_keys(c_kv, k_rope_pos, lp['w_kvb'])
    if ctx is not None:
        k_ctx, v_ctx = _mla_keys(ctx[0], ctx[1], lp['w_kvb'])
        k = jnp.concatenate([k_ctx, k], axis=1)
        v = jnp.concatenate([v_ctx, v], axis=1)
    o = _attend(jnp.concatenate([q_nope, q_rope], axis=-1), k, v)
    y_a = o.reshape(b, l, N_HEADS * V_HEAD) @ lp['w_o_mla']

    u = conf_in[..., :W_CONF] * jax.nn.sigmoid(conf_in[..., W_CONF:])
    u = _dwconv(u, lp['w_conf_dw']) + lp['b_conf_dw']
    u = jax.nn.silu(_layernorm(u, lp['g_conf_ln'], lp['b_conf_ln']))
    y_b = u @ lp['w_conf_pw']

    gb, gc, xs = jnp.split(sc_in, 3, axis=-1)
    y_c = (gb * _dwconv(gc * xs, lp['w_sc_conv'])) @ lp['w_sc_out']

    y_d = _fourier(fn_in) @ lp['w_fn']

    merged = (gates[..., 0, :] * y_a + gates[..., 1, :] * y_b
              + gates[..., 2, :] * y_c + gates[..., 3, :] * y_d)
    return merged @ lp['w_out'], (c_kv, k_rope)


def _swiglu(h, lp):
    return (jax.nn.silu(h @ lp['w_ffn_gate']) * (h @ lp['w_ffn_up'])) @ lp['w_ffn_down']


def _block(x, mod, lp, rope, ctx):
    sh1, sc1, g1, sh2, sc2, g2 = jnp.split(mod, 6, axis=-1)
    h = _rmsnorm(x, lp['g_norm1']) * (1 + sc1) + sh1
    y, ctx_kv = _token_mixers(h, lp, rope, ctx)
    x = x + g1 * y
    h = _rmsnorm(x, lp['g_norm2']) * (1 + sc2) + sh2
    x = x + g2 * _swiglu(h, lp)
    return x, ctx_kv


def setup_inputs(seed: int = 0) -> dict:
    key = jax.random.key(seed)
    ks = iter(jax.random.split(key, 40))

    def nrm(shape, scale):
        return jax.random.normal(next(ks), shape, jnp.float32) * scale

    def gain(shape):
        return 1.0 + nrm(shape, 0.02)

    d = D_MODEL
    return {
        'x_prompt': nrm((BATCH, SEQ, d), 1.0),
        'x_sample': nrm((DEC_BATCH, DEC_SEQ, d), 1.0),
        'cache_ckv': nrm((DEC_BATCH, DEPTH, PAST_LEN, KV_LORA), 1.0),
        'cache_krope': nrm((DEC_BATCH, DEPTH, PAST_LEN, QK_ROPE), 1.0),
        'c': nrm((DEC_BATCH, d), 1.0),
        'c_ctx': nrm((d,), 1.0),
        'w_ada': nrm((DEPTH, d, 6 * d), 0.5 * d ** -0.5),
        'b_ada': nrm((DEPTH, 6 * d), 0.01),
        'g_norm1': gain((DEPTH, d)),
        'g_norm2': gain((DEPTH, d)),
        'w_in': nrm((DEPTH, d, IN_COLS), d ** -0.5),
        'g_qa': gain((DEPTH, Q_LORA)),
        'w_qb': nrm((DEPTH, Q_LORA, N_HEADS * (QK_NOPE + QK_ROPE)), Q_LORA ** -0.5),
        'g_kva': gain((DEPTH, KV_LORA)),
        'w_kvb': nrm((DEPTH, KV_LORA, N_HEADS * (QK_NOPE + V_HEAD)), KV_LORA ** -0.5),
        'w_o_mla': nrm((DEPTH, N_HEADS * V_HEAD, d), (N_HEADS * V_HEAD) ** -0.5),
        'w_conf_dw': nrm((DEPTH, CONF_K, W_CONF), CONF_K ** -0.5),
        'b_conf_dw': nrm((DEPTH, W_CONF), 0.01),
        'g_conf_ln': gain((DEPTH, W_CONF)),
        'b_conf_ln': nrm((DEPTH, W_CONF), 0.01),
        'w_conf_pw': nrm((DEPTH, W_CONF, d), W_CONF ** -0.5),
        'w_sc_conv': nrm((DEPTH, SC_K, W_SC), SC_K ** -0.5),
        'w_sc_out': nrm((DEPTH, W_SC, d), W_SC ** -0.5),
        'w_fn': nrm((DEPTH, W_FN, d), W_FN ** -0.5),
        'w_out': nrm((DEPTH, d, d), d ** -0.5),
        'w_ffn_gate': nrm((DEPTH, d, FF_HIDDEN), d ** -0.5),
        'w_ffn_up': nrm((DEPTH, d, FF_HIDDEN), d ** -0.5),
        'w_ffn_down': nrm((DEPTH, FF_HIDDEN, d), FF_HIDDEN ** -0.5),
        'g_final': gain((d,)),
    }


def reference(x_prompt, x_sample, cache_ckv, cache_krope, c, c_ctx, w_ada, b_ada,
              g_norm1, g_norm2, w_in, g_qa, w_qb, g_kva, w_kvb, w_o_mla,
              w_conf_dw, b_conf_dw, g_conf_ln, b_conf_ln, w_conf_pw,
              w_sc_conv, w_sc_out, w_fn, w_out, w_ffn_gate, w_ffn_up, w_ffn_down, g_final):
    rope = _axial_rope_tables(x_sample.shape[1])
    xp = x_prompt
    xs = x_sample
    new_ckv = []
    new_krope = []
    for l in range(DEPTH):
        lp = {
            'g_norm1': g_norm1[l], 'g_norm2': g_norm2[l], 'w_in': w_in[l],
            'g_qa': g_qa[l], 'w_qb': w_qb[l], 'g_kva': g_kva[l], 'w_kvb': w_kvb[l],
            'w_o_mla': w_o_mla[l], 'w_conf_dw': w_conf_dw[l], 'b_conf_dw': b_conf_dw[l],
            'g_conf_ln': g_conf_ln[l], 'b_conf_ln': b_conf_ln[l], 'w_conf_pw': w_conf_pw[l],
            'w_sc_conv': w_sc_conv[l], 'w_sc_out': w_sc_out[l], 'w_fn': w_fn[l],
            'w_out': w_out[l], 'w_ffn_gate': w_ffn_gate[l], 'w_ffn_up': w_ffn_up[l],
            'w_ffn_down': w_ffn_down[l],
        }
        mod_ctx = (jax.nn.silu(c_ctx) @ w_ada[l] + b_ada[l])[None, None, :]
        mod_lat = (jax.nn.silu(c) @ w_ada[l] + b_ada[l])[:, None, :]
        xp, (ckv_l, krope_l) = _block(xp, mod_ctx, lp, None, None)
        new_ckv.append(ckv_l)
        new_krope.append(krope_l)
        xs, _ = _block(xs, mod_lat, lp, rope, (cache_ckv[:, l], cache_krope[:, l]))
    y_prompt = _rmsnorm(xp, g_final)
    y_sample = _rmsnorm(xs, g_final)
    ckv_out = jnp.stack(new_ckv, axis=1)
    krope_out = jnp.stack(new_krope, axis=1)
    return (y_prompt, y_sample, ckv_out, krope_out)
```

```python
import math
import os
from contextlib import ExitStack
import numpy as np
import concourse.bass as bass
import concourse.mybir as mybir
from concourse.bass_utils import run_bass_kernel_spmd

F32 = mybir.dt.float32
F32R = mybir.dt.float32r
AF = mybir.ActivationFunctionType
ALU = mybir.AluOpType

D = 1024
DEPTH = 2
NH = 8
FF = 2816
IN_COLS = 6304
OFF_GATE = 2208
EPS = 1e-6
EXROWS = 1056
EX_CKV, EX_KR, EX_FN, EX_UG, EX_VP = 0, 256, 288, 544, 800

SP_GN1, SP_GN2, SP_GQA, SP_GKVA, SP_BCDW, SP_GCLN, SP_BCLN, SP_WSC, SP_WCDW, SP_BADA = 0, 8, 16, 19, 21, 23, 25, 27, 33, 95
NSP_L = 143
SP_GFIN = 2 * NSP_L
SP_CCTX = SP_GFIN + 8
SP_CLAT = SP_CCTX + 8
NSP = SP_CLAT + 16
NCST = 2816
C_ID = 2688
C_ONES, C_R96, C_SEL, C_COS, C_SIN, C_BD, C_CL, C_NSL, C_EPS, C_ZERO = 0, 128, 224, 288, 800, 1312, 1568, 2080, 2592, 2624

SAME_ENGINE_SYNC = True


class Tile:
    def __init__(self, space, start, end, name):
        self.space, self.start, self.end, self.name = space, start, end, name
        self.last_w = None
        self.readers = []


class Prog:
    ENG = ("pe", "act", "dve", "pool", "sp")

    def __init__(self, nc):
        self.nc = nc
        self.ops = []
        self.tiles = {}
        self.keys = {}
        self.mute = False

    def tile(self, space, start, end, name):
        t = Tile(space, start, end, name)
        self.tiles.setdefault(space, []).append(t)
        return t

    def _overl(self, t):
        return [u for u in self.tiles[t.space] if u.start < t.end and t.start < u.end]

    def op(self, eng, fn, r=(), w=(), kind="c", nd=1, key=None):
        if self.mute:
            return -1
        i = len(self.ops)
        deps = set()
        for t in r:
            for u in self._overl(t):
                if u.last_w is not None:
                    deps.add(u.last_w)
        for t in w:
            for u in self._overl(t):
                if u.last_w is not None:
                    deps.add(u.last_w)
                deps.update(u.readers)
        deps.discard(i)
        self.ops.append(dict(eng=eng, fn=fn, deps=deps, kind=kind, nd=nd, key=key, inc=False))
        for t in r:
            t.readers.append(i)
        for t in w:
            t.last_w = i
            t.readers = []
        return i

    def emit(self, stack):
        nc = self.nc
        ops = self.ops
        for o in ops:
            for d in o["deps"]:
                src = ops[d]
                if src["kind"] == "c" and src["eng"] == o["eng"] and (o["eng"] == "pe" or not SAME_ENGINE_SYNC):
                    continue
                src["inc"] = True
        esem = {e: stack.enter_context(nc.semaphore("es_" + e)) for e in self.ENG}
        ksem = {}
        ecnt = {e: 0 for e in self.ENG}
        kcnt = {}
        for o in ops:
            if o["kind"] == "c":
                if o["inc"]:
                    ecnt[o["eng"]] += 1
                    o["tok"] = (esem[o["eng"]], ecnt[o["eng"]])
                else:
                    o["tok"] = None
            else:
                k = o["key"]
                if k not in ksem:
                    ksem[k] = stack.enter_context(nc.semaphore("ks_" + str(k)))
                    kcnt[k] = 0
                kcnt[k] += (16 * o["nd"]) if o["kind"] == "d" else 1
                o["tok"] = (ksem[k], kcnt[k])
        self.nsem = len(esem) + len(ksem)
        final = dict((id(ksem[k]), (ksem[k], kcnt[k])) for k in ksem)
        streams = {e: [o for o in ops if o["eng"] == e] for e in self.ENG}
        block = stack.enter_context(nc.Block())

        def run(eng_handle, ename):
            known = {}
            for o in streams[ename]:
                need = {}
                for d in o["deps"]:
                    src = ops[d]
                    if src["kind"] == "c" and src["eng"] == ename and (ename == "pe" or not SAME_ENGINE_SYNC):
                        continue
                    sem, cnt = src["tok"]
                    if need.get(id(sem), (None, 0))[1] < cnt:
                        need[id(sem)] = (sem, cnt)
                for sid, (sem, cnt) in need.items():
                    if known.get(sid, 0) < cnt:
                        eng_handle.wait_ge(sem, cnt)
                        known[sid] = cnt
                res = o["fn"](eng_handle)
                if o["kind"] == "c":
                    if o["inc"]:
                        res.then_inc(o["tok"][0], 1)
                elif o["kind"] == "d":
                    for ins in res:
                        ins.then_inc(o["tok"][0], 16)
                else:
                    res.then_inc(o["tok"][0])
            if ename == "pool":
                for sid, (sem, cnt) in final.items():
                    if known.get(sid, 0) < cnt:
                        eng_handle.wait_ge(sem, cnt)

        @block.sync
        def _(e):
            run(e, "sp")

        @block.tensor
        def _(e):
            run(e, "pe")

        @block.scalar
        def _(e):
            run(e, "act")

        @block.vector
        def _(e):
            run(e, "dve")

        @block.gpsimd
        def _(e):
            run(e, "pool")


def build_program(debug=False):
    nc = bass.Bass("TRN2", target_bir_lowering=False)
    nc.dge_precook = False
    P = Prog(nc)
    stack = ExitStack()

    def din(name, shape):
        return nc.dram_tensor(name, list(shape), F32, kind="ExternalInput").ap()

    def dout(name, shape):
        return nc.dram_tensor(name, list(shape), F32, kind="ExternalOutput").ap()

    xpT = din("xpT", [D, 512])
    xsT = din("xsT", [D, 512])
    cckvT = din("cckvT", [DEPTH, 256, 256])
    ckrT = din("ckrT", [DEPTH, 32, 256])
    smallp = din("smallp", [128, NSP])
    cst = din("cst", [128, NCST])
    dftS = din("dftS", [2, 2048, 512])
    w_ada_s = din("w_ada_s", [DEPTH, D, 1536])
    w_in = din("w_in", [DEPTH, D, IN_COLS])
    w_qb = din("w_qb", [DEPTH, 384, 768])
    w_kvb = din("w_kvb", [DEPTH, 256, 1024])
    w_out = din("w_out", [DEPTH, D, D])
    w_ffn_gate = din("w_ffn_gate", [DEPTH, D, FF])
    w_ffn_up = din("w_ffn_up", [DEPTH, D, FF])
    wmerge = din("wmerge", [DEPTH, 8, 128, 5888])
    wdown = din("wdown", [DEPTH, 2, 8, 128, 1408])

    ypT = dout("ypT", [D, 512])
    ysT = dout("ysT", [D, 512])
    nckvT = dout("nckvT", [DEPTH, 256, 512])
    nkrT = dout("nkrT", [DEPTH, 32, 512])

    RG = [[0, 1, 2, 3], [4, 5, 6, 7]]
    EXKV = nc.dram_tensor("exkv", [288, 512], F32).ap()
    GAKV = nc.dram_tensor("gakv", [4 * 288, 512], F32).ap()
    EXFN = nc.dram_tensor("exfn", [256, 512], F32).ap()
    GAFN = nc.dram_tensor("gafn", [4 * 256, 512], F32).ap()
    EXE = nc.dram_tensor("exe", [512, 32], F32).ap()
    GAE = nc.dram_tensor("gae", [6 * 512, 32], F32).ap()
    T_EXKV, T_GAKV = P.tile("d1", 0, 1, "EXKV"), P.tile("d2", 0, 1, "GAKV")
    T_EXFN, T_GAFN = P.tile("d3", 0, 1, "EXFN"), P.tile("d4", 0, 1, "GAFN")
    T_EXE, T_GAE = P.tile("d5", 0, 1, "EXE"), P.tile("d6", 0, 1, "GAE")
    SPL = nc.dram_tensor("spill", [128, 4736], F32).ap()
    T_SPL = P.tile("d9", 0, 1, "SPL")
    T_OUT = P.tile("dram_out", 0, 1, "OUT")

    NWR = 40640
    NWF = 12560
    SBR = stack.enter_context(nc.sbuf_tensor("SBR", [128, NWR], F32))
    SBF = stack.enter_context(nc.sbuf_tensor("SBF", [128, NWF], F32))
    PS = stack.enter_context(nc.psum_tensor("PS", [128, 8 * 512], F32))

    class V:
        def __init__(self, sp_, off, n, name):
            T_, lim = (SBR, NWR) if sp_ == "r" else (SBF, NWF)
            assert off + n <= lim, (name, off, n)
            self.off, self.n = off, n
            self.t = P.tile("sb" + sp_, off, off + n, name)
            self.f = T_[:, off:off + n]
            self.r = T_[:, off:off + n].bitcast(F32R)

    oo = {"r": 0, "f": 0}

    def take(n, name, sp_="r"):
        v = V(sp_, oo[sp_], n, name)
        oo[sp_] += n
        return v

    X = take(8 * 1024, "X", "f")
    CSTF = take(1088, "CSTF", "f")
    SPR = take(NSP, "SPR", "f")
    MOD = [take(96, "MOD%d" % l, "f") for l in range(DEPTH)]
    SCL = take(64, "SCL", "f")
    RS = take(512, "RS", "f")
    T1 = take(512, "T1", "f")
    T2 = take(512, "T2", "f")
    T3 = take(512, "T3", "f")
    IDENT = take(128, "IDENT", "f")
    HAL = take(128, "HAL", "f")

    RING = [take(3072, "ring%d" % i) for i in range(3)]
    CST = take(1568, "CST")
    SILC = take(32, "SILC")
    base = oo["r"]
    H = take(8 * 512, "H")
    QA = take(3 * 512, "QA")
    CKV = take(2 * 512, "CKV")
    KR96 = take(512, "KR96")
    UG = take(1144, "UG")
    VP = take(1032, "VP")
    GB = take(1024, "GB")
    FN = take(1024, "FN")
    F2 = take(1024, "F2")
    U2 = take(1024, "U2")
    OH = take(8 * 512, "OH")
    at0 = oo["r"]
    KH = take(8 * 512, "KH")
    VB = take(4 * 8 * 65, "VB")
    CKVB = take(1024, "CKVB")
    KRB = take(512, "KRB")
    ET = [take(512, "ET%d" % i) for i in range(3)]
    QH = V("r", H.off, 8 * 512, "QH")
    MERGED = V("r", at0, 8 * 512, "MERGED")
    FNFULL = V("r", at0, 4096, "FNFULL")
    AB = V("r", at0 + 4096, 8192, "AB")
    H2 = V("r", base, 8 * 1024, "H2")
    HID = V("r", base + 8 * 1024, 11 * 1024, "HID")
    OHh = [V("r", OH.off + h * 512, 512, "OHh%d" % h) for h in range(8)]
    ABt = [V("r", AB.off + i * 512, 512, "ABt%d" % i) for i in range(16)]
    KHq = [[V("r", KH.off + h * 512 + q * 256, 256, "KH%d_%d" % (h, q)) for q in range(2)] for h in range(8)]
    QHh = [V("r", QH.off + h * 512, 512, "QHh%d" % h) for h in range(8)]
    VBc = [V("r", VB.off + c * 520, 520, "VBc%d" % c) for c in range(4)]
    H2T = [[V("r", H2.off + c * 1024 + g * 512, 512, "H2_%d_%d" % (c, g)).t for g in range(2)] for c in range(8)]
    SQ = V("r", F2.off, 2048, "SQ")

    class TSet:
        pass
    TSM, TSA = TSet(), TSet()
    TSM.H, TSM.QA, TSM.CKV, TSM.KR96, TSM.UG, TSM.VP, TSM.GB, TSM.FN = H, QA, CKV, KR96, UG, VP, GB, FN
    ao = OH.off
    for nm, n in (("H", 4096), ("QA", 1536), ("CKV", 1024), ("KR96", 512), ("UG", 1144), ("VP", 1032), ("GB", 1024), ("FN", 1024)):
        setattr(TSA, nm, V("r", ao, n, nm + "_S"))
        ao += n
    assert ao <= NWR

    class B:
        def __init__(self, i):
            self.t = P.tile("ps", i * 512, (i + 1) * 512, "bank%d" % i)
            self.f = PS[:, i * 512:(i + 1) * 512]

    BANKS = [B(i) for i in range(8)]
    rr = {"a": 0, "b": 0, "c": 0}
    pools = {"a": [0, 1, 2], "b": [3, 4, 5], "c": [6, 7]}

    def bank(pool="a"):
        lst = pools[pool]
        b = BANKS[lst[rr[pool] % len(lst)]]
        rr[pool] += 1
        return b

    def dma(queue, key, pairs, r=(), w=()):
        def fn(e, pairs=pairs):
            return [e.dma_start(out=a, in_=b) for a, b in pairs]
        return P.op(queue, fn, r=r, w=w, kind="d", nd=len(pairs), key=key)

    ring_i = [0]

    def wload(pairs_fn):
        s = RING[ring_i[0] % len(RING)]
        k = "ring%d" % (ring_i[0] % len(RING))
        ring_i[0] += 1
        dma("sp", k, pairs_fn(s), w=[s.t])
        return s

    def mm(out_ap, pairs, r, w, plain=False):
        n = len(pairs)

        def fn(e, pairs=pairs, out_ap=out_ap):
            ins = None
            for i, (l, rh) in enumerate(pairs):
                ins = e.matmul(out_ap, lhsT=l, rhs=rh, start=(i == 0), stop=(i == n - 1))
            return ins
        return P.op("pe", fn, r=r, w=w)

    def act(out, in_, func, r, w, bias=None, scale=None):
        kw = {}
        if bias is not None:
            kw["bias"] = bias
        if scale is not None:
            kw["scale"] = scale
        return P.op("act", lambda e: e.activation(out, in_, func, **kw), r=r, w=w)

    def tt(eng, out, a, b, op, r, w):
        return P.op(eng, lambda e: e.tensor_tensor(out, a, b, op), r=r, w=w)

    def ts(eng, out, a, s1, s2, op0, op1, r, w):
        if s2 is None:
            return P.op(eng, lambda e: e.tensor_scalar(out, a, s1, None, op0), r=r, w=w)
        return P.op(eng, lambda e: e.tensor_scalar(out, a, s1, s2, op0, op1), r=r, w=w)

    def stt(eng, out, a, s, b, op0, op1, r, w):
        return P.op(eng, lambda e: e.scalar_tensor_tensor(out, a, s, b, op0, op1), r=r, w=w)

    def cp(eng, out, in_, r, w):
        if eng == "act":
            return P.op("act", lambda e: e.copy(out, in_), r=r, w=w)
        return P.op(eng, lambda e: e.tensor_copy(out, in_), r=r, w=w)

    def sp(col, n=1):
        return SPR.f[:, col:col + n]

    ONES_r = CST.r[:, 0:128]
    R96_r = CST.r[:, 128:224]
    SEL_r = CST.r[:, 224:288]
    BD_r = CST.r[:, 288:544]
    COS_f = CSTF.f[:, 0:512]
    SIN_f = CSTF.f[:, 512:1024]
    EPS_f = CSTF.f[:, 1024:1025]
    ZERO_f = CSTF.f[:, 1056:1088]

    dma("pool", "c0", [(CST.r[:, 0:288], cst[:, 0:288].bitcast(F32R)), (CST.r[:, 288:1568], cst[:, C_BD:C_BD + 1280].bitcast(F32R)),
                       (CSTF.f[:, 0:1024], cst[:, C_COS:C_COS + 1024]), (CSTF.f[:, 1024:1088], cst[:, C_EPS:C_EPS + 64]),
                       (IDENT.f, cst[:, C_ID:C_ID + 128]), (SPR.f, smallp)], w=[CST.t, CSTF.t, SPR.t, IDENT.t])
    dma("pool", "x0", [(X.f.rearrange("p (c t) -> p c t", c=8)[:, :, 0:512], xpT.rearrange("(c p) t -> p c t", p=128)),
                       (X.f.rearrange("p (c t) -> p c t", c=8)[:, :, 512:1024], xsT.rearrange("(c p) t -> p c t", p=128))],
        w=[X.t])
    zp = []
    for ch in range(4):
        zp.append((GAE[ch * 128:(ch + 1) * 128, :], ZERO_f))
        zp.append((GAE[5 * 512 + ch * 128:5 * 512 + (ch + 1) * 128, :], ZERO_f))
    dma("pool", "zp", zp, r=[CSTF.t], w=[T_GAE])

    X3 = X.f.rearrange("p (c t) -> p c t", c=8)

    def xg(c, g):
        return X3[:, c, g * 512:(g + 1) * 512]

    XT = [[V("f", X.off + c * 1024 + g * 512, 512, "X%d_%d" % (c, g)).t for g in range(2)] for c in range(8)]

    EXM = nc.dram_tensor("exm", [384, 24], F32).ap()
    GAM = nc.dram_tensor("gam", [5 * 384, 24], F32).ap()
    T_EXM, T_GAM = P.tile("d7", 0, 1, "EXM"), P.tile("d8", 0, 1, "GAM")
    SILC3 = SILC.r.rearrange("p (c t) -> p c t", t=4)
    for v in range(4):
        act(SILC3[:, :, v], sp(SP_CCTX + 8 * (v % 3), 8), AF.Silu, r=[SPR.t, SILC.t], w=[SILC.t])

    ADAT = take(272, "ADAT", "f")

    def ada_mm():
        MROW = V("r", KH.off, 3072, "MROW")
        for l in range(DEPTH):
            for pnl in range(4):
                s = wload(lambda s, l=l, pnl=pnl: [(s.r.rearrange("p (k n) -> p k n", k=8),
                                                    w_ada_s[l, :, pnl * 384:(pnl + 1) * 384].bitcast(F32R).rearrange("(k p) n -> p k n", p=128))])
                s3 = s.r.rearrange("p (k n) -> p k n", k=8)
                psA = bank("a")
                mm(psA.f[0:4, 0:384], [(SILC3[:, kc, :], s3[:, kc, :]) for kc in range(8)], r=[s.t, SILC.t], w=[psA.t])
                c0 = (l * 4 + pnl) * 384
                cp("act", MROW.r[0:4, c0:c0 + 384], psA.f[0:4, 0:384], r=[psA.t], w=[MROW.t])
        pb = bank("c")

        def tr(e):
            ins = None
            for idx in range(24):
                ins = e.matmul(pb.f[:, idx * 4:(idx + 1) * 4], lhsT=MROW.f[0:4, idx * 128:(idx + 1) * 128], rhs=IDENT.f[0:4, 0:4],
                               start=True, stop=True)
            return ins
        P.op("pe", tr, r=[MROW.t, IDENT.t], w=[pb.t])
        MODP3 = ADAT.f[:, 0:72].rearrange("p (v c) -> p v c", v=3)
        pb3 = pb.f[:, 0:96].rearrange("p (c v) -> p c v", v=4)
        for v in range(3):
            for l in range(DEPTH):
                tt("dve", MODP3[:, v, l * 12:(l + 1) * 12], pb3[:, l * 12:(l + 1) * 12, v], sp(l * NSP_L + SP_BADA, 12), ALU.add,
                   r=[pb.t, SPR.t, ADAT.t], w=[ADAT.t])
        dma("pool", "exm", [(EXM.rearrange("(v p) c -> p v c", p=128), MODP3)], r=[ADAT.t], w=[T_EXM])
        P.op("pool", lambda e: e.collective_compute("AllGather", ALU.bypass, replica_groups=RG,
                                                    ins=[EXM.opt()], outs=[GAM[0:1536].opt()]),
             r=[T_EXM, T_GAM], w=[T_GAM], kind="cc", key="ccm")
        STG4 = ADAT.f[:, 72:264].rearrange("p (t r c) -> p t r c", t=2, r=4)

        def ld(e):
            pid = e.partition_id()
            vb = (pid // 4) * 128 + 128
            return [e.dma_start(out=STG4[:, 0], in_=GAM[0:1536].rearrange("(r x) c -> x r c", x=384)[0:128]),
                    e.dma_start(out=STG4[:, 1], in_=GAM[bass.DynSlice(vb, 1536), :].rearrange("(r x) c -> x r c", x=384)[0:128])]
        P.op("pool", ld, r=[T_GAM], w=[ADAT.t], kind="d", nd=2, key="ldm")

    def ada_fin():
        STG4 = ADAT.f[:, 72:264].rearrange("p (t r c) -> p t r c", t=2, r=4)
        for l in range(DEPTH):
            M4 = MOD[l].f.rearrange("p (r i t) -> p r i t", r=4, i=12)
            for t in range(2):
                cp("dve", M4[:, :, :, t], STG4[:, t, :, l * 12:(l + 1) * 12], r=[ADAT.t, MOD[l].t], w=[MOD[l].t])

    def modv(l, m, t):
        return MOD[l].f[:, 2 * m + t:2 * m + t + 1]

    def scl_prep(l):
        M3 = MOD[l].f.rearrange("p (m t) -> p m t", t=2)
        S3 = SCL.f[:, 0:32].rearrange("p (a c t) -> p a c t", a=2, t=2)
        for which, (scm, gcol) in enumerate(((8, SP_GN1), (32, SP_GN2))):
            for t in range(2):
                stt("dve", S3[:, which, :, t], M3[:, scm:scm + 8, t], 1.0, sp(l * NSP_L + gcol, 8), ALU.add, ALU.mult,
                    r=[MOD[l].t, SPR.t, SCL.t], w=[SCL.t])

    def sclv(which, c, t):
        k = which * 16 + c * 2 + t
        return SCL.f[:, k:k + 1]

    def rstd_from_ps(ps, n_feat, ntok=512, eps=EPS, tmp=None, rs=None):
        tmp = tmp or T1
        rs = rs or RS
        act(tmp.f[:, 0:ntok], ps.f[:, 0:ntok], AF.Ln, r=[ps.t, CSTF.t], w=[tmp.t], bias=EPS_f, scale=1.0 / n_feat)
        act(rs.f[:, 0:ntok], tmp.f[:, 0:ntok], AF.Exp, r=[tmp.t], w=[rs.t], scale=-0.5)

    def norm_pre(dview, dtile):
        pss = [bank("a"), bank("a")]
        rs = [RS, T3]
        tmp = [T1, T2]
        for c in range(8):
            act(dview(0, c), xg(c, 0), AF.Square, r=[XT[c][0]], w=[dtile(0, c)])
            tt("dve", dview(1, c), xg(c, 1), xg(c, 1), ALU.mult, r=[XT[c][1]], w=[dtile(1, c)])
        for g in range(2):
            mm(pss[g].f, [(ONES_r, dview(g, c)) for c in range(8)], r=[CST.t] + [dtile(g, c) for c in range(8)], w=[pss[g].t])
        for g in range(2):
            rstd_from_ps(pss[g], D, tmp=tmp[g], rs=rs[g])

    def norm_both(l, which, dview, dtile, pre=False):
        shm = 0 if which == 0 else 24
        rs = [RS, T3]
        tmp = [T1, T2]
        if not pre:
            norm_pre(dview, dtile)
        for c in range(8):
            for g in range(2):
                tt("dve", tmp[g].f, xg(c, g), rs[g].f, ALU.mult, r=[XT[c][g], rs[g].t], w=[tmp[g].t])
                act(dview(g, c), tmp[g].f, AF.Identity, r=[tmp[g].t, SCL.t, MOD[l].t], w=[dtile(g, c)],
                    bias=modv(l, shm + c, g), scale=sclv(which, c, g))

    def adanorm(l, g, which, dst):
        t = 0 if g == 0 else 1
        d3 = dst.r.rearrange("p (c t) -> p c t", c=8)
        ps = bank("a")
        for c in range(8):
            act(d3[:, c, :], xg(c, g), AF.Square, r=[XT[c][g]], w=[dst.t])
        mm(ps.f, [(ONES_r, d3[:, c, :]) for c in range(8)], r=[CST.t, dst.t], w=[ps.t])
        rstd_from_ps(ps, D)
        shm = 0 if which == 0 else 24
        for c in range(8):
            tt("dve", T1.f, xg(c, g), RS.f, ALU.mult, r=[XT[c][g], RS.t], w=[T1.t])
            act(d3[:, c, :], T1.f, AF.Identity, r=[T1.t, SCL.t, MOD[l].t], w=[dst.t], bias=modv(l, shm + c, t), scale=sclv(which, c, t))

    def stage1_both(l, pre=False):
        spb = l * NSP_L
        TS = [TSM, TSA]
        geo = [(2, 256), (1, 512)]
        H3 = [TS[g].H.r.rearrange("p (c t) -> p c t", c=8) for g in range(2)]
        norm_both(l, 0, lambda g, c: H3[g][:, c, :], lambda g, c: TS[g].H.t, pre=pre)
        QA3 = [TS[g].QA.r.rearrange("p (c t) -> p c t", c=3) for g in range(2)]
        CKV3 = [TS[g].CKV.r.rearrange("p (c t) -> p c t", c=2) for g in range(2)]
        CKV3f = [TS[g].CKV.f.rearrange("p (c t) -> p c t", c=2) for g in range(2)]
        GBr3 = [TS[g].GB.r.rearrange("p (c t) -> p c t", c=2) for g in range(2)]
        FN3 = [TS[g].FN.r.rearrange("p (c t) -> p c t", c=2) for g in range(2)]
        UGv = [TS[g].UG.r[:, 0:2 * geo[g][0] * (geo[g][1] + 30)].rearrange("p (c s t) -> p c s t", c=2, s=geo[g][0]) for g in range(2)]
        VPv = [TS[g].VP.r[:, 0:2 * geo[g][0] * (geo[g][1] + 2)].rearrange("p (c s t) -> p c s t", c=2, s=geo[g][0]) for g in range(2)]
        UGf = [TS[g].UG.f[:, 0:2 * geo[g][0] * (geo[g][1] + 30)].rearrange("p (c s t) -> p c s t", c=2, s=geo[g][0]) for g in range(2)]
        VPf = [TS[g].VP.f[:, 0:2 * geo[g][0] * (geo[g][1] + 2)].rearrange("p (c s t) -> p c s t", c=2, s=geo[g][0]) for g in range(2)]
        SQ3 = SQ.r.rearrange("p (c t) -> p c t", c=4)

        def panel(c0, ncol):
            s = wload(lambda s: [(s.r[:, 0:8 * ncol].rearrange("p (k n) -> p k n", k=8),
                                  w_in[l, :, c0:c0 + ncol].bitcast(F32R).rearrange("(k p) n -> p k n", p=128))])
            return s, s.r[:, 0:8 * ncol].rearrange("p (k n) -> p k n", k=8)

        def s1(s, s3, m0, M, g):
            ps = bank("a")
            mm(ps.f[0:M, :], [(s3[:, kc, m0:m0 + M], H3[g][:, kc, :]) for kc in range(8)], r=[s.t, TS[g].H.t], w=[ps.t])
            return ps

        def seg(ap, g):
            return ap.rearrange("p (s t) -> p s t", s=geo[g][0])

        for c in range(2):
            for s_ in range(2):
                cp("dve", UGv[0][:, c, s_, 0:15], ZERO_f[:, 0:15], r=[CSTF.t], w=[UG.t])
                cp("dve", UGv[0][:, c, s_, 15 + 256:30 + 256], ZERO_f[:, 0:15], r=[CSTF.t], w=[UG.t])
                cp("dve", VPv[0][:, c, s_, 0:1], ZERO_f[:, 0:1], r=[CSTF.t], w=[VP.t])
                cp("dve", VPv[0][:, c, s_, 1 + 256:2 + 256], ZERO_f[:, 0:1], r=[CSTF.t], w=[VP.t])
        s, s3 = panel(0, 384)
        for c in range(3):
            for g in range(2):
                ps = s1(s, s3, c * 128, 128, g)
                cp("act", QA3[g][:, c, :], ps.f, r=[ps.t], w=[TS[g].QA.t])
        s, s3 = panel(384, 288)
        for c in range(2):
            for g in range(2):
                ps = s1(s, s3, c * 128, 128, g)
                cp("act" if g else "dve", CKV3[g][:, c, :], ps.f, r=[ps.t], w=[TS[g].CKV.t])
        for g in range(2):
            ps = s1(s, s3, 192, 96, g)
            cp("dve", TS[g].KR96.r[0:96, :], ps.f[0:96, :], r=[ps.t], w=[TS[g].KR96.t])
        for g in range(2):
            ps = bank("a")
            for c in range(2):
                act(SQ3[:, c, :], CKV3f[g][:, c, :], AF.Square, r=[TS[g].CKV.t], w=[SQ.t])
            mm(ps.f, [(ONES_r, SQ3[:, c, :]) for c in range(2)], r=[CST.t, SQ.t], w=[ps.t])
            rstd_from_ps(ps, 256)
            for c in range(2):
                stt("dve", CKV3[g][:, c, :], CKV3f[g][:, c, :], sp(spb + SP_GKVA + c), RS.f, ALU.mult, ALU.mult,
                    r=[TS[g].CKV.t, SPR.t, RS.t], w=[TS[g].CKV.t])
            ps = bank("a")
            for c in range(3):
                act(SQ3[:, c, :], QA3[g][:, c, :], AF.Square, r=[TS[g].QA.t], w=[SQ.t])
            mm(ps.f, [(ONES_r, SQ3[:, c, :]) for c in range(3)], r=[CST.t, SQ.t], w=[ps.t])
            rstd_from_ps(ps, 384)
            for c in range(3):
                stt("dve", QA3[g][:, c, :], QA3[g][:, c, :], sp(spb + SP_GQA + c), RS.f, ALU.mult, ALU.mult,
                    r=[TS[g].QA.t, SPR.t, RS.t], w=[TS[g].QA.t])
        sCb, sCb3 = panel(928, 256)
        sCa, sCa3 = panel(672, 256)
        for c in range(2):
            for g in range(2):
                L = geo[g][1]
                ps = s1(sCb, sCb3, c * 128, 128, g)
                act(UGv[g][:, c, :, 15:15 + L], seg(ps.f, g), AF.Sigmoid, r=[ps.t], w=[TS[g].UG.t])
                ps2 = s1(sCa, sCa3, c * 128, 128, g)
                tt("dve", UGv[g][:, c, :, 15:15 + L], seg(ps2.f, g), UGf[g][:, c, :, 15:15 + L], ALU.mult,
                   r=[ps2.t, TS[g].UG.t], w=[TS[g].UG.t])
        sD, sD3 = panel(1184, 256)
        for c in range(2):
            for g in range(2):
                ps = s1(sD, sD3, c * 128, 128, g)
                cp("act", GBr3[g][:, c, :], ps.f, r=[ps.t], w=[TS[g].GB.t])
        sD, sD3 = panel(1440, 256)
        for c in range(2):
            for g in range(2):
                L = geo[g][1]
                ps = s1(sD, sD3, c * 128, 128, g)
                cp("act", VPv[g][:, c, :, 1:1 + L], seg(ps.f, g), r=[ps.t], w=[TS[g].VP.t])
        sE, sE3 = panel(1696, 256)
        for c in range(2):
            for g in range(2):
                L = geo[g][1]
                ps = s1(sE, sE3, c * 128, 128, g)
                tt("dve", VPv[g][:, c, :, 1:1 + L], seg(ps.f, g), VPf[g][:, c, :, 1:1 + L], ALU.mult,
                   r=[ps.t, TS[g].VP.t], w=[TS[g].VP.t])
        sE, sE3 = panel(1952, 256)
        for c in range(2):
            for g in range(2):
                ps = s1(sE, sE3, c * 128, 128, g)
                cp("act", FN3[g][:, c, :], ps.f, r=[ps.t], w=[TS[g].FN.t])

        KRS = TSA.KR96
        ps = bank("a")
        mm(ps.f[0:96, :], [(R96_r[64:96, 0:96], KRS.r[64:96, :])], r=[CST.t, KRS.t], w=[ps.t])
        tt("dve", T1.f[64:96, :], KRS.f[64:96, :], COS_f[64:96, :], ALU.mult, r=[KRS.t, CSTF.t], w=[T1.t])
        tt("dve", T2.f[64:96, :], ps.f[64:96, :], SIN_f[64:96, :], ALU.mult, r=[ps.t, CSTF.t], w=[T2.t])
        tt("dve", KRS.r[64:96, :], T1.f[64:96, :], T2.f[64:96, :], ALU.add, r=[T1.t, T2.t], w=[KRS.t])

        dma("sp", "o_ckv%d" % l, [(nckvT[l].rearrange("(c p) t -> p c t", p=128), CKV.f.rearrange("p (c t) -> p c t", c=2)),
                                    (nkrT[l], KR96.f[64:96, :])], r=[CKV.t, KR96.t], w=[T_OUT])
        A = TSA
        dma("pool", "spill%d" % l, [(SPL[:, 0:1536], A.QA.f), (SPL[:, 1536:2680], A.UG.f), (SPL[:, 2680:3712], A.VP.f),
                                    (SPL[:, 3712:4736], A.GB.f)], r=[A.QA.t, A.UG.t, A.VP.t, A.GB.t, T_SPL], w=[T_SPL])
        dma("pool", "exkv%d" % l, [
            (EXKV[0:256].rearrange("(c p) t -> p c t", p=128), A.CKV.f.rearrange("p (c t) -> p c t", c=2)),
            (EXKV[256:288], A.KR96.f[64:96, :])], r=[A.CKV.t, A.KR96.t, T_EXKV], w=[T_EXKV])
        dma("pool", "exfn%d" % l, [
            (EXFN.rearrange("(c p) t -> p c t", p=128), A.FN.f.rearrange("p (c t) -> p c t", c=2))], r=[A.FN.t, T_EXFN], w=[T_EXFN])
        dma("pool", "exe%d" % l, [
            (EXE[0:256, 0:16].rearrange("(c p) t -> p c t", p=128), UGf[1][:, :, 0, 15:31]),
            (EXE[0:256, 16:32].rearrange("(c p) t -> p c t", p=128), UGf[1][:, :, 0, 512 - 1:512 + 15]),
            (EXE[256:512, 0:16].rearrange("(c p) t -> p c t", p=128), VPf[1][:, :, 0, 1:17]),
            (EXE[256:512, 16:32].rearrange("(c p) t -> p c t", p=128), VPf[1][:, :, 0, 512 - 15:512 + 1])],
            r=[A.UG.t, A.VP.t, T_EXE], w=[T_EXE])
        P.op("pool", lambda e: e.collective_compute("AllGather", ALU.bypass, replica_groups=RG, ins=[EXKV.opt()], outs=[GAKV.opt()]),
             r=[T_EXKV, T_EXFN, T_EXE, T_GAKV], w=[T_GAKV], kind="cc", key="cckv%d" % l)
        P.op("pool", lambda e: e.collective_compute("AllGather", ALU.bypass, replica_groups=RG, ins=[EXFN.opt()], outs=[GAFN.opt()]),
             r=[T_EXFN, T_GAFN, T_GAKV], w=[T_GAFN], kind="cc", key="ccfn%d" % l)
        P.op("pool", lambda e: e.collective_compute("AllGather", ALU.bypass, replica_groups=RG, ins=[EXE.opt()], outs=[GAE[512:5 * 512].opt()]),
             r=[T_EXE, T_GAE, T_GAFN], w=[T_GAE], kind="cc", key="cce%d" % l)
        HL = HAL.f[:, 0:64].rearrange("p (c t) -> p c t", c=4)
        HR = HAL.f[:, 64:128].rearrange("p (c t) -> p c t", c=4)

        def halo(e):
            pid = e.partition_id()
            jb = (pid % 4) * 512
            return [e.dma_start(out=HL, in_=GAE[bass.DynSlice(jb, 512), 16:32].rearrange("(c p) t -> p c t", p=128)),
                    e.dma_start(out=HR, in_=GAE[bass.DynSlice(jb + 1024, 512), 0:16].rearrange("(c p) t -> p c t", p=128))]
        P.op("pool", halo, r=[T_GAE], w=[HAL.t], kind="d", nd=2, key="halo%d" % l)

    def mixer(l, g, part=0):
        P.mute = (part == 2)
        lat = 0 if g == 0 else 1
        nseq, L = (2, 256) if g == 0 else (1, 512)
        spb = l * NSP_L
        H3 = H.r.rearrange("p (c t) -> p c t", c=8)
        adanorm(l, g, 0, H)

        def panel(c0, ncol):
            s = wload(lambda s: [(s.r[:, 0:8 * ncol].rearrange("p (k n) -> p k n", k=8),
                                  w_in[l, :, c0:c0 + ncol].bitcast(F32R).rearrange("(k p) n -> p k n", p=128))])
            return s, s.r[:, 0:8 * ncol].rearrange("p (k n) -> p k n", k=8)

        def s1(s, s3, m0, M):
            ps = bank("a")
            mm(ps.f[0:M, :], [(s3[:, kc, m0:m0 + M], H3[:, kc, :]) for kc in range(8)], r=[s.t, H.t], w=[ps.t])
            return ps

        QA3 = QA.r.rearrange("p (c t) -> p c t", c=3)
        CKV3 = CKV.r.rearrange("p (c t) -> p c t", c=2)
        GB3 = GB.f.rearrange("p (c t) -> p c t", c=2)
        FN3 = FN.r.rearrange("p (c t) -> p c t", c=2)
        UGv = UG.r[:, 0:2 * nseq * (L + 30)].rearrange("p (c s t) -> p c s t", c=2, s=nseq)
        VPv = VP.r[:, 0:2 * nseq * (L + 2)].rearrange("p (c s t) -> p c s t", c=2, s=nseq)
        UGf = UG.f[:, 0:2 * nseq * (L + 30)].rearrange("p (c s t) -> p c s t", c=2, s=nseq)
        VPf = VP.f[:, 0:2 * nseq * (L + 2)].rearrange("p (c s t) -> p c s t", c=2, s=nseq)
        if g == 0:
            for c in range(2):
                for s_ in range(nseq):
                    cp("dve", UGv[:, c, s_, 0:15], ZERO_f[:, 0:15], r=[CSTF.t], w=[UG.t])
                    cp("dve", UGv[:, c, s_, 15 + L:30 + L], ZERO_f[:, 0:15], r=[CSTF.t], w=[UG.t])
                    cp("dve", VPv[:, c, s_, 0:1], ZERO_f[:, 0:1], r=[CSTF.t], w=[VP.t])
                    cp("dve", VPv[:, c, s_, 1 + L:2 + L], ZERO_f[:, 0:1], r=[CSTF.t], w=[VP.t])
        s, s3 = panel(0, 384)
        for c in range(3):
            ps = s1(s, s3, c * 128, 128)
            cp("act", QA3[:, c, :], ps.f, r=[ps.t], w=[QA.t])
        s, s3 = panel(384, 288)
        ckv_raw = [T2, T3]
        for c in range(2):
            ps = s1(s, s3, c * 128, 128)
            cp("act", ckv_raw[c].f, ps.f, r=[ps.t], w=[ckv_raw[c].t])
        ps = s1(s, s3, 192, 96)
        cp("dve", KR96.r[0:96, :], ps.f[0:96, :], r=[ps.t], w=[KR96.t])
        ps = bank("a")
        for c in range(2):
            act(CKV3[:, c, :], ckv_raw[c].f, AF.Square, r=[ckv_raw[c].t], w=[CKV.t])
        mm(ps.f, [(ONES_r, CKV3[:, c, :]) for c in range(2)], r=[CST.t, CKV.t], w=[ps.t])
        rstd_from_ps(ps, 256)
        for c in range(2):
            stt("dve", CKV3[:, c, :], ckv_raw[c].f, sp(spb + SP_GKVA + c), RS.f, ALU.mult, ALU.mult,
                r=[ckv_raw[c].t, SPR.t, RS.t], w=[CKV.t])
        ps = bank("a")
        OHs = OH.r.rearrange("p (c t) -> p c t", c=8)
        for c in range(3):
            act(OHs[:, c, :], QA3[:, c, :], AF.Square, r=[QA.t], w=[OH.t])
        mm(ps.f, [(ONES_r, OHs[:, c, :]) for c in range(3)], r=[CST.t, OH.t], w=[ps.t])
        rstd_from_ps(ps, 384)
        for c in range(3):
            stt("dve", QA3[:, c, :], QA3[:, c, :], sp(spb + SP_GQA + c), RS.f, ALU.mult, ALU.mult,
                r=[QA.t, SPR.t, RS.t], w=[QA.t])
        sCb, sCb3 = panel(928, 256)
        sCa, sCa3 = panel(672, 256)
        for c in range(2):
            ps = s1(sCb, sCb3, c * 128, 128)
            act(T1.f, ps.f, AF.Sigmoid, r=[ps.t], w=[T1.t])
            ps2 = s1(sCa, sCa3, c * 128, 128)
            tt("dve", UGv[:, c, :, 15:15 + L], ps2.f.rearrange("p (s t) -> p s t", s=nseq), T1.f.rearrange("p (s t) -> p s t", s=nseq),
               ALU.mult, r=[ps2.t, T1.t], w=[UG.t])
        GBr3 = GB.r.rearrange("p (c t) -> p c t", c=2)
        sD, sD3 = panel(1184, 256)
        for c in range(2):
            ps = s1(sD, sD3, c * 128, 128)
            cp("act", GBr3[:, c, :], ps.f, r=[ps.t], w=[GB.t])
        gct = [T2, T3]
        sD, sD3 = panel(1440, 256)
        for c in range(2):
            ps = s1(sD, sD3, c * 128, 128)
            cp("act", gct[c].f, ps.f, r=[ps.t], w=[gct[c].t])
        sE, sE3 = panel(1696, 256)
        for c in range(2):
            ps = s1(sE, sE3, c * 128, 128)
            tt("dve", VPv[:, c, :, 1:1 + L], ps.f.rearrange("p (s t) -> p s t", s=nseq), gct[c].f.rearrange("p (s t) -> p s t", s=nseq),
               ALU.mult, r=[ps.t, gct[c].t], w=[VP.t])
        sE, sE3 = panel(1952, 256)
        for c in range(2):
            ps = s1(sE, sE3, c * 128, 128)
            cp("act", FN3[:, c, :], ps.f, r=[ps.t], w=[FN.t])

        def rope(dst_f, dst_r, dst_t):
            ps = bank("a")
            mm(ps.f[0:96, :], [(R96_r[64:96, 0:96], dst_r)], r=[CST.t, dst_t], w=[ps.t])
            tt("dve", T1.f[64:96, :], dst_f, COS_f[64:96, :], ALU.mult, r=[dst_t, CSTF.t], w=[T1.t])
            tt("dve", T2.f[64:96, :], ps.f[64:96, :], SIN_f[64:96, :], ALU.mult, r=[ps.t, CSTF.t], w=[T2.t])
            tt("dve", dst_r, T1.f[64:96, :], T2.f[64:96, :], ALU.add, r=[T1.t, T2.t], w=[dst_t])

        if g == 1:
            rope(KR96.f[64:96, :], KR96.r[64:96, :], KR96.t)

        if g == 0:
            dma("sp", "o_ckv%d" % l, [(nckvT[l].rearrange("(c p) t -> p c t", p=128), CKV.f.rearrange("p (c t) -> p c t", c=2)),
                                        (nkrT[l], KR96.f[64:96, :])], r=[CKV.t, KR96.t], w=[T_OUT])
        else:
            if not os.environ.get("KD_NOSPILL"):
              dma("pool", "spill%d" % l, [(SPL[:, 0:1536], QA.f), (SPL[:, 1536:2680], UG.f), (SPL[:, 2680:3712], VP.f),
                                        (SPL[:, 3712:4736], GB.f)], r=[QA.t, UG.t, VP.t, GB.t, T_SPL], w=[T_SPL])

            dma("pool", "exkv%d" % l, [
                (EXKV[0:256].rearrange("(c p) t -> p c t", p=128), CKV.f.rearrange("p (c t) -> p c t", c=2)),
                (EXKV[256:288], KR96.f[64:96, :])], r=[CKV.t, KR96.t, T_EXKV], w=[T_EXKV])
            P.op("pool", lambda e: e.collective_compute("AllGather", ALU.bypass, replica_groups=RG,
                                                        ins=[EXKV.opt()], outs=[GAKV.opt()]),
                 r=[T_EXKV, T_GAKV], w=[T_GAKV], kind="cc", key="cckv%d" % l)
            dma("pool", "exfn%d" % l, [
                (EXFN.rearrange("(c p) t -> p c t", p=128), FN.f.rearrange("p (c t) -> p c t", c=2))], r=[FN.t, T_EXFN], w=[T_EXFN])
            P.op("pool", lambda e: e.collective_compute("AllGather", ALU.bypass, replica_groups=RG,
                                                        ins=[EXFN.opt()], outs=[GAFN.opt()]),
                 r=[T_EXFN, T_GAFN], w=[T_GAFN], kind="cc", key="ccfn%d" % l)
            dma("pool", "exe%d" % l, [
                (EXE[0:256, 0:16].rearrange("(c p) t -> p c t", p=128), UGf[:, :, 0, 15:31]),
                (EXE[0:256, 16:32].rearrange("(c p) t -> p c t", p=128), UGf[:, :, 0, L - 1:L + 15]),
                (EXE[256:512, 0:16].rearrange("(c p) t -> p c t", p=128), VPf[:, :, 0, 1:17]),
                (EXE[256:512, 16:32].rearrange("(c p) t -> p c t", p=128), VPf[:, :, 0, L - 15:L + 1])],
                r=[UG.t, VP.t, T_EXE], w=[T_EXE])
            P.op("pool", lambda e: e.collective_compute("AllGather", ALU.bypass, replica_groups=RG,
                                                        ins=[EXE.opt()], outs=[GAE[512:5 * 512].opt()]),
                 r=[T_EXE, T_GAE], w=[T_GAE], kind="cc", key="cce%d" % l)
            HL = HAL.f[:, 0:64].rearrange("p (c t) -> p c t", c=4)
            HR = HAL.f[:, 64:128].rearrange("p (c t) -> p c t", c=4)

            def halo(e):
                pid = e.partition_id()
                jb = (pid % 4) * 512
                return [e.dma_start(out=HL, in_=GAE[bass.DynSlice(jb, 512), 16:32].rearrange("(c p) t -> p c t", p=128)),
                        e.dma_start(out=HR, in_=GAE[bass.DynSlice(jb + 1024, 512), 0:16].rearrange("(c p) t -> p c t", p=128))]
            P.op("pool", halo, r=[T_GAE], w=[HAL.t], kind="d", nd=2, key="halo%d" % l)
        if part == 1:
            return
        P.mute = False
        if part == 2 and g == 1:
            dma("pool", "reload%d" % l, [(QA.r, SPL[:, 0:1536].bitcast(F32R)), (UG.r, SPL[:, 1536:2680].bitcast(F32R)),
                                         (VP.r, SPL[:, 2680:3712].bitcast(F32R)), (GB.r, SPL[:, 3712:4736].bitcast(F32R))],
                r=[T_SPL], w=[QA.t, UG.t, VP.t, GB.t])

        if g == 1:
            HL = HAL.f[:, 0:64].rearrange("p (c t) -> p c t", c=4)
            HR = HAL.f[:, 64:128].rearrange("p (c t) -> p c t", c=4)
            for ch in range(2):
                cp("pool", UGv[:, ch, 0, 0:15], HL[:, ch, 1:16], r=[HAL.t], w=[UG.t])
                cp("pool", UGv[:, ch, 0, 15 + L:30 + L], HR[:, ch, 0:15], r=[HAL.t], w=[UG.t])
                cp("pool", VPv[:, ch, 0, 0:1], HL[:, 2 + ch, 15:16], r=[HAL.t], w=[VP.t])
                cp("pool", VPv[:, ch, 0, 1 + L:2 + L], HR[:, 2 + ch, 0:1], r=[HAL.t], w=[VP.t])

        for c in range(2):
            T1v = T1.f.rearrange("p (s t) -> p s t", s=nseq)
            ts("dve", T1v, VPf[:, c, :, 0:L], sp(spb + SP_WSC + 0 * 2 + c), None, ALU.mult, None, r=[VP.t, SPR.t], w=[T1.t])
            for k in (1, 2):
                stt("dve", T1v, VPf[:, c, :, k:k + L], sp(spb + SP_WSC + k * 2 + c), T1v, ALU.mult, ALU.add,
                    r=[VP.t, SPR.t, T1.t], w=[T1.t])
            tt("dve", GB.r.rearrange("p (c t) -> p c t", c=2)[:, c, :], GB3[:, c, :], T1.f, ALU.mult, r=[GB.t, T1.t], w=[GB.t])

        U23 = U2.r.rearrange("p (c t) -> p c t", c=2)
        cacc = [T2, T3]
        for c in range(2):
            pc = bank("c")
            for (k0, k1) in ((0, 16), (16, 31)):
                s = RING[ring_i[0] % len(RING)]
                ring_i[0] += 1
                blks = []
                for k in range(k0, k1):
                    bv = V("r", s.off + (k - k0) * 128, 128, "diag")
                    if k % 2:
                        act(bv.r, IDENT.f, AF.Identity, r=[IDENT.t, SPR.t], w=[bv.t], scale=sp(spb + SP_WCDW + k * 2 + c))
                    else:
                        ts("dve", bv.r, IDENT.f, sp(spb + SP_WCDW + k * 2 + c), None, ALU.mult, None, r=[IDENT.t, SPR.t], w=[bv.t])
                    blks.append(bv)

                def fn(e, k0=k0, k1=k1, c=c, pc=pc, blks=blks):
                    ins = None
                    for k in range(k0, k1):
                        ins = e.matmul(pc.f, lhsT=blks[k - k0].r, rhs=UGv[:, c, :, k:k + L], start=(k == 0), stop=(k == 30))
                    return ins
                P.op("pe", fn, r=[s.t, UG.t], w=[pc.t])
            act(cacc[c].f, pc.f, AF.Identity, r=[pc.t, SPR.t], w=[cacc[c].t], bias=sp(spb + SP_BCDW + c))
        ps = bank("a")
        for c in range(2):
            cp("act", U23[:, c, :], cacc[c].f, r=[cacc[c].t], w=[U2.t])
        mm(ps.f, [(ONES_r, U23[:, c, :]) for c in range(2)], r=[CST.t, U2.t], w=[ps.t])
        for c in range(2):
            stt("dve", cacc[c].f, ps.f, -1.0 / 256, cacc[c].f, ALU.mult, ALU.add, r=[ps.t, cacc[c].t], w=[cacc[c].t])
        ps = bank("a")
        for c in range(2):
            act(U23[:, c, :], cacc[c].f, AF.Square, r=[cacc[c].t], w=[U2.t])
        mm(ps.f, [(ONES_r, U23[:, c, :]) for c in range(2)], r=[CST.t, U2.t], w=[ps.t])
        rstd_from_ps(ps, 256)
        for c in range(2):
            tt("dve", cacc[c].f, cacc[c].f, RS.f, ALU.mult, r=[cacc[c].t, RS.t], w=[cacc[c].t])
            act(U23[:, c, :], cacc[c].f, AF.Silu, r=[cacc[c].t, SPR.t], w=[U2.t], bias=sp(spb + SP_BCLN + c), scale=sp(spb + SP_GCLN + c))

        WQ = wload(lambda s: [(s.r[:, 0:2304].rearrange("p (k n) -> p k n", k=3), w_qb[l].bitcast(F32R).rearrange("(k p) n -> p k n", p=128))])
        WKV = wload(lambda s: [(s.r[:, 0:2048].rearrange("p (k n) -> p k n", k=2), w_kvb[l].bitcast(F32R).rearrange("(k p) n -> p k n", p=128))])
        WQ3 = WQ.r[:, 0:2304].rearrange("p (k n) -> p k n", k=3)
        WKV4 = WKV.r[:, 0:2048].rearrange("p (k h n) -> p k h n", k=2, h=8)

        QH3r = QH.r.rearrange("p (h t) -> p h t", h=8)
        QH3f = QH.f.rearrange("p (h t) -> p h t", h=8)
        for h in range(8):
            ps = bank("a")
            mm(ps.f[0:96, :], [(WQ3[:, kc, h * 96:(h + 1) * 96], QA3[:, kc, :]) for kc in range(3)], r=[WQ.t, QA.t], w=[ps.t])
            cp("act", QH3r[0:96, h, :], ps.f[0:96, :], r=[ps.t], w=[QHh[h].t])

        if g == 1:
            for h in range(8):
                rope(QH3f[64:96, h, :], QH3r[64:96, h, :], QHh[h].t)
        KH3 = KH.r.rearrange("p (h t) -> p h t", h=8)
        VB4 = VB.r.rearrange("p (c h d) -> p c h d", c=4, h=8)
        VB4f = VB.f.rearrange("p (c h d) -> p c h d", c=4, h=8)
        CKVB3 = CKVB.r.rearrange("p (c t) -> p c t", c=2)
        OH3f = OH.f.rearrange("p (h t) -> p h t", h=8)
        OH3r = OH.r.rearrange("p (h t) -> p h t", h=8)
        for ck_ in range(4):
            cp("dve", VB4[:, ck_, :, 64], CST.f[:, 0:8], r=[CST.t], w=[VBc[ck_].t])
        scale = 96.0 ** -0.5
        et_i = [0]

        def attend_block(ckv_r, ckv_t, kr_r, kr_t, nk, q0, nq, first, half=None, phase="both"):
            kc0 = 0 if half is None else half * 256
            vc0 = 0 if half is None else half * 2

            def kt(h):
                return [KHq[h][0].t, KHq[h][1].t] if half is None else [KHq[h][half].t]
            nck = nk // 128
            if phase in ("both", "prep"):
              for h in range(8):
                ps = bank("a")
                mm(ps.f[0:64, 0:nk], [(WKV4[:, kc, h, 0:64], ckv_r(kc)) for kc in range(2)], r=[WKV.t, ckv_t], w=[ps.t])
                cp("act" if h % 2 else "dve", KH3[0:64, h, kc0:kc0 + nk], ps.f[0:64, 0:nk], r=[ps.t], w=kt(h))
              for h in range(8):
                cp("act" if h % 2 else "dve", KH3[64:96, h, kc0:kc0 + nk], kr_r, r=[kr_t], w=kt(h))
              for ck in range(nck):
                ps = bank("a")
                mm(ps.f, [(ckv_r(kc)[:, ck * 128:(ck + 1) * 128], WKV4[:, kc, :, 64:128]) for kc in range(2)], r=[WKV.t, ckv_t], w=[ps.t])
                cp("act" if ck % 2 else "dve", VB4[:, vc0 + ck, :, 0:64], ps.f.rearrange("p (h d) -> p h d", h=8), r=[ps.t], w=[VBc[vc0 + ck].t])
            if phase == "prep":
                return
            items = [(h, ck) for h in range(8) for ck in range(nck)]
            LOOK = 2
            pos = {}
            ets = {}
            for i in range(len(items) + LOOK):
                if i < len(items):
                    h, ck = items[i]
                    pss = bank("b")
                    mm(pss.f[:, 0:nq], [(KH3[0:96, h, kc0 + ck * 128:kc0 + (ck + 1) * 128], QH3r[0:96, h, q0:q0 + nq])], r=kt(h) + [QHh[h].t], w=[pss.t])
                    et = ET[et_i[0] % 3]
                    et_i[0] += 1
                    act(et.r[:, 0:nq], pss.f[:, 0:nq], AF.Exp, r=[pss.t], w=[et.t], scale=scale)
                    ets[i] = et
                j = i - LOOK
                if j >= 0:
                    h, ck = items[j]
                    if ck == 0:
                        pos[h] = bank("c")
                    po, et = pos[h], ets.pop(j)
                    P.op("pe", lambda e, po=po, ck=ck, h=h, et=et: e.matmul(po.f[0:65, 0:nq], lhsT=VB4[:, vc0 + ck, h, :], rhs=et.r[:, 0:nq],
                                                                            start=(ck == 0), stop=(ck == nck - 1)),
                         r=[VBc[vc0 + ck].t, et.t], w=[po.t])
                    if ck == nck - 1:
                        if first:
                            cp("dve", OH3r[0:65, h, q0:q0 + nq], po.f[0:65, 0:nq], r=[po.t], w=[OHh[h].t])
                        else:
                            tt("dve", OH3r[0:65, h, q0:q0 + nq], OH3f[0:65, h, q0:q0 + nq], po.f[0:65, 0:nq], ALU.add, r=[po.t, OHh[h].t], w=[OHh[h].t])

        if g == 0:
            for ph in ("prep", "loop"):
                for s_ in range(2):
                    q0 = s_ * 256
                    attend_block(lambda kc, q0=q0: CKV3[:, kc, q0:q0 + 256], CKV.t, KR96.r[64:96, q0:q0 + 256], KR96.t, 256, q0, 256, True,
                                 half=s_, phase=ph)
        else:
            dma("pool", "kvb", [(CKVB3[:, :, 0:256], cckvT[l].bitcast(F32R).rearrange("(c p) t -> p c t", p=128)),
                                (KRB.r[64:96, 0:256], ckrT[l].bitcast(F32R))], w=[CKVB.t, KRB.t])
            attend_block(lambda kc: CKVB3[:, kc, 0:256], CKVB.t, KRB.r[64:96, 0:256], KRB.t, 256, 0, 512, True)
            for rk in range(4):
                r0 = rk * 288
                dma("pool", "kvb", [(CKVB3, GAKV[r0:r0 + 256].bitcast(F32R).rearrange("(c p) t -> p c t", p=128)),
                                    (KRB.r[64:96, :], GAKV[r0 + 256:r0 + 288].bitcast(F32R))], r=[T_GAKV], w=[CKVB.t, KRB.t])
                attend_block(lambda kc: CKVB3[:, kc, :], CKVB.t, KRB.r[64:96, :], KRB.t, 512, 0, 512, False)
        for h in range(8):
            ps = bank("a")
            P.op("pe", lambda e, ps=ps, h=h: e.matmul(ps.f[0:64, :], lhsT=SEL_r[0:65, :], rhs=OH3r[0:65, h, :], start=True, stop=True),
                 r=[CST.t, OHh[h].t], w=[ps.t])
            tq = T1 if h % 2 == 0 else T2
            act(tq.f[0:64, :], ps.f[0:64, :], AF.Ln, r=[ps.t], w=[tq.t])
            act(tq.f[0:64, :], tq.f[0:64, :], AF.Exp, r=[tq.t], w=[tq.t], scale=-1.0)
            tt("dve", OH3r[0:64, h, :], OH3f[0:64, h, :], tq.f[0:64, :], ALU.mult, r=[OHh[h].t, tq.t], w=[OHh[h].t])

        F23 = F2.r.rearrange("p (c t) -> p c t", c=2)
        if g == 0:
            ABp = AB.r[:, 0:2048].rearrange("p (s l j n) -> p s l j n", s=2, l=2, j=2)
            for s_ in range(2):
                for lc in range(2):
                    for jc in range(2):
                        ps = bank("a")
                        t0 = s_ * 256 + lc * 128
                        mm(ps.f[:, 0:256], [(FN3[:, jc, t0:t0 + 128], BD_r)], r=[FN.t, CST.t], w=[ps.t])
                        cp("act", ABp[:, s_, lc, jc, :], ps.f[:, 0:256], r=[ps.t], w=[ABt[s_ * 2 + lc].t])
            adanorm(l, g, 0, H)
            for s_ in range(2):
                for jc in range(2):
                    ps = bank("a")
                    pairs = []
                    for lc in range(2):
                        pairs.append((ABp[:, s_, lc, jc, 0:128], DFP[0][:, lc, :]))
                        pairs.append((ABp[:, s_, lc, jc, 128:256], DFP[1][:, lc, :]))
                    mm(ps.f[:, 0:256], pairs, r=[AB.t, DFPT.t], w=[ps.t])
                    cp("act", F23[:, jc, s_ * 256:(s_ + 1) * 256], ps.f[:, 0:256], r=[ps.t], w=[F2.t])
        else:
            FNF3 = FNFULL.r.rearrange("p (c t) -> p c t", c=2)
            dma("pool", "fnf", [(FNF3[:, :, rk * 512:(rk + 1) * 512],
                                 GAFN[rk * 256:(rk + 1) * 256].bitcast(F32R).rearrange("(c p) t -> p c t", p=128))
                                for rk in range(4)], r=[T_GAFN], w=[FNFULL.t])
            ABs = AB.r.rearrange("p (l j n) -> p l j n", l=16, j=2)
            for lc in range(16):
                for jc in range(2):
                    ps = bank("a")
                    mm(ps.f[:, 0:256], [(FNF3[:, jc, lc * 128:(lc + 1) * 128], BD_r)], r=[FNFULL.t, CST.t], w=[ps.t])
                    cp("act" if (lc + jc) % 2 else "dve", ABs[:, lc, jc, :], ps.f[:, 0:256], r=[ps.t], w=[ABt[lc].t])
            adanorm(l, g, 0, H)
            pacc = [bank("c"), bank("c")]
            for pnl in range(4):
                sl = []
                for cs in range(2):
                    s = wload(lambda s, cs=cs, pnl=pnl: [(s.r[:, 0:2048].rearrange("p (l n) -> p l n", l=4),
                                                          dftS[cs, pnl * 512:(pnl + 1) * 512, :].bitcast(F32R).rearrange("(l p) n -> p l n", p=128))])
                    sl.append(s)
                for jc in range(2):
                    def fn(e, jc=jc, pnl=pnl, sl=sl):
                        ins = None
                        for li in range(4):
                            lc = pnl * 4 + li
                            for cs in range(2):
                                ins = e.matmul(pacc[jc].f, lhsT=ABs[:, lc, jc, cs * 128:(cs + 1) * 128],
                                               rhs=sl[cs].r[:, 0:2048].rearrange("p (l n) -> p l n", l=4)[:, li, :],
                                               start=(lc == 0 and cs == 0), stop=(lc == 15 and cs == 1))
                        return ins
                    P.op("pe", fn, r=[AB.t, sl[0].t, sl[1].t], w=[pacc[jc].t])
            for jc in range(2):
                cp("act", F23[:, jc, :], pacc[jc].f, r=[pacc[jc].t], w=[F2.t])

        MG3 = MERGED.r.rearrange("p (c t) -> p c t", c=8)
        for dm in range(8):
            c0 = OFF_GATE + dm * 128
            sg = wload(lambda s, dm=dm: [(s.r[:, 0:3072], wmerge[l, dm, :, 0:3072].bitcast(F32R))])
            so = wload(lambda s, dm=dm: [(s.r[:, 0:2816], wmerge[l, dm, :, 3072:5888].bitcast(F32R))])
            first = True
            for b in range(4):
                src = sg if b < 3 else so
                boff = (b if b < 3 else 0) * 1024
                g3 = src.r[:, boff:boff + 1024].rearrange("p (k n) -> p k n", k=8)
                psg = bank("a")
                mm(psg.f, [(g3[:, kc, :], H3[:, kc, :]) for kc in range(8)], r=[src.t, H.t], w=[psg.t])
                act(T2.f, psg.f, AF.Sigmoid, r=[psg.t], w=[T2.t])
                psy = bank("a")
                if b == 0:
                    wo = so.r[0:64, 1024:2048].rearrange("p (h n) -> p h n", h=8)
                    mm(psy.f, [(wo[:, h, :], OH3r[0:64, h, :]) for h in range(8)], r=[so.t, OH.t], w=[psy.t])
                else:
                    boffs = {1: 2048, 2: 2304, 3: 2560}[b]
                    w3 = so.r[:, boffs:boffs + 256].rearrange("p (k n) -> p k n", k=2)
                    srcv, srct = {1: (U23, U2.t), 2: (GBr3, GB.t), 3: (F23, F2.t)}[b]
                    mm(psy.f, [(w3[:, kc, :], srcv[:, kc, :]) for kc in range(2)], r=[so.t, srct], w=[psy.t])
                if first:
                    tt("dve", MG3[:, dm, :], psy.f, T2.f, ALU.mult, r=[psy.t, T2.t], w=[MERGED.t])
                    first = False
                else:
                    tt("dve", T1.f, psy.f, T2.f, ALU.mult, r=[psy.t, T2.t], w=[T1.t])
                    tt("dve", MG3[:, dm, :], MG3[:, dm, :], T1.f, ALU.add, r=[MERGED.t, T1.t], w=[MERGED.t])
        for pnl in range(3):
            c0 = pnl * 384
            nm = 3 if pnl < 2 else 2
            s = wload(lambda s, c0=c0, nm=nm: [(s.r[:, 0:8 * nm * 128].rearrange("p (k n) -> p k n", k=8),
                                                w_out[l, :, c0:c0 + nm * 128].bitcast(F32R).rearrange("(k p) n -> p k n", p=128))])
            s3 = s.r[:, 0:8 * nm * 128].rearrange("p (k n) -> p k n", k=8)
            for mi in range(nm):
                m = pnl * 3 + mi
                ps = bank("a")
                mm(ps.f, [(s3[:, kc, mi * 128:(mi + 1) * 128], MG3[:, kc, :]) for kc in range(8)], r=[s.t, MERGED.t], w=[ps.t])
                stt("dve", xg(m, g), ps.f, modv(l, 16 + m, lat), xg(m, g), ALU.mult, ALU.add, r=[ps.t, MOD[l].t, XT[m][g]], w=[XT[m][g]])

    DFPT = CST
    DFP = [CST.r[:, 544:1056].rearrange("p (l n) -> p l n", l=2), CST.r[:, 1056:1568].rearrange("p (l n) -> p l n", l=2)]

    def ffn(l):
        H23 = H2.r.rearrange("p (c t) -> p c t", c=8)
        HID3 = HID.r.rearrange("p (c t) -> p c t", c=11)
        norm_both(l, 1, lambda g, c: H23[:, c, g * 512:(g + 1) * 512], lambda g, c: H2T[c][g])
        for half in range(2):
            for jp in range(0, 11, 3):
                nj = min(3, 11 - jp)
                j0 = (half * 11 + jp) * 128
                sg = wload(lambda s, j0=j0, nj=nj: [(s.r[:, 0:8 * nj * 128].rearrange("p (k n) -> p k n", k=8),
                                                     w_ffn_gate[l, :, j0:j0 + nj * 128].bitcast(F32R).rearrange("(k p) n -> p k n", p=128))])
                su = wload(lambda s, j0=j0, nj=nj: [(s.r[:, 0:8 * nj * 128].rearrange("p (k n) -> p k n", k=8),
                                                     w_ffn_up[l, :, j0:j0 + nj * 128].bitcast(F32R).rearrange("(k p) n -> p k n", p=128))])
                sg3 = sg.r[:, 0:8 * nj * 128].rearrange("p (k n) -> p k n", k=8)
                su3 = su.r[:, 0:8 * nj * 128].rearrange("p (k n) -> p k n", k=8)
                for ji in range(nj):
                    for g in range(2):
                        tsl = slice(g * 512, (g + 1) * 512)
                        pg = bank("a")
                        mm(pg.f, [(sg3[:, kc, ji * 128:(ji + 1) * 128], H23[:, kc, tsl]) for kc in range(8)], r=[sg.t, H2.t], w=[pg.t])
                        tmp = T2 if g == 0 else T3
                        act(tmp.f, pg.f, AF.Silu, r=[pg.t], w=[tmp.t])
                        pu = bank("a")
                        mm(pu.f, [(su3[:, kc, ji * 128:(ji + 1) * 128], H23[:, kc, tsl]) for kc in range(8)], r=[su.t, H2.t], w=[pu.t])
                        tt("dve", HID3[:, jp + ji, tsl], pu.f, tmp.f, ALU.mult, r=[pu.t, tmp.t], w=[HID.t])
            for m in range(8):
                pss = [bank("c"), bank("c")]
                s = wload(lambda s, m=m, half=half: [(s.r[:, 0:1408], wdown[l, half, m].bitcast(F32R))])
                s3 = s.r[:, 0:1408].rearrange("p (j n) -> p j n", j=11)
                for g in range(2):
                    def fn(e, s3=s3, g=g, pss=pss):
                        ins = None
                        for j in range(11):
                            ins = e.matmul(pss[g].f, lhsT=s3[:, j, :], rhs=HID3[:, j, g * 512:(g + 1) * 512],
                                           start=(j == 0), stop=(j == 10))
                        return ins
                    P.op("pe", fn, r=[s.t, HID.t], w=[pss[g].t])
                for g in range(2):
                    stt("dve", xg(m, g), pss[g].f, modv(l, 40 + m, g), xg(m, g), ALU.mult, ALU.add, r=[pss[g].t, MOD[l].t, XT[m][g]], w=[XT[m][g]])

    ada_mm()
    H3G = [TSM.H.r.rearrange("p (c t) -> p c t", c=8), TSA.H.r.rearrange("p (c t) -> p c t", c=8)]
    norm_pre(lambda g, c: H3G[g][:, c, :], lambda g, c: (TSM, TSA)[g].H.t)
    ada_fin()
    for l in range(DEPTH):
        scl_prep(l)
        stage1_both(l, pre=(l == 0))
        mixer(l, 0, part=2)
        mixer(l, 1, part=2)
        ffn(l)
    for g in range(2):
        O3r = H2.r.rearrange("p (c t) -> p c t", c=8)
        tsl = slice(g * 512, (g + 1) * 512)
        ps = bank("a")
        for c in range(8):
            act(O3r[:, c, tsl], xg(c, g), AF.Square, r=[XT[c][g]], w=[H2.t])
        mm(ps.f, [(ONES_r, O3r[:, c, tsl]) for c in range(8)], r=[CST.t, H2.t], w=[ps.t])
        rstd_from_ps(ps, D)
        dst = (ypT if g == 0 else ysT).rearrange("(c p) t -> c p t", p=128)
        stg = [T1, T2, T3]
        for c in range(8):
            st_ = stg[c % 3]
            stt("dve", st_.f, xg(c, g), sp(SP_GFIN + c), RS.f, ALU.mult, ALU.mult, r=[XT[c][g], SPR.t, RS.t], w=[st_.t])
            dma("pool", "oy%d" % (c % 3), [(dst[c], st_.f)], r=[st_.t], w=[])

    P.emit(stack)
    stack.close()
    return nc, P


def _consts(j):
    cst = np.zeros((128, NCST), np.float32)
    cst[:, 0:128] = 1.0
    R = np.zeros((32, 32), np.float32)
    for blk in (0, 16):
        for i in range(8):
            R[blk + 8 + i, blk + i] = -1.0
            R[blk + i, blk + 8 + i] = 1.0
    cst[64:96, 128 + 64:128 + 96] = R
    cst[64, 224:288] = 1.0
    pos = np.arange(j * 512, (j + 1) * 512)
    row, col = (pos // 64).astype(np.float32), (pos % 64).astype(np.float32)
    inv = (10000.0 ** (-np.arange(0, 16, 2, dtype=np.float32) / 16)).astype(np.float32)
    ar, ac = row[None, :] * inv[:, None], col[None, :] * inv[:, None]
    cosT = np.concatenate([np.cos(ar), np.cos(ar), np.cos(ac), np.cos(ac)], 0)
    sinT = np.concatenate([np.sin(ar), np.sin(ar), np.sin(ac), np.sin(ac)], 0)
    cst[64:96, 288:800] = cosT
    cst[64:96, 800:1312] = sinT
    k = np.arange(64)
    a64 = 2 * np.pi * np.outer(k, k) / 64
    C64, S64 = np.cos(a64) / 8.0, np.sin(a64) / 8.0
    bd = np.zeros((128, 256))
    for b in range(2):
        bd[b * 64:(b + 1) * 64, b * 64:(b + 1) * 64] = C64
        bd[b * 64:(b + 1) * 64, 128 + b * 64:128 + (b + 1) * 64] = S64
    cst[:, 1312:1568] = bd
    n = np.arange(256)
    a = 2 * np.pi * np.outer(n, n) / 256
    CL, NSL = np.cos(a) / 16.0, -np.sin(a) / 16.0
    cst[:, C_CL:C_CL + 512] = CL.reshape(2, 128, 256).transpose(1, 0, 2).reshape(128, 512)
    cst[:, C_NSL:C_NSL + 512] = NSL.reshape(2, 128, 256).transpose(1, 0, 2).reshape(128, 512)
    cst[:, C_EPS] = EPS
    cst[:, C_ID:C_ID + 128] = np.eye(128, dtype=np.float32)
    return cst


def _dft_s(j):
    n = np.arange(2048, dtype=np.int64)
    m = np.arange(j * 512, (j + 1) * 512, dtype=np.int64)
    a = 2 * np.pi * ((np.outer(n, m) % 2048).astype(np.float64)) / 2048
    s = 1.0 / math.sqrt(2048.0)
    return np.stack([np.cos(a) * s, -np.sin(a) * s]).astype(np.float32)


_CACHE = {}


def kernel(**inputs):
    f = lambda k: np.ascontiguousarray(np.asarray(inputs[k], dtype=np.float32))
    if "nc" not in _CACHE:
        _CACHE["nc"] = build_program()[0]
    nc = _CACHE["nc"]
    xp, xs = f("x_prompt"), f("x_sample")
    cckv, ckr, cc, cctx = f("cache_ckv"), f("cache_krope"), f("c"), f("c_ctx")
    wnames = ["w_in", "w_qb", "w_kvb", "w_out", "w_ffn_gate", "w_ffn_up"]
    W = {k: f(k) for k in wnames}
    win, wo, wpw, wsc, wfn, wdn = f("w_in"), f("w_o_mla"), f("w_conf_pw"), f("w_sc_out"), f("w_fn"), f("w_ffn_down")
    wm = np.zeros((DEPTH, 8, 128, 5888), np.float32)
    for dm in range(8):
        d0 = dm * 128
        for b in range(4):
            blk = win[:, :, OFF_GATE + b * D + d0:OFF_GATE + b * D + d0 + 128].reshape(DEPTH, 8, 128, 128).transpose(0, 2, 1, 3).reshape(DEPTH, 128, 1024)
            off = b * 1024 if b < 3 else 3072
            wm[:, dm, :, off:off + 1024] = blk
        wm[:, dm, 0:64, 4096:5120] = wo[:, :, d0:d0 + 128].reshape(DEPTH, 8, 64, 128).transpose(0, 2, 1, 3).reshape(DEPTH, 64, 1024)
        for i, wsrc in enumerate((wpw, wsc, wfn)):
            wm[:, dm, :, 5120 + i * 256:5120 + (i + 1) * 256] = wsrc[:, :, d0:d0 + 128].reshape(DEPTH, 2, 128, 128).transpose(0, 2, 1, 3).reshape(DEPTH, 128, 256)
    W["wmerge"] = wm
    W["wdown"] = np.ascontiguousarray(wdn.reshape(DEPTH, 2, 11, 128, 8, 128).transpose(0, 1, 4, 3, 2, 5).reshape(DEPTH, 2, 8, 128, 1408))
    wada = f("w_ada")

    def colsT(v):
        return v.reshape(-1, 128).T

    in_maps = []
    for c in range(8):
        b, j = c // 4, c % 4
        spm = np.zeros((128, NSP), np.float32)
        for l in range(DEPTH):
            o = l * NSP_L
            spm[:, o + SP_GN1:o + SP_GN1 + 8] = colsT(f("g_norm1")[l])
            spm[:, o + SP_GN2:o + SP_GN2 + 8] = colsT(f("g_norm2")[l])
            spm[:, o + SP_GQA:o + SP_GQA + 3] = colsT(f("g_qa")[l])
            spm[:, o + SP_GKVA:o + SP_GKVA + 2] = colsT(f("g_kva")[l])
            spm[:, o + SP_BCDW:o + SP_BCDW + 2] = colsT(f("b_conf_dw")[l])
            spm[:, o + SP_GCLN:o + SP_GCLN + 2] = colsT(f("g_conf_ln")[l])
            spm[:, o + SP_BCLN:o + SP_BCLN + 2] = colsT(f("b_conf_ln")[l])
            spm[:, o + SP_WSC:o + SP_WSC + 6] = f("w_sc_conv")[l].reshape(3, 2, 128).transpose(2, 0, 1).reshape(128, 6)
            spm[:, o + SP_WCDW:o + SP_WCDW + 62] = f("w_conf_dw")[l].reshape(31, 2, 128).transpose(2, 0, 1).reshape(128, 62)
            spm[:, o + SP_BADA:o + SP_BADA + 12] = colsT(f("b_ada")[l][j * 1536:(j + 1) * 1536])
        spm[:, SP_GFIN:SP_GFIN + 8] = colsT(f("g_final"))
        spm[:, SP_CCTX:SP_CCTX + 8] = colsT(cctx)
        spm[:, SP_CLAT:SP_CLAT + 8] = colsT(cc[0])
        spm[:, SP_CLAT + 8:SP_CLAT + 16] = colsT(cc[1])
        m = {
            "xpT": np.ascontiguousarray(xp[2 * c:2 * c + 2].reshape(512, D).T),
            "xsT": np.ascontiguousarray(xs[b, j * 512:(j + 1) * 512].T),
            "cckvT": np.ascontiguousarray(cckv[b].transpose(0, 2, 1)),
            "ckrT": np.ascontiguousarray(ckr[b].transpose(0, 2, 1)),
            "smallp": spm,
            "cst": _consts(j),
            "dftS": _dft_s(j),
            "w_ada_s": np.ascontiguousarray(wada[:, :, j * 1536:(j + 1) * 1536]),
        }
        m.update(W)
        in_maps.append(m)
    res = run_bass_kernel_spmd(nc, in_maps, core_ids=list(range(8))).results
    y_prompt = np.zeros((16, 256, D), np.float32)
    y_sample = np.zeros((2, 2048, D), np.float32)
    new_ckv = np.zeros((16, DEPTH, 256, 256), np.float32)
    new_kr = np.zeros((16, DEPTH, 256, 32), np.float32)
    for c in range(8):
        b, j = c // 4, c % 4
        r = res[c]
        y_prompt[2 * c:2 * c + 2] = r["ypT"].T.reshape(2, 256, D)
        y_sample[b, j * 512:(j + 1) * 512] = r["ysT"].T
        new_ckv[2 * c:2 * c + 2] = r["nckvT"].transpose(2, 0, 1).reshape(2, 256, DEPTH, 256).transpose(0, 2, 1, 3)
        new_kr[2 * c:2 * c + 2] = r["nkrT"].transpose(2, 0, 1).reshape(2, 256, DEPTH, 32).transpose(0, 2, 1, 3)
    return (y_prompt, y_sample, new_ckv, new_kr)
```

```python
import math
import os
from contextlib import ExitStack
import numpy as np
import concourse.bass as bass
import concourse.mybir as mybir
from concourse.bass_utils import run_bass_kernel_spmd

F32 = mybir.dt.float32
F32R = mybir.dt.float32r
AF = mybir.ActivationFunctionType
ALU = mybir.AluOpType

D = 1024
DEPTH = 2
NH = 8
FF = 2816
IN_COLS = 6304
OFF_GATE = 2208
EPS = 1e-6
EXROWS = 1056
EX_CKV, EX_KR, EX_FN, EX_UG, EX_VP = 0, 256, 288, 544, 800

SP_GN1, SP_GN2, SP_GQA, SP_GKVA, SP_BCDW, SP_GCLN, SP_BCLN, SP_WSC, SP_WCDW, SP_BADA = 0, 8, 16, 19, 21, 23, 25, 27, 33, 95
NSP_L = 143
SP_GFIN = 2 * NSP_L
SP_CCTX = SP_GFIN + 8
SP_CLAT = SP_CCTX + 8
NSP = SP_CLAT + 16
NCST = 2816
C_ID = 2688
C_ONES, C_R96, C_SEL, C_COS, C_SIN, C_BD, C_CL, C_NSL, C_EPS, C_ZERO = 0, 128, 224, 288, 800, 1312, 1568, 2080, 2592, 2624

SAME_ENGINE_SYNC = True


class Tile:
    def __init__(self, space, start, end, name):
        self.space, self.start, self.end, self.name = space, start, end, name
        self.last_w = None
        self.readers = []


class Prog:
    ENG = ("pe", "act", "dve", "pool", "sp")

    def __init__(self, nc):
        self.nc = nc
        self.ops = []
        self.tiles = {}
        self.keys = {}
        self.mute = False

    def tile(self, space, start, end, name):
        t = Tile(space, start, end, name)
        self.tiles.setdefault(space, []).append(t)
        return t

    def _overl(self, t):
        return [u for u in self.tiles[t.space] if u.start < t.end and t.start < u.end]

    def op(self, eng, fn, r=(), w=(), kind="c", nd=1, key=None):
        if self.mute:
            return -1
        i = len(self.ops)
        deps = set()
        for t in r:
            for u in self._overl(t):
                if u.last_w is not None:
                    deps.add(u.last_w)
        for t in w:
            for u in self._overl(t):
                if u.last_w is not None:
                    deps.add(u.last_w)
                deps.update(u.readers)
        deps.discard(i)
        self.ops.append(dict(eng=eng, fn=fn, deps=deps, kind=kind, nd=nd, key=key, inc=False))
        for t in r:
            t.readers.append(i)
        for t in w:
            t.last_w = i
            t.readers = []
        return i

    def emit(self, stack):
        nc = self.nc
        ops = self.ops
        for o in ops:
            for d in o["deps"]:
                src = ops[d]
                if src["kind"] == "c" and src["eng"] == o["eng"] and (o["eng"] == "pe" or not SAME_ENGINE_SYNC):
                    continue
                src["inc"] = True
        esem = {e: stack.enter_context(nc.semaphore("es_" + e)) for e in self.ENG}
        ksem = {}
        ecnt = {e: 0 for e in self.ENG}
        kcnt = {}
        for o in ops:
            if o["kind"] == "c":
                if o["inc"]:
                    ecnt[o["eng"]] += 1
                    o["tok"] = (esem[o["eng"]], ecnt[o["eng"]])
                else:
                    o["tok"] = None
            else:
                k = o["key"]
                if k not in ksem:
                    ksem[k] = stack.enter_context(nc.semaphore("ks_" + str(k)))
                    kcnt[k] = 0
                kcnt[k] += (16 * o["nd"]) if o["kind"] == "d" else 1
                o["tok"] = (ksem[k], kcnt[k])
        self.nsem = len(esem) + len(ksem)
        final = dict((id(ksem[k]), (ksem[k], kcnt[k])) for k in ksem)
        streams = {e: [o for o in ops if o["eng"] == e] for e in self.ENG}
        block = stack.enter_context(nc.Block())

        def run(eng_handle, ename):
            known = {}
            for o in streams[ename]:
                need = {}
                for d in o["deps"]:
                    src = ops[d]
                    if src["kind"] == "c" and src["eng"] == ename and (ename == "pe" or not SAME_ENGINE_SYNC):
                        continue
                    sem, cnt = src["tok"]
                    if need.get(id(sem), (None, 0))[1] < cnt:
                        need[id(sem)] = (sem, cnt)
                for sid, (sem, cnt) in need.items():
                    if known.get(sid, 0) < cnt:
                        eng_handle.wait_ge(sem, cnt)
                        known[sid] = cnt
                res = o["fn"](eng_handle)
                if o["kind"] == "c":
                    if o["inc"]:
                        res.then_inc(o["tok"][0], 1)
                elif o["kind"] == "d":
                    for ins in res:
                        ins.then_inc(o["tok"][0], 16)
                else:
                    res.then_inc(o["tok"][0])
            if ename == "pool":
                for sid, (sem, cnt) in final.items():
                    if known.get(sid, 0) < cnt:
                        eng_handle.wait_ge(sem, cnt)

        @block.sync
        def _(e):
            run(e, "sp")

        @block.tensor
        def _(e):
            run(e, "pe")

        @block.scalar
        def _(e):
            run(e, "act")

        @block.vector
        def _(e):
            run(e, "dve")

        @block.gpsimd
        def _(e):
            run(e, "pool")


def build_program(debug=False):
    nc = bass.Bass("TRN2", target_bir_lowering=False)
    nc.dge_precook = False
    P = Prog(nc)
    stack = ExitStack()

    def din(name, shape):
        return nc.dram_tensor(name, list(shape), F32, kind="ExternalInput").ap()

    def dout(name, shape):
        return nc.dram_tensor(name, list(shape), F32, kind="ExternalOutput").ap()

    xpT = din("xpT", [D, 512])
    xsT = din("xsT", [D, 512])
    cckvT = din("cckvT", [DEPTH, 256, 256])
    ckrT = din("ckrT", [DEPTH, 32, 256])
    smallp = din("smallp", [128, NSP])
    cst = din("cst", [128, NCST])
    dftS = din("dftS", [2, 2048, 512])
    w_ada_s = din("w_ada_s", [DEPTH, D, 1536])
    w_in = din("w_in", [DEPTH, D, IN_COLS])
    w_qb = din("w_qb", [DEPTH, 384, 768])
    w_kvb = din("w_kvb", [DEPTH, 256, 1024])
    w_out = din("w_out", [DEPTH, D, D])
    w_ffn_gate = din("w_ffn_gate", [DEPTH, D, FF])
    w_ffn_up = din("w_ffn_up", [DEPTH, D, FF])
    wmerge = din("wmerge", [DEPTH, 8, 128, 5888])
    wdown = din("wdown", [DEPTH, 2, 8, 128, 1408])

    ypT = dout("ypT", [D, 512])
    ysT = dout("ysT", [D, 512])
    nckvT = dout("nckvT", [DEPTH, 256, 512])
    nkrT = dout("nkrT", [DEPTH, 32, 512])

    RG = [[0, 1, 2, 3], [4, 5, 6, 7]]
    EXKV = nc.dram_tensor("exkv", [288, 512], F32).ap()
    GAKV = nc.dram_tensor("gakv", [4 * 288, 512], F32).ap()
    EXFN = nc.dram_tensor("exfn", [256, 512], F32).ap()
    GAFN = nc.dram_tensor("gafn", [4 * 256, 512], F32).ap()
    EXE = nc.dram_tensor("exe", [512, 32], F32).ap()
    GAE = nc.dram_tensor("gae", [6 * 512, 32], F32).ap()
    T_EXKV, T_GAKV = P.tile("d1", 0, 1, "EXKV"), P.tile("d2", 0, 1, "GAKV")
    T_EXFN, T_GAFN = P.tile("d3", 0, 1, "EXFN"), P.tile("d4", 0, 1, "GAFN")
    T_EXE, T_GAE = P.tile("d5", 0, 1, "EXE"), P.tile("d6", 0, 1, "GAE")
    SPL = nc.dram_tensor("spill", [128, 4736], F32).ap()
    T_SPL = P.tile("d9", 0, 1, "SPL")
    T_OUT = P.tile("dram_out", 0, 1, "OUT")

    NWR = 40640
    NWF = 12560
    SBR = stack.enter_context(nc.sbuf_tensor("SBR", [128, NWR], F32))
    SBF = stack.enter_context(nc.sbuf_tensor("SBF", [128, NWF], F32))
    PS = stack.enter_context(nc.psum_tensor("PS", [128, 8 * 512], F32))

    class V:
        def __init__(self, sp_, off, n, name):
            T_, lim = (SBR, NWR) if sp_ == "r" else (SBF, NWF)
            assert off + n <= lim, (name, off, n)
            self.off, self.n = off, n
            self.t = P.tile("sb" + sp_, off, off + n, name)
            self.f = T_[:, off:off + n]
            self.r = T_[:, off:off + n].bitcast(F32R)

    oo = {"r": 0, "f": 0}

    def take(n, name, sp_="r"):
        v = V(sp_, oo[sp_], n, name)
        oo[sp_] += n
        return v

    X = take(8 * 1024, "X", "f")
    CSTF = take(1088, "CSTF", "f")
    SPR = take(NSP, "SPR", "f")
    MOD = [take(96, "MOD%d" % l, "f") for l in range(DEPTH)]
    SCL = take(64, "SCL", "f")
    RS = take(512, "RS", "f")
    T1 = take(512, "T1", "f")
    T2 = take(512, "T2", "f")
    T3 = take(512, "T3", "f")
    IDENT = take(128, "IDENT", "f")
    HAL = take(128, "HAL", "f")

    RING = [take(3072, "ring%d" % i) for i in range(3)]
    CST = take(1568, "CST")
    SILC = take(32, "SILC")
    base = oo["r"]
    H = take(8 * 512, "H")
    QA = take(3 * 512, "QA")
    CKV = take(2 * 512, "CKV")
    KR96 = take(512, "KR96")
    UG = take(1144, "UG")
    VP = take(1032, "VP")
    GB = take(1024, "GB")
    FN = take(1024, "FN")
    F2 = take(1024, "F2")
    U2 = take(1024, "U2")
    OH = take(8 * 512, "OH")
    at0 = oo["r"]
    KH = take(8 * 512, "KH")
    VB = take(4 * 8 * 65, "VB")
    CKVB = take(1024, "CKVB")
    KRB = take(512, "KRB")
    ET = [take(512, "ET%d" % i) for i in range(3)]
    QH = V("r", H.off, 8 * 512, "QH")
    MERGED = V("r", at0, 8 * 512, "MERGED")
    FNFULL = V("r", at0, 4096, "FNFULL")
    AB = V("r", at0 + 4096, 8192, "AB")
    H2 = V("r", base, 8 * 1024, "H2")
    HID = V("r", base + 8 * 1024, 11 * 1024, "HID")
    OHh = [V("r", OH.off + h * 512, 512, "OHh%d" % h) for h in range(8)]
    ABt = [V("r", AB.off + i * 512, 512, "ABt%d" % i) for i in range(16)]
    KHq = [[V("r", KH.off + h * 512 + q * 256, 256, "KH%d_%d" % (h, q)) for q in range(2)] for h in range(8)]
    QHh = [V("r", QH.off + h * 512, 512, "QHh%d" % h) for h in range(8)]
    VBc = [V("r", VB.off + c * 520, 520, "VBc%d" % c) for c in range(4)]
    H2T = [[V("r", H2.off + c * 1024 + g * 512, 512, "H2_%d_%d" % (c, g)).t for g in range(2)] for c in range(8)]
    SQ = V("r", F2.off, 2048, "SQ")

    class TSet:
        pass
    TSM, TSA = TSet(), TSet()
    TSM.H, TSM.QA, TSM.CKV, TSM.KR96, TSM.UG, TSM.VP, TSM.GB, TSM.FN = H, QA, CKV, KR96, UG, VP, GB, FN
    ao = OH.off
    for nm, n in (("H", 4096), ("QA", 1536), ("CKV", 1024), ("KR96", 512), ("UG", 1144), ("VP", 1032), ("GB", 1024), ("FN", 1024)):
        setattr(TSA, nm, V("r", ao, n, nm + "_S"))
        ao += n
    assert ao <= NWR

    class B:
        def __init__(self, i):
            self.t = P.tile("ps", i * 512, (i + 1) * 512, "bank%d" % i)
            self.f = PS[:, i * 512:(i + 1) * 512]

    BANKS = [B(i) for i in range(8)]
    rr = {"a": 0, "b": 0, "c": 0}
    pools = {"a": [0, 1, 2], "b": [3, 4, 5], "c": [6, 7]}

    def bank(pool="a"):
        lst = pools[pool]
        b = BANKS[lst[rr[pool] % len(lst)]]
        rr[pool] += 1
        return b

    def dma(queue, key, pairs, r=(), w=()):
        def fn(e, pairs=pairs):
            return [e.dma_start(out=a, in_=b) for a, b in pairs]
        return P.op(queue, fn, r=r, w=w, kind="d", nd=len(pairs), key=key)

    ring_i = [0]

    def wload(pairs_fn):
        s = RING[ring_i[0] % len(RING)]
        k = "ring%d" % (ring_i[0] % len(RING))
        ring_i[0] += 1
        dma("sp", k, pairs_fn(s), w=[s.t])
        return s

    def mm(out_ap, pairs, r, w, plain=False):
        n = len(pairs)

        def fn(e, pairs=pairs, out_ap=out_ap):
            ins = None
            for i, (l, rh) in enumerate(pairs):
                ins = e.matmul(out_ap, lhsT=l, rhs=rh, start=(i == 0), stop=(i == n - 1))
            return ins
        return P.op("pe", fn, r=r, w=w)

    def act(out, in_, func, r, w, bias=None, scale=None):
        kw = {}
        if bias is not None:
            kw["bias"] = bias
        if scale is not None:
            kw["scale"] = scale
        return P.op("act", lambda e: e.activation(out, in_, func, **kw), r=r, w=w)

    def tt(eng, out, a, b, op, r, w):
        return P.op(eng, lambda e: e.tensor_tensor(out, a, b, op), r=r, w=w)

    def ts(eng, out, a, s1, s2, op0, op1, r, w):
        if s2 is None:
            return P.op(eng, lambda e: e.tensor_scalar(out, a, s1, None, op0), r=r, w=w)
        return P.op(eng, lambda e: e.tensor_scalar(out, a, s1, s2, op0, op1), r=r, w=w)

    def stt(eng, out, a, s, b, op0, op1, r, w):
        return P.op(eng, lambda e: e.scalar_tensor_tensor(out, a, s, b, op0, op1), r=r, w=w)

    def cp(eng, out, in_, r, w):
        if eng == "act":
            return P.op("act", lambda e: e.copy(out, in_), r=r, w=w)
        return P.op(eng, lambda e: e.tensor_copy(out, in_), r=r, w=w)

    def sp(col, n=1):
        return SPR.f[:, col:col + n]

    ONES_r = CST.r[:, 0:128]
    R96_r = CST.r[:, 128:224]
    SEL_r = CST.r[:, 224:288]
    BD_r = CST.r[:, 288:544]
    COS_f = CSTF.f[:, 0:512]
    SIN_f = CSTF.f[:, 512:1024]
    EPS_f = CSTF.f[:, 1024:1025]
    ZERO_f = CSTF.f[:, 1056:1088]

    dma("pool", "c0", [(CST.r[:, 0:288], cst[:, 0:288].bitcast(F32R)), (CST.r[:, 288:1568], cst[:, C_BD:C_BD + 1280].bitcast(F32R)),
                       (CSTF.f[:, 0:1024], cst[:, C_COS:C_COS + 1024]), (CSTF.f[:, 1024:1088], cst[:, C_EPS:C_EPS + 64]),
                       (IDENT.f, cst[:, C_ID:C_ID + 128]), (SPR.f, smallp)], w=[CST.t, CSTF.t, SPR.t, IDENT.t])
    dma("pool", "x0", [(X.f.rearrange("p (c t) -> p c t", c=8)[:, :, 0:512], xpT.rearrange("(c p) t -> p c t", p=128)),
                       (X.f.rearrange("p (c t) -> p c t", c=8)[:, :, 512:1024], xsT.rearrange("(c p) t -> p c t", p=128))],
        w=[X.t])
    zp = []
    for ch in range(4):
        zp.append((GAE[ch * 128:(ch + 1) * 128, :], ZERO_f))
        zp.append((GAE[5 * 512 + ch * 128:5 * 512 + (ch + 1) * 128, :], ZERO_f))
    dma("pool", "zp", zp, r=[CSTF.t], w=[T_GAE])

    X3 = X.f.rearrange("p (c t) -> p c t", c=8)

    def xg(c, g):
        return X3[:, c, g * 512:(g + 1) * 512]

    XT = [[V("f", X.off + c * 1024 + g * 512, 512, "X%d_%d" % (c, g)).t for g in range(2)] for c in range(8)]

    EXM = nc.dram_tensor("exm", [384, 24], F32).ap()
    GAM = nc.dram_tensor("gam", [5 * 384, 24], F32).ap()
    T_EXM, T_GAM = P.tile("d7", 0, 1, "EXM"), P.tile("d8", 0, 1, "GAM")
    SILC3 = SILC.r.rearrange("p (c t) -> p c t", t=4)
    for v in range(4):
        act(SILC3[:, :, v], sp(SP_CCTX + 8 * (v % 3), 8), AF.Silu, r=[SPR.t, SILC.t], w=[SILC.t])

    ADAT = take(272, "ADAT", "f")

    def ada_mm():
        pb = bank("c")
        for l in range(DEPTH):
            for pnl in range(4):
                s = wload(lambda s, l=l, pnl=pnl: [(s.r.rearrange("p (k n) -> p k n", k=8),
                                                    w_ada_s[l, :, pnl * 384:(pnl + 1) * 384].bitcast(F32R).rearrange("(k p) n -> p k n", p=128))])
                s3 = s.r.rearrange("p (k n) -> p k n", k=8)
                for mi3 in range(3):
                    idx = l * 12 + pnl * 3 + mi3
                    mm(pb.f[:, idx * 4:(idx + 1) * 4], [(s3[:, kc, mi3 * 128:(mi3 + 1) * 128], SILC3[:, kc, :]) for kc in range(8)],
                       r=[s.t, SILC.t], w=[pb.t])
        MODP3 = ADAT.f[:, 0:72].rearrange("p (v c) -> p v c", v=3)
        pb3 = pb.f[:, 0:96].rearrange("p (c v) -> p c v", v=4)
        for v in range(3):
            for l in range(DEPTH):
                tt("dve", MODP3[:, v, l * 12:(l + 1) * 12], pb3[:, l * 12:(l + 1) * 12, v], sp(l * NSP_L + SP_BADA, 12), ALU.add,
                   r=[pb.t, SPR.t, ADAT.t], w=[ADAT.t])
        dma("pool", "exm", [(EXM.rearrange("(v p) c -> p v c", p=128), MODP3)], r=[ADAT.t], w=[T_EXM])
        P.op("pool", lambda e: e.collective_compute("AllGather", ALU.bypass, replica_groups=RG,
                                                    ins=[EXM.opt()], outs=[GAM[0:1536].opt()]),
             r=[T_EXM, T_GAM], w=[T_GAM], kind="cc", key="ccm")
        STG4 = ADAT.f[:, 72:264].rearrange("p (t r c) -> p t r c", t=2, r=4)

        def ld(e):
            pid = e.partition_id()
            vb = (pid // 4) * 128 + 128
            return [e.dma_start(out=STG4[:, 0], in_=GAM[0:1536].rearrange("(r x) c -> x r c", x=384)[0:128]),
                    e.dma_start(out=STG4[:, 1], in_=GAM[bass.DynSlice(vb, 1536), :].rearrange("(r x) c -> x r c", x=384)[0:128])]
        P.op("pool", ld, r=[T_GAM], w=[ADAT.t], kind="d", nd=2, key="ldm")

    def ada_fin():
        STG4 = ADAT.f[:, 72:264].rearrange("p (t r c) -> p t r c", t=2, r=4)
        for l in range(DEPTH):
            M4 = MOD[l].f.rearrange("p (r i t) -> p r i t", r=4, i=12)
            for t in range(2):
                cp("dve", M4[:, :, :, t], STG4[:, t, :, l * 12:(l + 1) * 12], r=[ADAT.t, MOD[l].t], w=[MOD[l].t])

    def modv(l, m, t):
        return MOD[l].f[:, 2 * m + t:2 * m + t + 1]

    def scl_prep(l):
        M3 = MOD[l].f.rearrange("p (m t) -> p m t", t=2)
        S3 = SCL.f[:, 0:32].rearrange("p (a c t) -> p a c t", a=2, t=2)
        for which, (scm, gcol) in enumerate(((8, SP_GN1), (32, SP_GN2))):
            for t in range(2):
                stt("dve", S3[:, which, :, t], M3[:, scm:scm + 8, t], 1.0, sp(l * NSP_L + gcol, 8), ALU.add, ALU.mult,
                    r=[MOD[l].t, SPR.t, SCL.t], w=[SCL.t])

    def sclv(which, c, t):
        k = which * 16 + c * 2 + t
        return SCL.f[:, k:k + 1]

    def rstd_from_ps(ps, n_feat, ntok=512, eps=EPS, tmp=None, rs=None):
        tmp = tmp or T1
        rs = rs or RS
        act(tmp.f[:, 0:ntok], ps.f[:, 0:ntok], AF.Ln, r=[ps.t, CSTF.t], w=[tmp.t], bias=EPS_f, scale=1.0 / n_feat)
        act(rs.f[:, 0:ntok], tmp.f[:, 0:ntok], AF.Exp, r=[tmp.t], w=[rs.t], scale=-0.5)

    def norm_pre(dview, dtile):
        pss = [bank("a"), bank("a")]
        rs = [RS, T3]
        tmp = [T1, T2]
        for c in range(8):
            act(dview(0, c), xg(c, 0), AF.Square, r=[XT[c][0]], w=[dtile(0, c)])
            tt("dve", dview(1, c), xg(c, 1), xg(c, 1), ALU.mult, r=[XT[c][1]], w=[dtile(1, c)])
        for g in range(2):
            mm(pss[g].f, [(ONES_r, dview(g, c)) for c in range(8)], r=[CST.t] + [dtile(g, c) for c in range(8)], w=[pss[g].t])
        for g in range(2):
            rstd_from_ps(pss[g], D, tmp=tmp[g], rs=rs[g])

    def norm_both(l, which, dview, dtile, pre=False):
        shm = 0 if which == 0 else 24
        rs = [RS, T3]
        tmp = [T1, T2]
        if not pre:
            norm_pre(dview, dtile)
        for c in range(8):
            for g in range(2):
                tt("dve", tmp[g].f, xg(c, g), rs[g].f, ALU.mult, r=[XT[c][g], rs[g].t], w=[tmp[g].t])
                act(dview(g, c), tmp[g].f, AF.Identity, r=[tmp[g].t, SCL.t, MOD[l].t], w=[dtile(g, c)],
                    bias=modv(l, shm + c, g), scale=sclv(which, c, g))

    def adanorm(l, g, which, dst):
        t = 0 if g == 0 else 1
        d3 = dst.r.rearrange("p (c t) -> p c t", c=8)
        ps = bank("a")
        for c in range(8):
            act(d3[:, c, :], xg(c, g), AF.Square, r=[XT[c][g]], w=[dst.t])
        mm(ps.f, [(ONES_r, d3[:, c, :]) for c in range(8)], r=[CST.t, dst.t], w=[ps.t])
        rstd_from_ps(ps, D)
        shm = 0 if which == 0 else 24
        for c in range(8):
            tt("dve", T1.f, xg(c, g), RS.f, ALU.mult, r=[XT[c][g], RS.t], w=[T1.t])
            act(d3[:, c, :], T1.f, AF.Identity, r=[T1.t, SCL.t, MOD[l].t], w=[dst.t], bias=modv(l, shm + c, t), scale=sclv(which, c, t))

    def stage1_both(l, pre=False):
        spb = l * NSP_L
        TS = [TSM, TSA]
        geo = [(2, 256), (1, 512)]
        H3 = [TS[g].H.r.rearrange("p (c t) -> p c t", c=8) for g in range(2)]
        norm_both(l, 0, lambda g, c: H3[g][:, c, :], lambda g, c: TS[g].H.t, pre=pre)
        QA3 = [TS[g].QA.r.rearrange("p (c t) -> p c t", c=3) for g in range(2)]
        CKV3 = [TS[g].CKV.r.rearrange("p (c t) -> p c t", c=2) for g in range(2)]
        CKV3f = [TS[g].CKV.f.rearrange("p (c t) -> p c t", c=2) for g in range(2)]
        GBr3 = [TS[g].GB.r.rearrange("p (c t) -> p c t", c=2) for g in range(2)]
        FN3 = [TS[g].FN.r.rearrange("p (c t) -> p c t", c=2) for g in range(2)]
        UGv = [TS[g].UG.r[:, 0:2 * geo[g][0] * (geo[g][1] + 30)].rearrange("p (c s t) -> p c s t", c=2, s=geo[g][0]) for g in range(2)]
        VPv = [TS[g].VP.r[:, 0:2 * geo[g][0] * (geo[g][1] + 2)].rearrange("p (c s t) -> p c s t", c=2, s=geo[g][0]) for g in range(2)]
        UGf = [TS[g].UG.f[:, 0:2 * geo[g][0] * (geo[g][1] + 30)].rearrange("p (c s t) -> p c s t", c=2, s=geo[g][0]) for g in range(2)]
        VPf = [TS[g].VP.f[:, 0:2 * geo[g][0] * (geo[g][1] + 2)].rearrange("p (c s t) -> p c s t", c=2, s=geo[g][0]) for g in range(2)]
        SQ3 = SQ.r.rearrange("p (c t) -> p c t", c=4)

        def panel(c0, ncol):
            s = wload(lambda s: [(s.r[:, 0:8 * ncol].rearrange("p (k n) -> p k n", k=8),
                                  w_in[l, :, c0:c0 + ncol].bitcast(F32R).rearrange("(k p) n -> p k n", p=128))])
            return s, s.r[:, 0:8 * ncol].rearrange("p (k n) -> p k n", k=8)

        def s1(s, s3, m0, M, g):
            ps = bank("a")
            mm(ps.f[0:M, :], [(s3[:, kc, m0:m0 + M], H3[g][:, kc, :]) for kc in range(8)], r=[s.t, TS[g].H.t], w=[ps.t])
            return ps

        def seg(ap, g):
            return ap.rearrange("p (s t) -> p s t", s=geo[g][0])

        for c in range(2):
            for s_ in range(2):
                cp("dve", UGv[0][:, c, s_, 0:15], ZERO_f[:, 0:15], r=[CSTF.t], w=[UG.t])
                cp("dve", UGv[0][:, c, s_, 15 + 256:30 + 256], ZERO_f[:, 0:15], r=[CSTF.t], w=[UG.t])
                cp("dve", VPv[0][:, c, s_, 0:1], ZERO_f[:, 0:1], r=[CSTF.t], w=[VP.t])
                cp("dve", VPv[0][:, c, s_, 1 + 256:2 + 256], ZERO_f[:, 0:1], r=[CSTF.t], w=[VP.t])
        s, s3 = panel(0, 384)
        for c in range(3):
            for g in range(2):
                ps = s1(s, s3, c * 128, 128, g)
                cp("act", QA3[g][:, c, :], ps.f, r=[ps.t], w=[TS[g].QA.t])
        s, s3 = panel(384, 288)
        for c in range(2):
            for g in range(2):
                ps = s1(s, s3, c * 128, 128, g)
                cp("act" if g else "dve", CKV3[g][:, c, :], ps.f, r=[ps.t], w=[TS[g].CKV.t])
        for g in range(2):
            ps = s1(s, s3, 192, 96, g)
            cp("dve", TS[g].KR96.r[0:96, :], ps.f[0:96, :], r=[ps.t], w=[TS[g].KR96.t])
        for g in range(2):
            ps = bank("a")
            for c in range(2):
                act(SQ3[:, c, :], CKV3f[g][:, c, :], AF.Square, r=[TS[g].CKV.t], w=[SQ.t])
            mm(ps.f, [(ONES_r, SQ3[:, c, :]) for c in range(2)], r=[CST.t, SQ.t], w=[ps.t])
            rstd_from_ps(ps, 256)
            for c in range(2):
                stt("dve", CKV3[g][:, c, :], CKV3f[g][:, c, :], sp(spb + SP_GKVA + c), RS.f, ALU.mult, ALU.mult,
                    r=[TS[g].CKV.t, SPR.t, RS.t], w=[TS[g].CKV.t])
            ps = bank("a")
            for c in range(3):
                act(SQ3[:, c, :], QA3[g][:, c, :], AF.Square, r=[TS[g].QA.t], w=[SQ.t])
            mm(ps.f, [(ONES_r, SQ3[:, c, :]) for c in range(3)], r=[CST.t, SQ.t], w=[ps.t])
            rstd_from_ps(ps, 384)
            for c in range(3):
                stt("dve", QA3[g][:, c, :], QA3[g][:, c, :], sp(spb + SP_GQA + c), RS.f, ALU.mult, ALU.mult,
                    r=[TS[g].QA.t, SPR.t, RS.t], w=[TS[g].QA.t])
        sCb, sCb3 = panel(928, 256)
        sCa, sCa3 = panel(672, 256)
        for c in range(2):
            for g in range(2):
                L = geo[g][1]
                ps = s1(sCb, sCb3, c * 128, 128, g)
                act(UGv[g][:, c, :, 15:15 + L], seg(ps.f, g), AF.Sigmoid, r=[ps.t], w=[TS[g].UG.t])
                ps2 = s1(sCa, sCa3, c * 128, 128, g)
                tt("dve", UGv[g][:, c, :, 15:15 + L], seg(ps2.f, g), UGf[g][:, c, :, 15:15 + L], ALU.mult,
                   r=[ps2.t, TS[g].UG.t], w=[TS[g].UG.t])
        sD, sD3 = panel(1184, 256)
        for c in range(2):
            for g in range(2):
                ps = s1(sD, sD3, c * 128, 128, g)
                cp("act", GBr3[g][:, c, :], ps.f, r=[ps.t], w=[TS[g].GB.t])
        sD, sD3 = panel(1440, 256)
        for c in range(2):
            for g in range(2):
                L = geo[g][1]
                ps = s1(sD, sD3, c * 128, 128, g)
                cp("act", VPv[g][:, c, :, 1:1 + L], seg(ps.f, g), r=[ps.t], w=[TS[g].VP.t])
        sE, sE3 = panel(1696, 256)
        for c in range(2):
            for g in range(2):
                L = geo[g][1]
                ps = s1(sE, sE3, c * 128, 128, g)
                tt("dve", VPv[g][:, c, :, 1:1 + L], seg(ps.f, g), VPf[g][:, c, :, 1:1 + L], ALU.mult,
                   r=[ps.t, TS[g].VP.t], w=[TS[g].VP.t])
        sE, sE3 = panel(1952, 256)
        for c in range(2):
            for g in range(2):
                ps = s1(sE, sE3, c * 128, 128, g)
                cp("act", FN3[g][:, c, :], ps.f, r=[ps.t], w=[TS[g].FN.t])

        KRS = TSA.KR96
        ps = bank("a")
        mm(ps.f[0:96, :], [(R96_r[64:96, 0:96], KRS.r[64:96, :])], r=[CST.t, KRS.t], w=[ps.t])
        tt("dve", T1.f[64:96, :], KRS.f[64:96, :], COS_f[64:96, :], ALU.mult, r=[KRS.t, CSTF.t], w=[T1.t])
        tt("dve", T2.f[64:96, :], ps.f[64:96, :], SIN_f[64:96, :], ALU.mult, r=[ps.t, CSTF.t], w=[T2.t])
        tt("dve", KRS.r[64:96, :], T1.f[64:96, :], T2.f[64:96, :], ALU.add, r=[T1.t, T2.t], w=[KRS.t])

        dma("sp", "o_ckv%d" % l, [(nckvT[l].rearrange("(c p) t -> p c t", p=128), CKV.f.rearrange("p (c t) -> p c t", c=2)),
                                    (nkrT[l], KR96.f[64:96, :])], r=[CKV.t, KR96.t], w=[T_OUT])
        A = TSA
        dma("pool", "spill%d" % l, [(SPL[:, 0:1536], A.QA.f),
                                    (SPL[:, 1536:2560].rearrange("p (c t) -> p c t", c=2), UGf[1][:, :, 0, 15:15 + 512]),
                                    (SPL[:, 2560:3584].rearrange("p (c t) -> p c t", c=2), VPf[1][:, :, 0, 1:1 + 512]),
                                    (SPL[:, 3584:4608], A.GB.f)], r=[A.QA.t, A.UG.t, A.VP.t, A.GB.t, T_SPL], w=[T_SPL])
        dma("pool", "exkv%d" % l, [
            (EXKV[0:256].rearrange("(c p) t -> p c t", p=128), A.CKV.f.rearrange("p (c t) -> p c t", c=2)),
            (EXKV[256:288], A.KR96.f[64:96, :])], r=[A.CKV.t, A.KR96.t, T_EXKV], w=[T_EXKV])
        dma("pool", "exfn%d" % l, [
            (EXFN.rearrange("(c p) t -> p c t", p=128), A.FN.f.rearrange("p (c t) -> p c t", c=2))], r=[A.FN.t, T_EXFN], w=[T_EXFN])
        dma("pool", "exe%d" % l, [
            (EXE[0:256, 0:16].rearrange("(c p) t -> p c t", p=128), UGf[1][:, :, 0, 15:31]),
            (EXE[0:256, 16:32].rearrange("(c p) t -> p c t", p=128), UGf[1][:, :, 0, 512 - 1:512 + 15]),
            (EXE[256:512, 0:16].rearrange("(c p) t -> p c t", p=128), VPf[1][:, :, 0, 1:17]),
            (EXE[256:512, 16:32].rearrange("(c p) t -> p c t", p=128), VPf[1][:, :, 0, 512 - 15:512 + 1])],
            r=[A.UG.t, A.VP.t, T_EXE], w=[T_EXE])
        P.op("pool", lambda e: e.collective_compute("AllGather", ALU.bypass, replica_groups=RG, ins=[EXKV.opt()], outs=[GAKV.opt()]),
             r=[T_EXKV, T_EXFN, T_EXE, T_GAKV], w=[T_GAKV], kind="cc", key="cckv%d" % l)
        P.op("pool", lambda e: e.collective_compute("AllGather", ALU.bypass, replica_groups=RG, ins=[EXFN.opt()], outs=[GAFN.opt()]),
             r=[T_EXFN, T_GAFN, T_GAKV], w=[T_GAFN], kind="cc", key="ccfn%d" % l)
        P.op("pool", lambda e: e.collective_compute("AllGather", ALU.bypass, replica_groups=RG, ins=[EXE.opt()], outs=[GAE[512:5 * 512].opt()]),
             r=[T_EXE, T_GAE, T_GAFN], w=[T_GAE], kind="cc", key="cce%d" % l)
        HL = HAL.f[:, 0:64].rearrange("p (c t) -> p c t", c=4)
        HR = HAL.f[:, 64:128].rearrange("p (c t) -> p c t", c=4)

        def halo(e):
            pid = e.partition_id()
            jb = (pid % 4) * 512
            return [e.dma_start(out=HL, in_=GAE[bass.DynSlice(jb, 512), 16:32].rearrange("(c p) t -> p c t", p=128)),
                    e.dma_start(out=HR, in_=GAE[bass.DynSlice(jb + 1024, 512), 0:16].rearrange("(c p) t -> p c t", p=128))]
        P.op("pool", halo, r=[T_GAE], w=[HAL.t], kind="d", nd=2, key="halo%d" % l)

    def mixer(l, g, part=0):
        P.mute = (part == 2)
        lat = 0 if g == 0 else 1
        nseq, L = (2, 256) if g == 0 else (1, 512)
        spb = l * NSP_L
        H3 = H.r.rearrange("p (c t) -> p c t", c=8)
        adanorm(l, g, 0, H)

        def panel(c0, ncol):
            s = wload(lambda s: [(s.r[:, 0:8 * ncol].rearrange("p (k n) -> p k n", k=8),
                                  w_in[l, :, c0:c0 + ncol].bitcast(F32R).rearrange("(k p) n -> p k n", p=128))])
            return s, s.r[:, 0:8 * ncol].rearrange("p (k n) -> p k n", k=8)

        def s1(s, s3, m0, M):
            ps = bank("a")
            mm(ps.f[0:M, :], [(s3[:, kc, m0:m0 + M], H3[:, kc, :]) for kc in range(8)], r=[s.t, H.t], w=[ps.t])
            return ps

        QA3 = QA.r.rearrange("p (c t) -> p c t", c=3)
        CKV3 = CKV.r.rearrange("p (c t) -> p c t", c=2)
        GB3 = GB.f.rearrange("p (c t) -> p c t", c=2)
        FN3 = FN.r.rearrange("p (c t) -> p c t", c=2)
        UGv = UG.r[:, 0:2 * nseq * (L + 30)].rearrange("p (c s t) -> p c s t", c=2, s=nseq)
        VPv = VP.r[:, 0:2 * nseq * (L + 2)].rearrange("p (c s t) -> p c s t", c=2, s=nseq)
        UGf = UG.f[:, 0:2 * nseq * (L + 30)].rearrange("p (c s t) -> p c s t", c=2, s=nseq)
        VPf = VP.f[:, 0:2 * nseq * (L + 2)].rearrange("p (c s t) -> p c s t", c=2, s=nseq)
        if g == 0:
            for c in range(2):
                for s_ in range(nseq):
                    cp("dve", UGv[:, c, s_, 0:15], ZERO_f[:, 0:15], r=[CSTF.t], w=[UG.t])
                    cp("dve", UGv[:, c, s_, 15 + L:30 + L], ZERO_f[:, 0:15], r=[CSTF.t], w=[UG.t])
                    cp("dve", VPv[:, c, s_, 0:1], ZERO_f[:, 0:1], r=[CSTF.t], w=[VP.t])
                    cp("dve", VPv[:, c, s_, 1 + L:2 + L], ZERO_f[:, 0:1], r=[CSTF.t], w=[VP.t])
        s, s3 = panel(0, 384)
        for c in range(3):
            ps = s1(s, s3, c * 128, 128)
            cp("act", QA3[:, c, :], ps.f, r=[ps.t], w=[QA.t])
        s, s3 = panel(384, 288)
        ckv_raw = [T2, T3]
        for c in range(2):
            ps = s1(s, s3, c * 128, 128)
            cp("act", ckv_raw[c].f, ps.f, r=[ps.t], w=[ckv_raw[c].t])
        ps = s1(s, s3, 192, 96)
        cp("dve", KR96.r[0:96, :], ps.f[0:96, :], r=[ps.t], w=[KR96.t])
        ps = bank("a")
        for c in range(2):
            act(CKV3[:, c, :], ckv_raw[c].f, AF.Square, r=[ckv_raw[c].t], w=[CKV.t])
        mm(ps.f, [(ONES_r, CKV3[:, c, :]) for c in range(2)], r=[CST.t, CKV.t], w=[ps.t])
        rstd_from_ps(ps, 256)
        for c in range(2):
            stt("dve", CKV3[:, c, :], ckv_raw[c].f, sp(spb + SP_GKVA + c), RS.f, ALU.mult, ALU.mult,
                r=[ckv_raw[c].t, SPR.t, RS.t], w=[CKV.t])
        ps = bank("a")
        OHs = OH.r.rearrange("p (c t) -> p c t", c=8)
        for c in range(3):
            act(OHs[:, c, :], QA3[:, c, :], AF.Square, r=[QA.t], w=[OH.t])
        mm(ps.f, [(ONES_r, OHs[:, c, :]) for c in range(3)], r=[CST.t, OH.t], w=[ps.t])
        rstd_from_ps(ps, 384)
        for c in range(3):
            stt("dve", QA3[:, c, :], QA3[:, c, :], sp(spb + SP_GQA + c), RS.f, ALU.mult, ALU.mult,
                r=[QA.t, SPR.t, RS.t], w=[QA.t])
        sCb, sCb3 = panel(928, 256)
        sCa, sCa3 = panel(672, 256)
        for c in range(2):
            ps = s1(sCb, sCb3, c * 128, 128)
            act(T1.f, ps.f, AF.Sigmoid, r=[ps.t], w=[T1.t])
            ps2 = s1(sCa, sCa3, c * 128, 128)
            tt("dve", UGv[:, c, :, 15:15 + L], ps2.f.rearrange("p (s t) -> p s t", s=nseq), T1.f.rearrange("p (s t) -> p s t", s=nseq),
               ALU.mult, r=[ps2.t, T1.t], w=[UG.t])
        GBr3 = GB.r.rearrange("p (c t) -> p c t", c=2)
        sD, sD3 = panel(1184, 256)
        for c in range(2):
            ps = s1(sD, sD3, c * 128, 128)
            cp("act", GBr3[:, c, :], ps.f, r=[ps.t], w=[GB.t])
        gct = [T2, T3]
        sD, sD3 = panel(1440, 256)
        for c in range(2):
            ps = s1(sD, sD3, c * 128, 128)
            cp("act", gct[c].f, ps.f, r=[ps.t], w=[gct[c].t])
        sE, sE3 = panel(1696, 256)
        for c in range(2):
            ps = s1(sE, sE3, c * 128, 128)
            tt("dve", VPv[:, c, :, 1:1 + L], ps.f.rearrange("p (s t) -> p s t", s=nseq), gct[c].f.rearrange("p (s t) -> p s t", s=nseq),
               ALU.mult, r=[ps.t, gct[c].t], w=[VP.t])
        sE, sE3 = panel(1952, 256)
        for c in range(2):
            ps = s1(sE, sE3, c * 128, 128)
            cp("act", FN3[:, c, :], ps.f, r=[ps.t], w=[FN.t])

        def rope(dst_f, dst_r, dst_t):
            ps = bank("a")
            mm(ps.f[0:96, :], [(R96_r[64:96, 0:96], dst_r)], r=[CST.t, dst_t], w=[ps.t])
            tt("dve", T1.f[64:96, :], dst_f, COS_f[64:96, :], ALU.mult, r=[dst_t, CSTF.t], w=[T1.t])
            tt("dve", T2.f[64:96, :], ps.f[64:96, :], SIN_f[64:96, :], ALU.mult, r=[ps.t, CSTF.t], w=[T2.t])
            tt("dve", dst_r, T1.f[64:96, :], T2.f[64:96, :], ALU.add, r=[T1.t, T2.t], w=[dst_t])

        if g == 1:
            rope(KR96.f[64:96, :], KR96.r[64:96, :], KR96.t)

        if g == 0:
            dma("sp", "o_ckv%d" % l, [(nckvT[l].rearrange("(c p) t -> p c t", p=128), CKV.f.rearrange("p (c t) -> p c t", c=2)),
                                        (nkrT[l], KR96.f[64:96, :])], r=[CKV.t, KR96.t], w=[T_OUT])
        else:
            if not os.environ.get("KD_NOSPILL"):
              dma("pool", "spill%d" % l, [(SPL[:, 0:1536], QA.f), (SPL[:, 1536:2680], UG.f), (SPL[:, 2680:3712], VP.f),
                                        (SPL[:, 3712:4736], GB.f)], r=[QA.t, UG.t, VP.t, GB.t, T_SPL], w=[T_SPL])

            dma("pool", "exkv%d" % l, [
                (EXKV[0:256].rearrange("(c p) t -> p c t", p=128), CKV.f.rearrange("p (c t) -> p c t", c=2)),
                (EXKV[256:288], KR96.f[64:96, :])], r=[CKV.t, KR96.t, T_EXKV], w=[T_EXKV])
            P.op("pool", lambda e: e.collective_compute("AllGather", ALU.bypass, replica_groups=RG,
                                                        ins=[EXKV.opt()], outs=[GAKV.opt()]),
                 r=[T_EXKV, T_GAKV], w=[T_GAKV], kind="cc", key="cckv%d" % l)
            dma("pool", "exfn%d" % l, [
                (EXFN.rearrange("(c p) t -> p c t", p=128), FN.f.rearrange("p (c t) -> p c t", c=2))], r=[FN.t, T_EXFN], w=[T_EXFN])
            P.op("pool", lambda e: e.collective_compute("AllGather", ALU.bypass, replica_groups=RG,
                                                        ins=[EXFN.opt()], outs=[GAFN.opt()]),
                 r=[T_EXFN, T_GAFN], w=[T_GAFN], kind="cc", key="ccfn%d" % l)
            dma("pool", "exe%d" % l, [
                (EXE[0:256, 0:16].rearrange("(c p) t -> p c t", p=128), UGf[:, :, 0, 15:31]),
                (EXE[0:256, 16:32].rearrange("(c p) t -> p c t", p=128), UGf[:, :, 0, L - 1:L + 15]),
                (EXE[256:512, 0:16].rearrange("(c p) t -> p c t", p=128), VPf[:, :, 0, 1:17]),
                (EXE[256:512, 16:32].rearrange("(c p) t -> p c t", p=128), VPf[:, :, 0, L - 15:L + 1])],
                r=[UG.t, VP.t, T_EXE], w=[T_EXE])
            P.op("pool", lambda e: e.collective_compute("AllGather", ALU.bypass, replica_groups=RG,
                                                        ins=[EXE.opt()], outs=[GAE[512:5 * 512].opt()]),
                 r=[T_EXE, T_GAE], w=[T_GAE], kind="cc", key="cce%d" % l)
            HL = HAL.f[:, 0:64].rearrange("p (c t) -> p c t", c=4)
            HR = HAL.f[:, 64:128].rearrange("p (c t) -> p c t", c=4)

            def halo(e):
                pid = e.partition_id()
                jb = (pid % 4) * 512
                return [e.dma_start(out=HL, in_=GAE[bass.DynSlice(jb, 512), 16:32].rearrange("(c p) t -> p c t", p=128)),
                        e.dma_start(out=HR, in_=GAE[bass.DynSlice(jb + 1024, 512), 0:16].rearrange("(c p) t -> p c t", p=128))]
            P.op("pool", halo, r=[T_GAE], w=[HAL.t], kind="d", nd=2, key="halo%d" % l)
        if part == 1:
            return
        P.mute = False
        if part == 2 and g == 1:
            dma("pool", "reload%d" % l, [(QA.r, SPL[:, 0:1536].bitcast(F32R)),
                                         (UGv[:, :, 0, 15:15 + L], SPL[:, 1536:2560].bitcast(F32R).rearrange("p (c t) -> p c t", c=2)),
                                         (VPv[:, :, 0, 1:1 + L], SPL[:, 2560:3584].bitcast(F32R).rearrange("p (c t) -> p c t", c=2)),
                                         (GB.r, SPL[:, 3584:4608].bitcast(F32R))],
                r=[T_SPL], w=[QA.t, UG.t, VP.t, GB.t])

        if g == 1:
            HL = HAL.f[:, 0:64].rearrange("p (c t) -> p c t", c=4)
            HR = HAL.f[:, 64:128].rearrange("p (c t) -> p c t", c=4)
            for ch in range(2):
                cp("pool", UGv[:, ch, 0, 0:15], HL[:, ch, 1:16], r=[HAL.t], w=[UG.t])
                cp("pool", UGv[:, ch, 0, 15 + L:30 + L], HR[:, ch, 0:15], r=[HAL.t], w=[UG.t])
                cp("pool", VPv[:, ch, 0, 0:1], HL[:, 2 + ch, 15:16], r=[HAL.t], w=[VP.t])
                cp("pool", VPv[:, ch, 0, 1 + L:2 + L], HR[:, 2 + ch, 0:1], r=[HAL.t], w=[VP.t])

        for c in range(2):
            T1v = T1.f.rearrange("p (s t) -> p s t", s=nseq)
            ts("dve", T1v, VPf[:, c, :, 0:L], sp(spb + SP_WSC + 0 * 2 + c), None, ALU.mult, None, r=[VP.t, SPR.t], w=[T1.t])
            for k in (1, 2):
                stt("dve", T1v, VPf[:, c, :, k:k + L], sp(spb + SP_WSC + k * 2 + c), T1v, ALU.mult, ALU.add,
                    r=[VP.t, SPR.t, T1.t], w=[T1.t])
            tt("dve", GB.r.rearrange("p (c t) -> p c t", c=2)[:, c, :], GB3[:, c, :], T1.f, ALU.mult, r=[GB.t, T1.t], w=[GB.t])

        U23 = U2.r.rearrange("p (c t) -> p c t", c=2)
        cacc = [T2, T3]
        for c in range(2):
            pc = bank("c")
            for (k0, k1) in ((0, 16), (16, 31)):
                s = RING[ring_i[0] % len(RING)]
                ring_i[0] += 1
                blks = []
                for k in range(k0, k1):
                    bv = V("r", s.off + (k - k0) * 128, 128, "diag")
                    if k % 2:
                        act(bv.r, IDENT.f, AF.Identity, r=[IDENT.t, SPR.t], w=[bv.t], scale=sp(spb + SP_WCDW + k * 2 + c))
                    else:
                        ts("dve", bv.r, IDENT.f, sp(spb + SP_WCDW + k * 2 + c), None, ALU.mult, None, r=[IDENT.t, SPR.t], w=[bv.t])
                    blks.append(bv)

                def fn(e, k0=k0, k1=k1, c=c, pc=pc, blks=blks):
                    ins = None
                    for k in range(k0, k1):
                        ins = e.matmul(pc.f, lhsT=blks[k - k0].r, rhs=UGv[:, c, :, k:k + L], start=(k == 0), stop=(k == 30))
                    return ins
                P.op("pe", fn, r=[s.t, UG.t], w=[pc.t])
            act(cacc[c].f, pc.f, AF.Identity, r=[pc.t, SPR.t], w=[cacc[c].t], bias=sp(spb + SP_BCDW + c))
        ps = bank("a")
        for c in range(2):
            cp("act", U23[:, c, :], cacc[c].f, r=[cacc[c].t], w=[U2.t])
        mm(ps.f, [(ONES_r, U23[:, c, :]) for c in range(2)], r=[CST.t, U2.t], w=[ps.t])
        for c in range(2):
            stt("dve", cacc[c].f, ps.f, -1.0 / 256, cacc[c].f, ALU.mult, ALU.add, r=[ps.t, cacc[c].t], w=[cacc[c].t])
        ps = bank("a")
        for c in range(2):
            act(U23[:, c, :], cacc[c].f, AF.Square, r=[cacc[c].t], w=[U2.t])
        mm(ps.f, [(ONES_r, U23[:, c, :]) for c in range(2)], r=[CST.t, U2.t], w=[ps.t])
        rstd_from_ps(ps, 256)
        for c in range(2):
            tt("dve", cacc[c].f, cacc[c].f, RS.f, ALU.mult, r=[cacc[c].t, RS.t], w=[cacc[c].t])
            act(U23[:, c, :], cacc[c].f, AF.Silu, r=[cacc[c].t, SPR.t], w=[U2.t], bias=sp(spb + SP_BCLN + c), scale=sp(spb + SP_GCLN + c))

        WQ = wload(lambda s: [(s.r[:, 0:2304].rearrange("p (k n) -> p k n", k=3), w_qb[l].bitcast(F32R).rearrange("(k p) n -> p k n", p=128))])
        WKV = wload(lambda s: [(s.r[:, 0:2048].rearrange("p (k n) -> p k n", k=2), w_kvb[l].bitcast(F32R).rearrange("(k p) n -> p k n", p=128))])
        WQ3 = WQ.r[:, 0:2304].rearrange("p (k n) -> p k n", k=3)
        WKV4 = WKV.r[:, 0:2048].rearrange("p (k h n) -> p k h n", k=2, h=8)

        QH3r = QH.r.rearrange("p (h t) -> p h t", h=8)
        QH3f = QH.f.rearrange("p (h t) -> p h t", h=8)
        for h in range(8):
            ps = bank("a")
            mm(ps.f[0:96, :], [(WQ3[:, kc, h * 96:(h + 1) * 96], QA3[:, kc, :]) for kc in range(3)], r=[WQ.t, QA.t], w=[ps.t])
            cp("act", QH3r[0:96, h, :], ps.f[0:96, :], r=[ps.t], w=[QHh[h].t])

        if g == 1:
            for h in range(8):
                rope(QH3f[64:96, h, :], QH3r[64:96, h, :], QHh[h].t)
        KH3 = KH.r.rearrange("p (h t) -> p h t", h=8)
        VB4 = VB.r.rearrange("p (c h d) -> p c h d", c=4, h=8)
        VB4f = VB.f.rearrange("p (c h d) -> p c h d", c=4, h=8)
        CKVB3 = CKVB.r.rearrange("p (c t) -> p c t", c=2)
        OH3f = OH.f.rearrange("p (h t) -> p h t", h=8)
        OH3r = OH.r.rearrange("p (h t) -> p h t", h=8)
        for ck_ in range(4):
            cp("dve", VB4[:, ck_, :, 64], CST.f[:, 0:8], r=[CST.t], w=[VBc[ck_].t])
        scale = 96.0 ** -0.5
        et_i = [0]

        def attend_block(ckv_r, ckv_t, kr_r, kr_t, nk, q0, nq, first, half=None, phase="both"):
            kc0 = 0 if half is None else half * 256
            vc0 = 0 if half is None else half * 2

            def kt(h):
                return [KHq[h][0].t, KHq[h][1].t] if half is None else [KHq[h][half].t]
            nck = nk // 128
            if phase in ("both", "prep"):
              for h in range(8):
                ps = bank("a")
                mm(ps.f[0:64, 0:nk], [(WKV4[:, kc, h, 0:64], ckv_r(kc)) for kc in range(2)], r=[WKV.t, ckv_t], w=[ps.t])
                cp("act" if h % 2 else "dve", KH3[0:64, h, kc0:kc0 + nk], ps.f[0:64, 0:nk], r=[ps.t], w=kt(h))
              for h in range(8):
                cp("act" if h % 2 else "dve", KH3[64:96, h, kc0:kc0 + nk], kr_r, r=[kr_t], w=kt(h))
              for ck in range(nck):
                ps = bank("a")
                mm(ps.f, [(ckv_r(kc)[:, ck * 128:(ck + 1) * 128], WKV4[:, kc, :, 64:128]) for kc in range(2)], r=[WKV.t, ckv_t], w=[ps.t])
                cp("act" if ck % 2 else "dve", VB4[:, vc0 + ck, :, 0:64], ps.f.rearrange("p (h d) -> p h d", h=8), r=[ps.t], w=[VBc[vc0 + ck].t])
            if phase == "prep":
                return
            items = [(h, ck) for h in range(8) for ck in range(nck)]
            LOOK = 2
            pos = {}
            ets = {}
            for i in range(len(items) + LOOK):
                if i < len(items):
                    h, ck = items[i]
                    pss = bank("b")
                    mm(pss.f[:, 0:nq], [(KH3[0:96, h, kc0 + ck * 128:kc0 + (ck + 1) * 128], QH3r[0:96, h, q0:q0 + nq])], r=kt(h) + [QHh[h].t], w=[pss.t])
                    et = ET[et_i[0] % 3]
                    et_i[0] += 1
                    act(et.r[:, 0:nq], pss.f[:, 0:nq], AF.Exp, r=[pss.t], w=[et.t], scale=scale)
                    ets[i] = et
                j = i - LOOK
                if j >= 0:
                    h, ck = items[j]
                    if ck == 0:
                        pos[h] = bank("c")
                    po, et = pos[h], ets.pop(j)
                    P.op("pe", lambda e, po=po, ck=ck, h=h, et=et: e.matmul(po.f[0:65, 0:nq], lhsT=VB4[:, vc0 + ck, h, :], rhs=et.r[:, 0:nq],
                                                                            start=(ck == 0), stop=(ck == nck - 1)),
                         r=[VBc[vc0 + ck].t, et.t], w=[po.t])
                    if ck == nck - 1:
                        if first:
                            cp("dve", OH3r[0:65, h, q0:q0 + nq], po.f[0:65, 0:nq], r=[po.t], w=[OHh[h].t])
                        else:
                            tt("dve", OH3r[0:65, h, q0:q0 + nq], OH3f[0:65, h, q0:q0 + nq], po.f[0:65, 0:nq], ALU.add, r=[po.t, OHh[h].t], w=[OHh[h].t])

        if g == 0:
            for ph in ("prep", "loop"):
                for s_ in range(2):
                    q0 = s_ * 256
                    attend_block(lambda kc, q0=q0: CKV3[:, kc, q0:q0 + 256], CKV.t, KR96.r[64:96, q0:q0 + 256], KR96.t, 256, q0, 256, True,
                                 half=s_, phase=ph)
        else:
            dma("pool", "kvb", [(CKVB3[:, :, 0:256], cckvT[l].bitcast(F32R).rearrange("(c p) t -> p c t", p=128)),
                                (KRB.r[64:96, 0:256], ckrT[l].bitcast(F32R))], w=[CKVB.t, KRB.t])
            attend_block(lambda kc: CKVB3[:, kc, 0:256], CKVB.t, KRB.r[64:96, 0:256], KRB.t, 256, 0, 512, True)
            for rk in range(4):
                r0 = rk * 288
                dma("pool", "kvb", [(CKVB3, GAKV[r0:r0 + 256].bitcast(F32R).rearrange("(c p) t -> p c t", p=128)),
                                    (KRB.r[64:96, :], GAKV[r0 + 256:r0 + 288].bitcast(F32R))], r=[T_GAKV], w=[CKVB.t, KRB.t])
                attend_block(lambda kc: CKVB3[:, kc, :], CKVB.t, KRB.r[64:96, :], KRB.t, 512, 0, 512, False)
        for h in range(8):
            ps = bank("a")
            P.op("pe", lambda e, ps=ps, h=h: e.matmul(ps.f[0:64, :], lhsT=SEL_r[0:65, :], rhs=OH3r[0:65, h, :], start=True, stop=True),
                 r=[CST.t, OHh[h].t], w=[ps.t])
            tq = T1 if h % 2 == 0 else T2
            act(tq.f[0:64, :], ps.f[0:64, :], AF.Ln, r=[ps.t], w=[tq.t])
            act(tq.f[0:64, :], tq.f[0:64, :], AF.Exp, r=[tq.t], w=[tq.t], scale=-1.0)
            tt("dve", OH3r[0:64, h, :], OH3f[0:64, h, :], tq.f[0:64, :], ALU.mult, r=[OHh[h].t, tq.t], w=[OHh[h].t])

        F23 = F2.r.rearrange("p (c t) -> p c t", c=2)
        if g == 0:
            ABp = AB.r[:, 0:2048].rearrange("p (s l j n) -> p s l j n", s=2, l=2, j=2)
            for s_ in range(2):
                for lc in range(2):
                    for jc in range(2):
                        ps = bank("a")
                        t0 = s_ * 256 + lc * 128
                        mm(ps.f[:, 0:256], [(FN3[:, jc, t0:t0 + 128], BD_r)], r=[FN.t, CST.t], w=[ps.t])
                        cp("act", ABp[:, s_, lc, jc, :], ps.f[:, 0:256], r=[ps.t], w=[ABt[s_ * 2 + lc].t])
            adanorm(l, g, 0, H)
            for s_ in range(2):
                for jc in range(2):
                    ps = bank("a")
                    pairs = []
                    for lc in range(2):
                        pairs.append((ABp[:, s_, lc, jc, 0:128], DFP[0][:, lc, :]))
                        pairs.append((ABp[:, s_, lc, jc, 128:256], DFP[1][:, lc, :]))
                    mm(ps.f[:, 0:256], pairs, r=[AB.t, DFPT.t], w=[ps.t])
                    cp("act", F23[:, jc, s_ * 256:(s_ + 1) * 256], ps.f[:, 0:256], r=[ps.t], w=[F2.t])
        else:
            FNF3 = FNFULL.r.rearrange("p (c t) -> p c t", c=2)
            dma("pool", "fnf", [(FNF3[:, :, rk * 512:(rk + 1) * 512],
                                 GAFN[rk * 256:(rk + 1) * 256].bitcast(F32R).rearrange("(c p) t -> p c t", p=128))
                                for rk in range(4)], r=[T_GAFN], w=[FNFULL.t])
            ABs = AB.r.rearrange("p (l j n) -> p l j n", l=16, j=2)
            for lc in range(16):
                for jc in range(2):
                    ps = bank("a")
                    mm(ps.f[:, 0:256], [(FNF3[:, jc, lc * 128:(lc + 1) * 128], BD_r)], r=[FNFULL.t, CST.t], w=[ps.t])
                    cp("act" if (lc + jc) % 2 else "dve", ABs[:, lc, jc, :], ps.f[:, 0:256], r=[ps.t], w=[ABt[lc].t])
            adanorm(l, g, 0, H)
            pacc = [bank("c"), bank("c")]
            for pnl in range(4):
                sl = []
                for cs in range(2):
                    s = wload(lambda s, cs=cs, pnl=pnl: [(s.r[:, 0:2048].rearrange("p (l n) -> p l n", l=4),
                                                          dftS[cs, pnl * 512:(pnl + 1) * 512, :].bitcast(F32R).rearrange("(l p) n -> p l n", p=128))])
                    sl.append(s)
                for jc in range(2):
                    def fn(e, jc=jc, pnl=pnl, sl=sl):
                        ins = None
                        for li in range(4):
                            lc = pnl * 4 + li
                            for cs in range(2):
                                ins = e.matmul(pacc[jc].f, lhsT=ABs[:, lc, jc, cs * 128:(cs + 1) * 128],
                                               rhs=sl[cs].r[:, 0:2048].rearrange("p (l n) -> p l n", l=4)[:, li, :],
                                               start=(lc == 0 and cs == 0), stop=(lc == 15 and cs == 1))
                        return ins
                    P.op("pe", fn, r=[AB.t, sl[0].t, sl[1].t], w=[pacc[jc].t])
            for jc in range(2):
                cp("act", F23[:, jc, :], pacc[jc].f, r=[pacc[jc].t], w=[F2.t])

        MG3 = MERGED.r.rearrange("p (c t) -> p c t", c=8)
        for dm in range(8):
            c0 = OFF_GATE + dm * 128
            sg = wload(lambda s, dm=dm: [(s.r[:, 0:3072], wmerge[l, dm, :, 0:3072].bitcast(F32R))])
            so = wload(lambda s, dm=dm: [(s.r[:, 0:2816], wmerge[l, dm, :, 3072:5888].bitcast(F32R))])
            first = True
            for b in range(4):
                src = sg if b < 3 else so
                boff = (b if b < 3 else 0) * 1024
                g3 = src.r[:, boff:boff + 1024].rearrange("p (k n) -> p k n", k=8)
                psg = bank("a")
                mm(psg.f, [(g3[:, kc, :], H3[:, kc, :]) for kc in range(8)], r=[src.t, H.t], w=[psg.t])
                act(T2.f, psg.f, AF.Sigmoid, r=[psg.t], w=[T2.t])
                psy = bank("a")
                if b == 0:
                    wo = so.r[0:64, 1024:2048].rearrange("p (h n) -> p h n", h=8)
                    mm(psy.f, [(wo[:, h, :], OH3r[0:64, h, :]) for h in range(8)], r=[so.t, OH.t], w=[psy.t])
                else:
                    boffs = {1: 2048, 2: 2304, 3: 2560}[b]
                    w3 = so.r[:, boffs:boffs + 256].rearrange("p (k n) -> p k n", k=2)
                    srcv, srct = {1: (U23, U2.t), 2: (GBr3, GB.t), 3: (F23, F2.t)}[b]
                    mm(psy.f, [(w3[:, kc, :], srcv[:, kc, :]) for kc in range(2)], r=[so.t, srct], w=[psy.t])
                if first:
                    tt("dve", MG3[:, dm, :], psy.f, T2.f, ALU.mult, r=[psy.t, T2.t], w=[MERGED.t])
                    first = False
                else:
                    tt("dve", T1.f, psy.f, T2.f, ALU.mult, r=[psy.t, T2.t], w=[T1.t])
                    tt("dve", MG3[:, dm, :], MG3[:, dm, :], T1.f, ALU.add, r=[MERGED.t, T1.t], w=[MERGED.t])
        for pnl in range(3):
            c0 = pnl * 384
            nm = 3 if pnl < 2 else 2
            s = wload(lambda s, c0=c0, nm=nm: [(s.r[:, 0:8 * nm * 128].rearrange("p (k n) -> p k n", k=8),
                                                w_out[l, :, c0:c0 + nm * 128].bitcast(F32R).rearrange("(k p) n -> p k n", p=128))])
            s3 = s.r[:, 0:8 * nm * 128].rearrange("p (k n) -> p k n", k=8)
            for mi in range(nm):
                m = pnl * 3 + mi
                ps = bank("a")
                mm(ps.f, [(s3[:, kc, mi * 128:(mi + 1) * 128], MG3[:, kc, :]) for kc in range(8)], r=[s.t, MERGED.t], w=[ps.t])
                stt("dve", xg(m, g), ps.f, modv(l, 16 + m, lat), xg(m, g), ALU.mult, ALU.add, r=[ps.t, MOD[l].t, XT[m][g]], w=[XT[m][g]])

    DFPT = CST
    DFP = [CST.r[:, 544:1056].rearrange("p (l n) -> p l n", l=2), CST.r[:, 1056:1568].rearrange("p (l n) -> p l n", l=2)]

    def ffn(l):
        H23 = H2.r.rearrange("p (c t) -> p c t", c=8)
        HID3 = HID.r.rearrange("p (c t) -> p c t", c=11)
        norm_both(l, 1, lambda g, c: H23[:, c, g * 512:(g + 1) * 512], lambda g, c: H2T[c][g])
        for half in range(2):
            for jp in range(0, 11, 3):
                nj = min(3, 11 - jp)
                j0 = (half * 11 + jp) * 128
                sg = wload(lambda s, j0=j0, nj=nj: [(s.r[:, 0:8 * nj * 128].rearrange("p (k n) -> p k n", k=8),
                                                     w_ffn_gate[l, :, j0:j0 + nj * 128].bitcast(F32R).rearrange("(k p) n -> p k n", p=128))])
                su = wload(lambda s, j0=j0, nj=nj: [(s.r[:, 0:8 * nj * 128].rearrange("p (k n) -> p k n", k=8),
                                                     w_ffn_up[l, :, j0:j0 + nj * 128].bitcast(F32R).rearrange("(k p) n -> p k n", p=128))])
                sg3 = sg.r[:, 0:8 * nj * 128].rearrange("p (k n) -> p k n", k=8)
                su3 = su.r[:, 0:8 * nj * 128].rearrange("p (k n) -> p k n", k=8)
                for ji in range(nj):
                    for g in range(2):
                        tsl = slice(g * 512, (g + 1) * 512)
                        pg = bank("a")
                        mm(pg.f, [(sg3[:, kc, ji * 128:(ji + 1) * 128], H23[:, kc, tsl]) for kc in range(8)], r=[sg.t, H2.t], w=[pg.t])
                        tmp = T2 if g == 0 else T3
                        act(tmp.f, pg.f, AF.Silu, r=[pg.t], w=[tmp.t])
                        pu = bank("a")
                        mm(pu.f, [(su3[:, kc, ji * 128:(ji + 1) * 128], H23[:, kc, tsl]) for kc in range(8)], r=[su.t, H2.t], w=[pu.t])
                        tt("dve", HID3[:, jp + ji, tsl], pu.f, tmp.f, ALU.mult, r=[pu.t, tmp.t], w=[HID.t])
            for m in range(8):
                pss = [bank("c"), bank("c")]
                s = wload(lambda s, m=m, half=half: [(s.r[:, 0:1408], wdown[l, half, m].bitcast(F32R))])
                s3 = s.r[:, 0:1408].rearrange("p (j n) -> p j n", j=11)
                for g in range(2):
                    def fn(e, s3=s3, g=g, pss=pss):
                        ins = None
                        for j in range(11):
                            ins = e.matmul(pss[g].f, lhsT=s3[:, j, :], rhs=HID3[:, j, g * 512:(g + 1) * 512],
                                           start=(j == 0), stop=(j == 10))
                        return ins
                    P.op("pe", fn, r=[s.t, HID.t], w=[pss[g].t])
                for g in range(2):
                    stt("dve", xg(m, g), pss[g].f, modv(l, 40 + m, g), xg(m, g), ALU.mult, ALU.add, r=[pss[g].t, MOD[l].t, XT[m][g]], w=[XT[m][g]])

    ada_mm()
    H3G = [TSM.H.r.rearrange("p (c t) -> p c t", c=8), TSA.H.r.rearrange("p (c t) -> p c t", c=8)]
    norm_pre(lambda g, c: H3G[g][:, c, :], lambda g, c: (TSM, TSA)[g].H.t)
    ada_fin()
    for l in range(DEPTH):
        scl_prep(l)
        stage1_both(l, pre=(l == 0))
        mixer(l, 0, part=2)
        mixer(l, 1, part=2)
        ffn(l)
    for g in range(2):
        O3r = H2.r.rearrange("p (c t) -> p c t", c=8)
        tsl = slice(g * 512, (g + 1) * 512)
        ps = bank("a")
        for c in range(8):
            act(O3r[:, c, tsl], xg(c, g), AF.Square, r=[XT[c][g]], w=[H2.t])
        mm(ps.f, [(ONES_r, O3r[:, c, tsl]) for c in range(8)], r=[CST.t, H2.t], w=[ps.t])
        rstd_from_ps(ps, D)
        dst = (ypT if g == 0 else ysT).rearrange("(c p) t -> c p t", p=128)
        stg = [T1, T2, T3]
        for c in range(8):
            st_ = stg[c % 3]
            stt("dve", st_.f, xg(c, g), sp(SP_GFIN + c), RS.f, ALU.mult, ALU.mult, r=[XT[c][g], SPR.t, RS.t], w=[st_.t])
            dma("pool", "oy%d" % (c % 3), [(dst[c], st_.f)], r=[st_.t], w=[])

    P.emit(stack)
    stack.close()
    return nc, P


def _consts(j):
    cst = np.zeros((128, NCST), np.float32)
    cst[:, 0:128] = 1.0
    R = np.zeros((32, 32), np.float32)
    for blk in (0, 16):
        for i in range(8):
            R[blk + 8 + i, blk + i] = -1.0
            R[blk + i, blk + 8 + i] = 1.0
    cst[64:96, 128 + 64:128 + 96] = R
    cst[64, 224:288] = 1.0
    pos = np.arange(j * 512, (j + 1) * 512)
    row, col = (pos // 64).astype(np.float32), (pos % 64).astype(np.float32)
    inv = (10000.0 ** (-np.arange(0, 16, 2, dtype=np.float32) / 16)).astype(np.float32)
    ar, ac = row[None, :] * inv[:, None], col[None, :] * inv[:, None]
    cosT = np.concatenate([np.cos(ar), np.cos(ar), np.cos(ac), np.cos(ac)], 0)
    sinT = np.concatenate([np.sin(ar), np.sin(ar), np.sin(ac), np.sin(ac)], 0)
    cst[64:96, 288:800] = cosT
    cst[64:96, 800:1312] = sinT
    k = np.arange(64)
    a64 = 2 * np.pi * np.outer(k, k) / 64
    C64, S64 = np.cos(a64) / 8.0, np.sin(a64) / 8.0
    bd = np.zeros((128, 256))
    for b in range(2):
        bd[b * 64:(b + 1) * 64, b * 64:(b + 1) * 64] = C64
        bd[b * 64:(b + 1) * 64, 128 + b * 64:128 + (b + 1) * 64] = S64
    cst[:, 1312:1568] = bd
    n = np.arange(256)
    a = 2 * np.pi * np.outer(n, n) / 256
    CL, NSL = np.cos(a) / 16.0, -np.sin(a) / 16.0
    cst[:, C_CL:C_CL + 512] = CL.reshape(2, 128, 256).transpose(1, 0, 2).reshape(128, 512)
    cst[:, C_NSL:C_NSL + 512] = NSL.reshape(2, 128, 256).transpose(1, 0, 2).reshape(128, 512)
    cst[:, C_EPS] = EPS
    cst[:, C_ID:C_ID + 128] = np.eye(128, dtype=np.float32)
    return cst


def _dft_s(j):
    n = np.arange(2048, dtype=np.int64)
    m = np.arange(j * 512, (j + 1) * 512, dtype=np.int64)
    a = 2 * np.pi * ((np.outer(n, m) % 2048).astype(np.float64)) / 2048
    s = 1.0 / math.sqrt(2048.0)
    return np.stack([np.cos(a) * s, -np.sin(a) * s]).astype(np.float32)


_CACHE = {}


def kernel(**inputs):
    f = lambda k: np.ascontiguousarray(np.asarray(inputs[k], dtype=np.float32))
    if "nc" not in _CACHE:
        _CACHE["nc"] = build_program()[0]
    nc = _CACHE["nc"]
    xp, xs = f("x_prompt"), f("x_sample")
    cckv, ckr, cc, cctx = f("cache_ckv"), f("cache_krope"), f("c"), f("c_ctx")
    wnames = ["w_in", "w_qb", "w_kvb", "w_out", "w_ffn_gate", "w_ffn_up"]
    W = {k: f(k) for k in wnames}
    win, wo, wpw, wsc, wfn, wdn = f("w_in"), f("w_o_mla"), f("w_conf_pw"), f("w_sc_out"), f("w_fn"), f("w_ffn_down")
    wm = np.zeros((DEPTH, 8, 128, 5888), np.float32)
    for dm in range(8):
        d0 = dm * 128
        for b in range(4):
            blk = win[:, :, OFF_GATE + b * D + d0:OFF_GATE + b * D + d0 + 128].reshape(DEPTH, 8, 128, 128).transpose(0, 2, 1, 3).reshape(DEPTH, 128, 1024)
            off = b * 1024 if b < 3 else 3072
            wm[:, dm, :, off:off + 1024] = blk
        wm[:, dm, 0:64, 4096:5120] = wo[:, :, d0:d0 + 128].reshape(DEPTH, 8, 64, 128).transpose(0, 2, 1, 3).reshape(DEPTH, 64, 1024)
        for i, wsrc in enumerate((wpw, wsc, wfn)):
            wm[:, dm, :, 5120 + i * 256:5120 + (i + 1) * 256] = wsrc[:, :, d0:d0 + 128].reshape(DEPTH, 2, 128, 128).transpose(0, 2, 1, 3).reshape(DEPTH, 128, 256)
    W["wmerge"] = wm
    W["wdown"] = np.ascontiguousarray(wdn.reshape(DEPTH, 2, 11, 128, 8, 128).transpose(0, 1, 4, 3, 2, 5).reshape(DEPTH, 2, 8, 128, 1408))
    wada = f("w_ada")

    def colsT(v):
        return v.reshape(-1, 128).T

    in_maps = []
    for c in range(8):
        b, j = c // 4, c % 4
        spm = np.zeros((128, NSP), np.float32)
        for l in range(DEPTH):
            o = l * NSP_L
            spm[:, o + SP_GN1:o + SP_GN1 + 8] = colsT(f("g_norm1")[l])
            spm[:, o + SP_GN2:o + SP_GN2 + 8] = colsT(f("g_norm2")[l])
            spm[:, o + SP_GQA:o + SP_GQA + 3] = colsT(f("g_qa")[l])
            spm[:, o + SP_GKVA:o + SP_GKVA + 2] = colsT(f("g_kva")[l])
            spm[:, o + SP_BCDW:o + SP_BCDW + 2] = colsT(f("b_conf_dw")[l])
            spm[:, o + SP_GCLN:o + SP_GCLN + 2] = colsT(f("g_conf_ln")[l])
            spm[:, o + SP_BCLN:o + SP_BCLN + 2] = colsT(f("b_conf_ln")[l])
            spm[:, o + SP_WSC:o + SP_WSC + 6] = f("w_sc_conv")[l].reshape(3, 2, 128).transpose(2, 0, 1).reshape(128, 6)
            spm[:, o + SP_WCDW:o + SP_WCDW + 62] = f("w_conf_dw")[l].reshape(31, 2, 128).transpose(2, 0, 1).reshape(128, 62)
            spm[:, o + SP_BADA:o + SP_BADA + 12] = colsT(f("b_ada")[l][j * 1536:(j + 1) * 1536])
        spm[:, SP_GFIN:SP_GFIN + 8] = colsT(f("g_final"))
        spm[:, SP_CCTX:SP_CCTX + 8] = colsT(cctx)
        spm[:, SP_CLAT:SP_CLAT + 8] = colsT(cc[0])
        spm[:, SP_CLAT + 8:SP_CLAT + 16] = colsT(cc[1])
        m = {
            "xpT": np.ascontiguousarray(xp[2 * c:2 * c + 2].reshape(512, D).T),
            "xsT": np.ascontiguousarray(xs[b, j * 512:(j + 1) * 512].T),
            "cckvT": np.ascontiguousarray(cckv[b].transpose(0, 2, 1)),
            "ckrT": np.ascontiguousarray(ckr[b].transpose(0, 2, 1)),
            "smallp": spm,
            "cst": _consts(j),
            "dftS": _dft_s(j),
            "w_ada_s": np.ascontiguousarray(wada[:, :, j * 1536:(j + 1) * 1536]),
        }
        m.update(W)
        in_maps.append(m)
    res = run_bass_kernel_spmd(nc, in_maps, core_ids=list(range(8))).results
    y_prompt = np.zeros((16, 256, D), np.float32)
    y_sample = np.zeros((2, 2048, D), np.float32)
    new_ckv = np.zeros((16, DEPTH, 256, 256), np.float32)
    new_kr = np.zeros((16, DEPTH, 256, 32), np.float32)
    for c in range(8):
        b, j = c // 4, c % 4
        r = res[c]
        y_prompt[2 * c:2 * c + 2] = r["ypT"].T.reshape(2, 256, D)
        y_sample[b, j * 512:(j + 1) * 512] = r["ysT"].T
        new_ckv[2 * c:2 * c + 2] = r["nckvT"].transpose(2, 0, 1).reshape(2, 256, DEPTH, 256).transpose(0, 2, 1, 3)
        new_kr[2 * c:2 * c + 2] = r["nkrT"].transpose(2, 0, 1).reshape(2, 256, DEPTH, 32).transpose(0, 2, 1, 3)
    return (y_prompt, y_sample, new_ckv, new_kr)
```

```python
import math
import os
from contextlib import ExitStack
import numpy as np
import concourse.bass as bass
import concourse.mybir as mybir
from concourse.bass_utils import run_bass_kernel_spmd

F32 = mybir.dt.float32
F32R = mybir.dt.float32r
AF = mybir.ActivationFunctionType
ALU = mybir.AluOpType

D = 1024
DEPTH = 2
NH = 8
FF = 2816
IN_COLS = 6304
OFF_GATE = 2208
EPS = 1e-6
EXROWS = 1056
EX_CKV, EX_KR, EX_FN, EX_UG, EX_VP = 0, 256, 288, 544, 800

SP_GN1, SP_GN2, SP_GQA, SP_GKVA, SP_BCDW, SP_GCLN, SP_BCLN, SP_WSC, SP_WCDW, SP_BADA = 0, 8, 16, 19, 21, 23, 25, 27, 33, 95
NSP_L = 143
SP_GFIN = 2 * NSP_L
SP_CCTX = SP_GFIN + 8
SP_CLAT = SP_CCTX + 8
NSP = SP_CLAT + 16
NCST = 2816
C_ID = 2688
C_ONES, C_R96, C_SEL, C_COS, C_SIN, C_BD, C_CL, C_NSL, C_EPS, C_ZERO = 0, 128, 224, 288, 800, 1312, 1568, 2080, 2592, 2624

SAME_ENGINE_SYNC = True


class Tile:
    def __init__(self, space, start, end, name):
        self.space, self.start, self.end, self.name = space, start, end, name
        self.last_w = None
        self.readers = []


class Prog:
    ENG = ("pe", "act", "dve", "pool", "sp")

    def __init__(self, nc):
        self.nc = nc
        self.ops = []
        self.tiles = {}
        self.keys = {}
        self.mute = False

    def tile(self, space, start, end, name):
        t = Tile(space, start, end, name)
        self.tiles.setdefault(space, []).append(t)
        return t

    def _overl(self, t):
        return [u for u in self.tiles[t.space] if u.start < t.end and t.start < u.end]

    def op(self, eng, fn, r=(), w=(), kind="c", nd=1, key=None):
        if self.mute:
            return -1
        i = len(self.ops)
        deps = set()
        for t in r:
            for u in self._overl(t):
                if u.last_w is not None:
                    deps.add(u.last_w)
        for t in w:
            for u in self._overl(t):
                if u.last_w is not None:
                    deps.add(u.last_w)
                deps.update(u.readers)
        deps.discard(i)
        self.ops.append(dict(eng=eng, fn=fn, deps=deps, kind=kind, nd=nd, key=key, inc=False))
        for t in r:
            t.readers.append(i)
        for t in w:
            t.last_w = i
            t.readers = []
        return i

    def emit(self, stack):
        nc = self.nc
        ops = self.ops
        for o in ops:
            for d in o["deps"]:
                src = ops[d]
                if src["kind"] == "c" and src["eng"] == o["eng"] and (o["eng"] == "pe" or not SAME_ENGINE_SYNC):
                    continue
                src["inc"] = True
        esem = {e: stack.enter_context(nc.semaphore("es_" + e)) for e in self.ENG}
        ksem = {}
        ecnt = {e: 0 for e in self.ENG}
        kcnt = {}
        for o in ops:
            if o["kind"] == "c":
                if o["inc"]:
                    ecnt[o["eng"]] += 1
                    o["tok"] = (esem[o["eng"]], ecnt[o["eng"]])
                else:
                    o["tok"] = None
            else:
                k = o["key"]
                if k not in ksem:
                    ksem[k] = stack.enter_context(nc.semaphore("ks_" + str(k)))
                    kcnt[k] = 0
                kcnt[k] += (16 * o["nd"]) if o["kind"] == "d" else 1
                o["tok"] = (ksem[k], kcnt[k])
        self.nsem = len(esem) + len(ksem)
        final = dict((id(ksem[k]), (ksem[k], kcnt[k])) for k in ksem)
        streams = {e: [o for o in ops if o["eng"] == e] for e in self.ENG}
        block = stack.enter_context(nc.Block())

        def run(eng_handle, ename):
            known = {}
            for o in streams[ename]:
                need = {}
                for d in o["deps"]:
                    src = ops[d]
                    if src["kind"] == "c" and src["eng"] == ename and (ename == "pe" or not SAME_ENGINE_SYNC):
                        continue
                    sem, cnt = src["tok"]
                    if need.get(id(sem), (None, 0))[1] < cnt:
                        need[id(sem)] = (sem, cnt)
                for sid, (sem, cnt) in need.items():
                    if known.get(sid, 0) < cnt:
                        eng_handle.wait_ge(sem, cnt)
                        known[sid] = cnt
                res = o["fn"](eng_handle)
                if o["kind"] == "c":
                    if o["inc"]:
                        res.then_inc(o["tok"][0], 1)
                elif o["kind"] == "d":
                    for ins in res:
                        ins.then_inc(o["tok"][0], 16)
                else:
                    res.then_inc(o["tok"][0])
            if ename == "pool":
                for sid, (sem, cnt) in final.items():
                    if known.get(sid, 0) < cnt:
                        eng_handle.wait_ge(sem, cnt)

        @block.sync
        def _(e):
            run(e, "sp")

        @block.tensor
        def _(e):
            run(e, "pe")

        @block.scalar
        def _(e):
            run(e, "act")

        @block.vector
        def _(e):
            run(e, "dve")

        @block.gpsimd
        def _(e):
            run(e, "pool")


def build_program(debug=False):
    nc = bass.Bass("TRN2", target_bir_lowering=False)
    nc.dge_precook = False
    P = Prog(nc)
    stack = ExitStack()

    def din(name, shape):
        return nc.dram_tensor(name, list(shape), F32, kind="ExternalInput").ap()

    def dout(name, shape):
        return nc.dram_tensor(name, list(shape), F32, kind="ExternalOutput").ap()

    xpT = din("xpT", [D, 512])
    xsT = din("xsT", [D, 512])
    cckvT = din("cckvT", [DEPTH, 256, 256])
    ckrT = din("ckrT", [DEPTH, 32, 256])
    smallp = din("smallp", [128, NSP])
    cst = din("cst", [128, NCST])
    dftS = din("dftS", [2, 2048, 512])
    w_ada_s = din("w_ada_s", [DEPTH, D, 1536])
    w_in = din("w_in", [DEPTH, D, IN_COLS])
    w_qb = din("w_qb", [DEPTH, 384, 768])
    w_kvb = din("w_kvb", [DEPTH, 256, 1024])
    w_out = din("w_out", [DEPTH, D, D])
    w_ffn_gate = din("w_ffn_gate", [DEPTH, D, FF])
    w_ffn_up = din("w_ffn_up", [DEPTH, D, FF])
    wmerge = din("wmerge", [DEPTH, 8, 128, 5888])
    wdown = din("wdown", [DEPTH, 2, 8, 128, 1408])

    ypT = dout("ypT", [D, 512])
    ysT = dout("ysT", [D, 512])
    nckvT = dout("nckvT", [DEPTH, 256, 512])
    nkrT = dout("nkrT", [DEPTH, 32, 512])

    RG = [[0, 1, 2, 3], [4, 5, 6, 7]]
    EXKV = nc.dram_tensor("exkv", [288, 512], F32).ap()
    GAKV = nc.dram_tensor("gakv", [4 * 288, 512], F32).ap()
    EXFN = nc.dram_tensor("exfn", [256, 512], F32).ap()
    GAFN = nc.dram_tensor("gafn", [4 * 256, 512], F32).ap()
    EXE = nc.dram_tensor("exe", [512, 32], F32).ap()
    GAE = nc.dram_tensor("gae", [6 * 512, 32], F32).ap()
    T_EXKV, T_GAKV = P.tile("d1", 0, 1, "EXKV"), P.tile("d2", 0, 1, "GAKV")
    T_EXFN, T_GAFN = P.tile("d3", 0, 1, "EXFN"), P.tile("d4", 0, 1, "GAFN")
    T_EXE, T_GAE = P.tile("d5", 0, 1, "EXE"), P.tile("d6", 0, 1, "GAE")
    SPL = nc.dram_tensor("spill", [128, 4736], F32).ap()
    T_SPL = P.tile("d9", 0, 1, "SPL")
    T_OUT = P.tile("dram_out", 0, 1, "OUT")

    NWR = 40640
    NWF = 12560
    SBR = stack.enter_context(nc.sbuf_tensor("SBR", [128, NWR], F32))
    SBF = stack.enter_context(nc.sbuf_tensor("SBF", [128, NWF], F32))
    PS = stack.enter_context(nc.psum_tensor("PS", [128, 8 * 512], F32))

    class V:
        def __init__(self, sp_, off, n, name):
            T_, lim = (SBR, NWR) if sp_ == "r" else (SBF, NWF)
            assert off + n <= lim, (name, off, n)
            self.off, self.n = off, n
            self.t = P.tile("sb" + sp_, off, off + n, name)
            self.f = T_[:, off:off + n]
            self.r = T_[:, off:off + n].bitcast(F32R)

    oo = {"r": 0, "f": 0}

    def take(n, name, sp_="r"):
        v = V(sp_, oo[sp_], n, name)
        oo[sp_] += n
        return v

    X = take(8 * 1024, "X", "f")
    CSTF = take(1088, "CSTF", "f")
    SPR = take(NSP, "SPR", "f")
    MOD = [take(96, "MOD%d" % l, "f") for l in range(DEPTH)]
    SCL = take(64, "SCL", "f")
    RS = take(512, "RS", "f")
    T1 = take(512, "T1", "f")
    T2 = take(512, "T2", "f")
    T3 = take(512, "T3", "f")
    IDENT = take(128, "IDENT", "f")
    HAL = take(128, "HAL", "f")

    RING = [take(3072, "ring%d" % i) for i in range(3)]
    CST = take(1568, "CST")
    SILC = take(32, "SILC")
    base = oo["r"]
    H = take(8 * 512, "H")
    QA = take(3 * 512, "QA")
    CKV = take(2 * 512, "CKV")
    KR96 = take(512, "KR96")
    UG = take(1144, "UG")
    VP = take(1032, "VP")
    GB = take(1024, "GB")
    FN = take(1024, "FN")
    F2 = take(1024, "F2")
    U2 = take(1024, "U2")
    OH = take(8 * 512, "OH")
    at0 = oo["r"]
    KH = take(8 * 512, "KH")
    VB = take(4 * 8 * 65, "VB")
    CKVB = take(1024, "CKVB")
    KRB = take(512, "KRB")
    ET = [take(512, "ET%d" % i) for i in range(3)]
    QH = V("r", H.off, 8 * 512, "QH")
    MERGED = V("r", at0, 8 * 512, "MERGED")
    FNFULL = V("r", at0, 4096, "FNFULL")
    AB = V("r", at0 + 4096, 8192, "AB")
    H2 = V("r", base, 8 * 1024, "H2")
    HID = V("r", base + 8 * 1024, 11 * 1024, "HID")
    OHh = [V("r", OH.off + h * 512, 512, "OHh%d" % h) for h in range(8)]
    ABt = [V("r", AB.off + i * 512, 512, "ABt%d" % i) for i in range(16)]
    KHq = [[V("r", KH.off + h * 512 + q * 256, 256, "KH%d_%d" % (h, q)) for q in range(2)] for h in range(8)]
    QHh = [V("r", QH.off + h * 512, 512, "QHh%d" % h) for h in range(8)]
    VBc = [V("r", VB.off + c * 520, 520, "VBc%d" % c) for c in range(4)]
    H2T = [[V("r", H2.off + c * 1024 + g * 512, 512, "H2_%d_%d" % (c, g)).t for g in range(2)] for c in range(8)]
    PF = [V("r", FN.off, 2048, "PF0"), V("r", QA.off, 2048, "PF1"), V("r", UG.off, 2048, "PF2")]
    SQ = V("r", F2.off, 2048, "SQ")

    class TSet:
        pass
    TSM, TSA = TSet(), TSet()
    TSM.H, TSM.QA, TSM.CKV, TSM.KR96, TSM.UG, TSM.VP, TSM.GB, TSM.FN = H, QA, CKV, KR96, UG, VP, GB, FN
    ao = OH.off
    for nm, n in (("H", 4096), ("QA", 1536), ("CKV", 1024), ("KR96", 512), ("UG", 1144), ("VP", 1032), ("GB", 1024), ("FN", 1024)):
        setattr(TSA, nm, V("r", ao, n, nm + "_S"))
        ao += n
    assert ao <= NWR

    class B:
        def __init__(self, i):
            self.t = P.tile("ps", i * 512, (i + 1) * 512, "bank%d" % i)
            self.f = PS[:, i * 512:(i + 1) * 512]

    BANKS = [B(i) for i in range(8)]
    rr = {"a": 0, "b": 0, "c": 0}
    pools = {"a": [0, 1, 2], "b": [3, 4, 5], "c": [6, 7]}

    def bank(pool="a"):
        lst = pools[pool]
        b = BANKS[lst[rr[pool] % len(lst)]]
        rr[pool] += 1
        return b

    def dma(queue, key, pairs, r=(), w=()):
        def fn(e, pairs=pairs):
            return [e.dma_start(out=a, in_=b) for a, b in pairs]
        return P.op(queue, fn, r=r, w=w, kind="d", nd=len(pairs), key=key)

    ring_i = [0]

    def wload(pairs_fn):
        s = RING[ring_i[0] % len(RING)]
        k = "ring%d" % (ring_i[0] % len(RING))
        ring_i[0] += 1
        dma("sp", k, pairs_fn(s), w=[s.t])
        return s

    def mm(out_ap, pairs, r, w, plain=False):
        n = len(pairs)

        def fn(e, pairs=pairs, out_ap=out_ap):
            ins = None
            for i, (l, rh) in enumerate(pairs):
                ins = e.matmul(out_ap, lhsT=l, rhs=rh, start=(i == 0), stop=(i == n - 1))
            return ins
        return P.op("pe", fn, r=r, w=w)

    def act(out, in_, func, r, w, bias=None, scale=None):
        kw = {}
        if bias is not None:
            kw["bias"] = bias
        if scale is not None:
            kw["scale"] = scale
        return P.op("act", lambda e: e.activation(out, in_, func, **kw), r=r, w=w)

    def tt(eng, out, a, b, op, r, w):
        return P.op(eng, lambda e: e.tensor_tensor(out, a, b, op), r=r, w=w)

    def ts(eng, out, a, s1, s2, op0, op1, r, w):
        if s2 is None:
            return P.op(eng, lambda e: e.tensor_scalar(out, a, s1, None, op0), r=r, w=w)
        return P.op(eng, lambda e: e.tensor_scalar(out, a, s1, s2, op0, op1), r=r, w=w)

    def stt(eng, out, a, s, b, op0, op1, r, w):
        return P.op(eng, lambda e: e.scalar_tensor_tensor(out, a, s, b, op0, op1), r=r, w=w)

    def cp(eng, out, in_, r, w):
        if eng == "act":
            return P.op("act", lambda e: e.copy(out, in_), r=r, w=w)
        return P.op(eng, lambda e: e.tensor_copy(out, in_), r=r, w=w)

    def sp(col, n=1):
        return SPR.f[:, col:col + n]

    ONES_r = CST.r[:, 0:128]
    R96_r = CST.r[:, 128:224]
    SEL_r = CST.r[:, 224:288]
    BD_r = CST.r[:, 288:544]
    COS_f = CSTF.f[:, 0:512]
    SIN_f = CSTF.f[:, 512:1024]
    EPS_f = CSTF.f[:, 1024:1025]
    ZERO_f = CSTF.f[:, 1056:1088]

    dma("pool", "c0", [(CST.r[:, 0:288], cst[:, 0:288].bitcast(F32R)), (CST.r[:, 288:1568], cst[:, C_BD:C_BD + 1280].bitcast(F32R)),
                       (CSTF.f[:, 0:1024], cst[:, C_COS:C_COS + 1024]), (CSTF.f[:, 1024:1088], cst[:, C_EPS:C_EPS + 64]),
                       (IDENT.f, cst[:, C_ID:C_ID + 128]), (SPR.f, smallp)], w=[CST.t, CSTF.t, SPR.t, IDENT.t])
    dma("pool", "x0", [(X.f.rearrange("p (c t) -> p c t", c=8)[:, :, 0:512], xpT.rearrange("(c p) t -> p c t", p=128)),
                       (X.f.rearrange("p (c t) -> p c t", c=8)[:, :, 512:1024], xsT.rearrange("(c p) t -> p c t", p=128))],
        w=[X.t])
    zp = []
    for ch in range(4):
        zp.append((GAE[ch * 128:(ch + 1) * 128, :], ZERO_f))
        zp.append((GAE[5 * 512 + ch * 128:5 * 512 + (ch + 1) * 128, :], ZERO_f))
    dma("pool", "zp", zp, r=[CSTF.t], w=[T_GAE])

    X3 = X.f.rearrange("p (c t) -> p c t", c=8)

    def xg(c, g):
        return X3[:, c, g * 512:(g + 1) * 512]

    XT = [[V("f", X.off + c * 1024 + g * 512, 512, "X%d_%d" % (c, g)).t for g in range(2)] for c in range(8)]

    EXM = nc.dram_tensor("exm", [384, 24], F32).ap()
    GAM = nc.dram_tensor("gam", [5 * 384, 24], F32).ap()
    T_EXM, T_GAM = P.tile("d7", 0, 1, "EXM"), P.tile("d8", 0, 1, "GAM")
    SILC3 = SILC.r.rearrange("p (c t) -> p c t", t=4)
    for v in range(4):
        act(SILC3[:, :, v], sp(SP_CCTX + 8 * (v % 3), 8), AF.Silu, r=[SPR.t, SILC.t], w=[SILC.t])

    ADAT = take(272, "ADAT", "f")

    def ada_mm():
        pb = bank("c")
        for l in range(DEPTH):
            for pnl in range(4):
                s = wload(lambda s, l=l, pnl=pnl: [(s.r.rearrange("p (k n) -> p k n", k=8),
                                                    w_ada_s[l, :, pnl * 384:(pnl + 1) * 384].bitcast(F32R).rearrange("(k p) n -> p k n", p=128))])
                s3 = s.r.rearrange("p (k n) -> p k n", k=8)
                for mi3 in range(3):
                    idx = l * 12 + pnl * 3 + mi3
                    mm(pb.f[:, idx * 4:(idx + 1) * 4], [(s3[:, kc, mi3 * 128:(mi3 + 1) * 128], SILC3[:, kc, :]) for kc in range(8)],
                       r=[s.t, SILC.t], w=[pb.t])
        MODP3 = ADAT.f[:, 0:72].rearrange("p (v c) -> p v c", v=3)
        pb3 = pb.f[:, 0:96].rearrange("p (c v) -> p c v", v=4)
        for v in range(3):
            for l in range(DEPTH):
                tt("dve", MODP3[:, v, l * 12:(l + 1) * 12], pb3[:, l * 12:(l + 1) * 12, v], sp(l * NSP_L + SP_BADA, 12), ALU.add,
                   r=[pb.t, SPR.t, ADAT.t], w=[ADAT.t])
        dma("pool", "exm", [(EXM.rearrange("(v p) c -> p v c", p=128), MODP3)], r=[ADAT.t], w=[T_EXM])
        P.op("pool", lambda e: e.collective_compute("AllGather", ALU.bypass, replica_groups=RG,
                                                    ins=[EXM.opt()], outs=[GAM[0:1536].opt()]),
             r=[T_EXM, T_GAM], w=[T_GAM], kind="cc", key="ccm")
        STG4 = ADAT.f[:, 72:264].rearrange("p (t r c) -> p t r c", t=2, r=4)

        def ld(e):
            pid = e.partition_id()
            vb = (pid // 4) * 128 + 128
            return [e.dma_start(out=STG4[:, 0], in_=GAM[0:1536].rearrange("(r x) c -> x r c", x=384)[0:128]),
                    e.dma_start(out=STG4[:, 1], in_=GAM[bass.DynSlice(vb, 1536), :].rearrange("(r x) c -> x r c", x=384)[0:128])]
        P.op("pool", ld, r=[T_GAM], w=[ADAT.t], kind="d", nd=2, key="ldm")

    def ada_fin():
        STG4 = ADAT.f[:, 72:264].rearrange("p (t r c) -> p t r c", t=2, r=4)
        for l in range(DEPTH):
            M4 = MOD[l].f.rearrange("p (r i t) -> p r i t", r=4, i=12)
            for t in range(2):
                cp("dve", M4[:, :, :, t], STG4[:, t, :, l * 12:(l + 1) * 12], r=[ADAT.t, MOD[l].t], w=[MOD[l].t])

    def modv(l, m, t):
        return MOD[l].f[:, 2 * m + t:2 * m + t + 1]

    def scl_prep(l):
        M3 = MOD[l].f.rearrange("p (m t) -> p m t", t=2)
        S3 = SCL.f[:, 0:32].rearrange("p (a c t) -> p a c t", a=2, t=2)
        for which, (scm, gcol) in enumerate(((8, SP_GN1), (32, SP_GN2))):
            for t in range(2):
                stt("dve", S3[:, which, :, t], M3[:, scm:scm + 8, t], 1.0, sp(l * NSP_L + gcol, 8), ALU.add, ALU.mult,
                    r=[MOD[l].t, SPR.t, SCL.t], w=[SCL.t])

    def sclv(which, c, t):
        k = which * 16 + c * 2 + t
        return SCL.f[:, k:k + 1]

    def rstd_from_ps(ps, n_feat, ntok=512, eps=EPS, tmp=None, rs=None):
        tmp = tmp or T1
        rs = rs or RS
        act(tmp.f[:, 0:ntok], ps.f[:, 0:ntok], AF.Ln, r=[ps.t, CSTF.t], w=[tmp.t], bias=EPS_f, scale=1.0 / n_feat)
        act(rs.f[:, 0:ntok], tmp.f[:, 0:ntok], AF.Exp, r=[tmp.t], w=[rs.t], scale=-0.5)

    def norm_pre(dview, dtile):
        pss = [bank("a"), bank("a")]
        rs = [RS, T3]
        tmp = [T1, T2]
        for c in range(8):
            act(dview(0, c), xg(c, 0), AF.Square, r=[XT[c][0]], w=[dtile(0, c)])
            tt("dve", dview(1, c), xg(c, 1), xg(c, 1), ALU.mult, r=[XT[c][1]], w=[dtile(1, c)])
        for g in range(2):
            mm(pss[g].f, [(ONES_r, dview(g, c)) for c in range(8)], r=[CST.t] + [dtile(g, c) for c in range(8)], w=[pss[g].t])
        for g in range(2):
            rstd_from_ps(pss[g], D, tmp=tmp[g], rs=rs[g])

    def norm_both(l, which, dview, dtile, pre=False):
        shm = 0 if which == 0 else 24
        rs = [RS, T3]
        tmp = [T1, T2]
        if not pre:
            norm_pre(dview, dtile)
        for c in range(8):
            for g in range(2):
                tt("dve", tmp[g].f, xg(c, g), rs[g].f, ALU.mult, r=[XT[c][g], rs[g].t], w=[tmp[g].t])
                act(dview(g, c), tmp[g].f, AF.Identity, r=[tmp[g].t, SCL.t, MOD[l].t], w=[dtile(g, c)],
                    bias=modv(l, shm + c, g), scale=sclv(which, c, g))

    def adanorm(l, g, which, dst):
        t = 0 if g == 0 else 1
        d3 = dst.r.rearrange("p (c t) -> p c t", c=8)
        ps = bank("a")
        for c in range(8):
            act(d3[:, c, :], xg(c, g), AF.Square, r=[XT[c][g]], w=[dst.t])
        mm(ps.f, [(ONES_r, d3[:, c, :]) for c in range(8)], r=[CST.t, dst.t], w=[ps.t])
        rstd_from_ps(ps, D)
        shm = 0 if which == 0 else 24
        for c in range(8):
            tt("dve", T1.f, xg(c, g), RS.f, ALU.mult, r=[XT[c][g], RS.t], w=[T1.t])
            act(d3[:, c, :], T1.f, AF.Identity, r=[T1.t, SCL.t, MOD[l].t], w=[dst.t], bias=modv(l, shm + c, t), scale=sclv(which, c, t))

    def stage1_both(l, pre=False):
        spb = l * NSP_L
        TS = [TSM, TSA]
        geo = [(2, 256), (1, 512)]
        H3 = [TS[g].H.r.rearrange("p (c t) -> p c t", c=8) for g in range(2)]
        norm_both(l, 0, lambda g, c: H3[g][:, c, :], lambda g, c: TS[g].H.t, pre=pre)
        QA3 = [TS[g].QA.r.rearrange("p (c t) -> p c t", c=3) for g in range(2)]
        CKV3 = [TS[g].CKV.r.rearrange("p (c t) -> p c t", c=2) for g in range(2)]
        CKV3f = [TS[g].CKV.f.rearrange("p (c t) -> p c t", c=2) for g in range(2)]
        GBr3 = [TS[g].GB.r.rearrange("p (c t) -> p c t", c=2) for g in range(2)]
        FN3 = [TS[g].FN.r.rearrange("p (c t) -> p c t", c=2) for g in range(2)]
        UGv = [TS[g].UG.r[:, 0:2 * geo[g][0] * (geo[g][1] + 30)].rearrange("p (c s t) -> p c s t", c=2, s=geo[g][0]) for g in range(2)]
        VPv = [TS[g].VP.r[:, 0:2 * geo[g][0] * (geo[g][1] + 2)].rearrange("p (c s t) -> p c s t", c=2, s=geo[g][0]) for g in range(2)]
        UGf = [TS[g].UG.f[:, 0:2 * geo[g][0] * (geo[g][1] + 30)].rearrange("p (c s t) -> p c s t", c=2, s=geo[g][0]) for g in range(2)]
        VPf = [TS[g].VP.f[:, 0:2 * geo[g][0] * (geo[g][1] + 2)].rearrange("p (c s t) -> p c s t", c=2, s=geo[g][0]) for g in range(2)]
        SQ3 = SQ.r.rearrange("p (c t) -> p c t", c=4)

        def panel(c0, ncol):
            s = wload(lambda s: [(s.r[:, 0:8 * ncol].rearrange("p (k n) -> p k n", k=8),
                                  w_in[l, :, c0:c0 + ncol].bitcast(F32R).rearrange("(k p) n -> p k n", p=128))])
            return s, s.r[:, 0:8 * ncol].rearrange("p (k n) -> p k n", k=8)

        def s1(s, s3, m0, M, g):
            ps = bank("a")
            mm(ps.f[0:M, :], [(s3[:, kc, m0:m0 + M], H3[g][:, kc, :]) for kc in range(8)], r=[s.t, TS[g].H.t], w=[ps.t])
            return ps

        def seg(ap, g):
            return ap.rearrange("p (s t) -> p s t", s=geo[g][0])

        for c in range(2):
            for s_ in range(2):
                cp("dve", UGv[0][:, c, s_, 0:15], ZERO_f[:, 0:15], r=[CSTF.t], w=[UG.t])
                cp("dve", UGv[0][:, c, s_, 15 + 256:30 + 256], ZERO_f[:, 0:15], r=[CSTF.t], w=[UG.t])
                cp("dve", VPv[0][:, c, s_, 0:1], ZERO_f[:, 0:1], r=[CSTF.t], w=[VP.t])
                cp("dve", VPv[0][:, c, s_, 1 + 256:2 + 256], ZERO_f[:, 0:1], r=[CSTF.t], w=[VP.t])
        s, s3 = panel(0, 384)
        for c in range(3):
            for g in range(2):
                ps = s1(s, s3, c * 128, 128, g)
                cp("act", QA3[g][:, c, :], ps.f, r=[ps.t], w=[TS[g].QA.t])
        s, s3 = panel(384, 288)
        for c in range(2):
            for g in range(2):
                ps = s1(s, s3, c * 128, 128, g)
                cp("act" if g else "dve", CKV3[g][:, c, :], ps.f, r=[ps.t], w=[TS[g].CKV.t])
        for g in range(2):
            ps = s1(s, s3, 192, 96, g)
            cp("dve", TS[g].KR96.r[0:96, :], ps.f[0:96, :], r=[ps.t], w=[TS[g].KR96.t])
        for g in range(2):
            ps = bank("a")
            for c in range(2):
                act(SQ3[:, c, :], CKV3f[g][:, c, :], AF.Square, r=[TS[g].CKV.t], w=[SQ.t])
            mm(ps.f, [(ONES_r, SQ3[:, c, :]) for c in range(2)], r=[CST.t, SQ.t], w=[ps.t])
            rstd_from_ps(ps, 256)
            for c in range(2):
                stt("dve", CKV3[g][:, c, :], CKV3f[g][:, c, :], sp(spb + SP_GKVA + c), RS.f, ALU.mult, ALU.mult,
                    r=[TS[g].CKV.t, SPR.t, RS.t], w=[TS[g].CKV.t])
            ps = bank("a")
            for c in range(3):
                act(SQ3[:, c, :], QA3[g][:, c, :], AF.Square, r=[TS[g].QA.t], w=[SQ.t])
            mm(ps.f, [(ONES_r, SQ3[:, c, :]) for c in range(3)], r=[CST.t, SQ.t], w=[ps.t])
            rstd_from_ps(ps, 384)
            for c in range(3):
                stt("dve", QA3[g][:, c, :], QA3[g][:, c, :], sp(spb + SP_GQA + c), RS.f, ALU.mult, ALU.mult,
                    r=[TS[g].QA.t, SPR.t, RS.t], w=[TS[g].QA.t])
        sCb, sCb3 = panel(928, 256)
        sCa, sCa3 = panel(672, 256)
        for c in range(2):
            for g in range(2):
                L = geo[g][1]
                ps = s1(sCb, sCb3, c * 128, 128, g)
                act(UGv[g][:, c, :, 15:15 + L], seg(ps.f, g), AF.Sigmoid, r=[ps.t], w=[TS[g].UG.t])
                ps2 = s1(sCa, sCa3, c * 128, 128, g)
                tt("dve", UGv[g][:, c, :, 15:15 + L], seg(ps2.f, g), UGf[g][:, c, :, 15:15 + L], ALU.mult,
                   r=[ps2.t, TS[g].UG.t], w=[TS[g].UG.t])
        sD, sD3 = panel(1184, 256)
        for c in range(2):
            for g in range(2):
                ps = s1(sD, sD3, c * 128, 128, g)
                cp("act", GBr3[g][:, c, :], ps.f, r=[ps.t], w=[TS[g].GB.t])
        sD, sD3 = panel(1440, 256)
        for c in range(2):
            for g in range(2):
                L = geo[g][1]
                ps = s1(sD, sD3, c * 128, 128, g)
                cp("act", VPv[g][:, c, :, 1:1 + L], seg(ps.f, g), r=[ps.t], w=[TS[g].VP.t])
        sE, sE3 = panel(1696, 256)
        for c in range(2):
            for g in range(2):
                L = geo[g][1]
                ps = s1(sE, sE3, c * 128, 128, g)
                tt("dve", VPv[g][:, c, :, 1:1 + L], seg(ps.f, g), VPf[g][:, c, :, 1:1 + L], ALU.mult,
                   r=[ps.t, TS[g].VP.t], w=[TS[g].VP.t])
        sE, sE3 = panel(1952, 256)
        for c in range(2):
            for g in range(2):
                ps = s1(sE, sE3, c * 128, 128, g)
                cp("act", FN3[g][:, c, :], ps.f, r=[ps.t], w=[TS[g].FN.t])

        KRS = TSA.KR96
        ps = bank("a")
        mm(ps.f[0:96, :], [(R96_r[64:96, 0:96], KRS.r[64:96, :])], r=[CST.t, KRS.t], w=[ps.t])
        tt("dve", T1.f[64:96, :], KRS.f[64:96, :], COS_f[64:96, :], ALU.mult, r=[KRS.t, CSTF.t], w=[T1.t])
        tt("dve", T2.f[64:96, :], ps.f[64:96, :], SIN_f[64:96, :], ALU.mult, r=[ps.t, CSTF.t], w=[T2.t])
        tt("dve", KRS.r[64:96, :], T1.f[64:96, :], T2.f[64:96, :], ALU.add, r=[T1.t, T2.t], w=[KRS.t])

        dma("sp", "o_ckv%d" % l, [(nckvT[l].rearrange("(c p) t -> p c t", p=128), CKV.f.rearrange("p (c t) -> p c t", c=2)),
                                    (nkrT[l], KR96.f[64:96, :])], r=[CKV.t, KR96.t], w=[T_OUT])
        A = TSA
        dma("pool", "spill%d" % l, [(SPL[:, 0:1536], A.QA.f),
                                    (SPL[:, 1536:2560].rearrange("p (c t) -> p c t", c=2), UGf[1][:, :, 0, 15:15 + 512]),
                                    (SPL[:, 2560:3584].rearrange("p (c t) -> p c t", c=2), VPf[1][:, :, 0, 1:1 + 512]),
                                    (SPL[:, 3584:4608], A.GB.f)], r=[A.QA.t, A.UG.t, A.VP.t, A.GB.t, T_SPL], w=[T_SPL])
        dma("pool", "exkv%d" % l, [
            (EXKV[0:256].rearrange("(c p) t -> p c t", p=128), A.CKV.f.rearrange("p (c t) -> p c t", c=2)),
            (EXKV[256:288], A.KR96.f[64:96, :])], r=[A.CKV.t, A.KR96.t, T_EXKV], w=[T_EXKV])
        dma("pool", "exfn%d" % l, [
            (EXFN.rearrange("(c p) t -> p c t", p=128), A.FN.f.rearrange("p (c t) -> p c t", c=2))], r=[A.FN.t, T_EXFN], w=[T_EXFN])
        dma("pool", "exe%d" % l, [
            (EXE[0:256, 0:16].rearrange("(c p) t -> p c t", p=128), UGf[1][:, :, 0, 15:31]),
            (EXE[0:256, 16:32].rearrange("(c p) t -> p c t", p=128), UGf[1][:, :, 0, 512 - 1:512 + 15]),
            (EXE[256:512, 0:16].rearrange("(c p) t -> p c t", p=128), VPf[1][:, :, 0, 1:17]),
            (EXE[256:512, 16:32].rearrange("(c p) t -> p c t", p=128), VPf[1][:, :, 0, 512 - 15:512 + 1])],
            r=[A.UG.t, A.VP.t, T_EXE], w=[T_EXE])
        P.op("pool", lambda e: e.collective_compute("AllGather", ALU.bypass, replica_groups=RG, ins=[EXKV.opt()], outs=[GAKV.opt()]),
             r=[T_EXKV, T_EXFN, T_EXE, T_GAKV], w=[T_GAKV], kind="cc", key="cckv%d" % l)
        P.op("pool", lambda e: e.collective_compute("AllGather", ALU.bypass, replica_groups=RG, ins=[EXFN.opt()], outs=[GAFN.opt()]),
             r=[T_EXFN, T_GAFN, T_GAKV], w=[T_GAFN], kind="cc", key="ccfn%d" % l)
        P.op("pool", lambda e: e.collective_compute("AllGather", ALU.bypass, replica_groups=RG, ins=[EXE.opt()], outs=[GAE[512:5 * 512].opt()]),
             r=[T_EXE, T_GAE, T_GAFN], w=[T_GAE], kind="cc", key="cce%d" % l)
        HL = HAL.f[:, 0:64].rearrange("p (c t) -> p c t", c=4)
        HR = HAL.f[:, 64:128].rearrange("p (c t) -> p c t", c=4)

        def halo(e):
            pid = e.partition_id()
            jb = (pid % 4) * 512
            return [e.dma_start(out=HL, in_=GAE[bass.DynSlice(jb, 512), 16:32].rearrange("(c p) t -> p c t", p=128)),
                    e.dma_start(out=HR, in_=GAE[bass.DynSlice(jb + 1024, 512), 0:16].rearrange("(c p) t -> p c t", p=128))]
        P.op("pool", halo, r=[T_GAE], w=[HAL.t], kind="d", nd=2, key="halo%d" % l)

    def mixer(l, g, part=0):
        P.mute = (part == 2)
        lat = 0 if g == 0 else 1
        nseq, L = (2, 256) if g == 0 else (1, 512)
        spb = l * NSP_L
        H3 = H.r.rearrange("p (c t) -> p c t", c=8)
        adanorm(l, g, 0, H)

        def panel(c0, ncol):
            s = wload(lambda s: [(s.r[:, 0:8 * ncol].rearrange("p (k n) -> p k n", k=8),
                                  w_in[l, :, c0:c0 + ncol].bitcast(F32R).rearrange("(k p) n -> p k n", p=128))])
            return s, s.r[:, 0:8 * ncol].rearrange("p (k n) -> p k n", k=8)

        def s1(s, s3, m0, M):
            ps = bank("a")
            mm(ps.f[0:M, :], [(s3[:, kc, m0:m0 + M], H3[:, kc, :]) for kc in range(8)], r=[s.t, H.t], w=[ps.t])
            return ps

        QA3 = QA.r.rearrange("p (c t) -> p c t", c=3)
        CKV3 = CKV.r.rearrange("p (c t) -> p c t", c=2)
        GB3 = GB.f.rearrange("p (c t) -> p c t", c=2)
        FN3 = FN.r.rearrange("p (c t) -> p c t", c=2)
        UGv = UG.r[:, 0:2 * nseq * (L + 30)].rearrange("p (c s t) -> p c s t", c=2, s=nseq)
        VPv = VP.r[:, 0:2 * nseq * (L + 2)].rearrange("p (c s t) -> p c s t", c=2, s=nseq)
        UGf = UG.f[:, 0:2 * nseq * (L + 30)].rearrange("p (c s t) -> p c s t", c=2, s=nseq)
        VPf = VP.f[:, 0:2 * nseq * (L + 2)].rearrange("p (c s t) -> p c s t", c=2, s=nseq)
        if g == 0:
            for c in range(2):
                for s_ in range(nseq):
                    cp("dve", UGv[:, c, s_, 0:15], ZERO_f[:, 0:15], r=[CSTF.t], w=[UG.t])
                    cp("dve", UGv[:, c, s_, 15 + L:30 + L], ZERO_f[:, 0:15], r=[CSTF.t], w=[UG.t])
                    cp("dve", VPv[:, c, s_, 0:1], ZERO_f[:, 0:1], r=[CSTF.t], w=[VP.t])
                    cp("dve", VPv[:, c, s_, 1 + L:2 + L], ZERO_f[:, 0:1], r=[CSTF.t], w=[VP.t])
        s, s3 = panel(0, 384)
        for c in range(3):
            ps = s1(s, s3, c * 128, 128)
            cp("act", QA3[:, c, :], ps.f, r=[ps.t], w=[QA.t])
        s, s3 = panel(384, 288)
        ckv_raw = [T2, T3]
        for c in range(2):
            ps = s1(s, s3, c * 128, 128)
            cp("act", ckv_raw[c].f, ps.f, r=[ps.t], w=[ckv_raw[c].t])
        ps = s1(s, s3, 192, 96)
        cp("dve", KR96.r[0:96, :], ps.f[0:96, :], r=[ps.t], w=[KR96.t])
        ps = bank("a")
        for c in range(2):
            act(CKV3[:, c, :], ckv_raw[c].f, AF.Square, r=[ckv_raw[c].t], w=[CKV.t])
        mm(ps.f, [(ONES_r, CKV3[:, c, :]) for c in range(2)], r=[CST.t, CKV.t], w=[ps.t])
        rstd_from_ps(ps, 256)
        for c in range(2):
            stt("dve", CKV3[:, c, :], ckv_raw[c].f, sp(spb + SP_GKVA + c), RS.f, ALU.mult, ALU.mult,
                r=[ckv_raw[c].t, SPR.t, RS.t], w=[CKV.t])
        ps = bank("a")
        OHs = OH.r.rearrange("p (c t) -> p c t", c=8)
        for c in range(3):
            act(OHs[:, c, :], QA3[:, c, :], AF.Square, r=[QA.t], w=[OH.t])
        mm(ps.f, [(ONES_r, OHs[:, c, :]) for c in range(3)], r=[CST.t, OH.t], w=[ps.t])
        rstd_from_ps(ps, 384)
        for c in range(3):
            stt("dve", QA3[:, c, :], QA3[:, c, :], sp(spb + SP_GQA + c), RS.f, ALU.mult, ALU.mult,
                r=[QA.t, SPR.t, RS.t], w=[QA.t])
        sCb, sCb3 = panel(928, 256)
        sCa, sCa3 = panel(672, 256)
        for c in range(2):
            ps = s1(sCb, sCb3, c * 128, 128)
            act(T1.f, ps.f, AF.Sigmoid, r=[ps.t], w=[T1.t])
            ps2 = s1(sCa, sCa3, c * 128, 128)
            tt("dve", UGv[:, c, :, 15:15 + L], ps2.f.rearrange("p (s t) -> p s t", s=nseq), T1.f.rearrange("p (s t) -> p s t", s=nseq),
               ALU.mult, r=[ps2.t, T1.t], w=[UG.t])
        GBr3 = GB.r.rearrange("p (c t) -> p c t", c=2)
        sD, sD3 = panel(1184, 256)
        for c in range(2):
            ps = s1(sD, sD3, c * 128, 128)
            cp("act", GBr3[:, c, :], ps.f, r=[ps.t], w=[GB.t])
        gct = [T2, T3]
        sD, sD3 = panel(1440, 256)
        for c in range(2):
            ps = s1(sD, sD3, c * 128, 128)
            cp("act", gct[c].f, ps.f, r=[ps.t], w=[gct[c].t])
        sE, sE3 = panel(1696, 256)
        for c in range(2):
            ps = s1(sE, sE3, c * 128, 128)
            tt("dve", VPv[:, c, :, 1:1 + L], ps.f.rearrange("p (s t) -> p s t", s=nseq), gct[c].f.rearrange("p (s t) -> p s t", s=nseq),
               ALU.mult, r=[ps.t, gct[c].t], w=[VP.t])
        sE, sE3 = panel(1952, 256)
        for c in range(2):
            ps = s1(sE, sE3, c * 128, 128)
            cp("act", FN3[:, c, :], ps.f, r=[ps.t], w=[FN.t])

        def rope(dst_f, dst_r, dst_t):
            ps = bank("a")
            mm(ps.f[0:96, :], [(R96_r[64:96, 0:96], dst_r)], r=[CST.t, dst_t], w=[ps.t])
            tt("dve", T1.f[64:96, :], dst_f, COS_f[64:96, :], ALU.mult, r=[dst_t, CSTF.t], w=[T1.t])
            tt("dve", T2.f[64:96, :], ps.f[64:96, :], SIN_f[64:96, :], ALU.mult, r=[ps.t, CSTF.t], w=[T2.t])
            tt("dve", dst_r, T1.f[64:96, :], T2.f[64:96, :], ALU.add, r=[T1.t, T2.t], w=[dst_t])

        if g == 1:
            rope(KR96.f[64:96, :], KR96.r[64:96, :], KR96.t)

        if g == 0:
            dma("sp", "o_ckv%d" % l, [(nckvT[l].rearrange("(c p) t -> p c t", p=128), CKV.f.rearrange("p (c t) -> p c t", c=2)),
                                        (nkrT[l], KR96.f[64:96, :])], r=[CKV.t, KR96.t], w=[T_OUT])
        else:
            if not os.environ.get("KD_NOSPILL"):
              dma("pool", "spill%d" % l, [(SPL[:, 0:1536], QA.f), (SPL[:, 1536:2680], UG.f), (SPL[:, 2680:3712], VP.f),
                                        (SPL[:, 3712:4736], GB.f)], r=[QA.t, UG.t, VP.t, GB.t, T_SPL], w=[T_SPL])

            dma("pool", "exkv%d" % l, [
                (EXKV[0:256].rearrange("(c p) t -> p c t", p=128), CKV.f.rearrange("p (c t) -> p c t", c=2)),
                (EXKV[256:288], KR96.f[64:96, :])], r=[CKV.t, KR96.t, T_EXKV], w=[T_EXKV])
            P.op("pool", lambda e: e.collective_compute("AllGather", ALU.bypass, replica_groups=RG,
                                                        ins=[EXKV.opt()], outs=[GAKV.opt()]),
                 r=[T_EXKV, T_GAKV], w=[T_GAKV], kind="cc", key="cckv%d" % l)
            dma("pool", "exfn%d" % l, [
                (EXFN.rearrange("(c p) t -> p c t", p=128), FN.f.rearrange("p (c t) -> p c t", c=2))], r=[FN.t, T_EXFN], w=[T_EXFN])
            P.op("pool", lambda e: e.collective_compute("AllGather", ALU.bypass, replica_groups=RG,
                                                        ins=[EXFN.opt()], outs=[GAFN.opt()]),
                 r=[T_EXFN, T_GAFN], w=[T_GAFN], kind="cc", key="ccfn%d" % l)
            dma("pool", "exe%d" % l, [
                (EXE[0:256, 0:16].rearrange("(c p) t -> p c t", p=128), UGf[:, :, 0, 15:31]),
                (EXE[0:256, 16:32].rearrange("(c p) t -> p c t", p=128), UGf[:, :, 0, L - 1:L + 15]),
                (EXE[256:512, 0:16].rearrange("(c p) t -> p c t", p=128), VPf[:, :, 0, 1:17]),
                (EXE[256:512, 16:32].rearrange("(c p) t -> p c t", p=128), VPf[:, :, 0, L - 15:L + 1])],
                r=[UG.t, VP.t, T_EXE], w=[T_EXE])
            P.op("pool", lambda e: e.collective_compute("AllGather", ALU.bypass, replica_groups=RG,
                                                        ins=[EXE.opt()], outs=[GAE[512:5 * 512].opt()]),
                 r=[T_EXE, T_GAE], w=[T_GAE], kind="cc", key="cce%d" % l)
            HL = HAL.f[:, 0:64].rearrange("p (c t) -> p c t", c=4)
            HR = HAL.f[:, 64:128].rearrange("p (c t) -> p c t", c=4)

            def halo(e):
                pid = e.partition_id()
                jb = (pid % 4) * 512
                return [e.dma_start(out=HL, in_=GAE[bass.DynSlice(jb, 512), 16:32].rearrange("(c p) t -> p c t", p=128)),
                        e.dma_start(out=HR, in_=GAE[bass.DynSlice(jb + 1024, 512), 0:16].rearrange("(c p) t -> p c t", p=128))]
            P.op("pool", halo, r=[T_GAE], w=[HAL.t], kind="d", nd=2, key="halo%d" % l)
        if part == 1:
            return
        P.mute = False
        if part == 2 and g == 1:
            dma("pool", "reload%d" % l, [(QA.r, SPL[:, 0:1536].bitcast(F32R)),
                                         (UGv[:, :, 0, 15:15 + L], SPL[:, 1536:2560].bitcast(F32R).rearrange("p (c t) -> p c t", c=2)),
                                         (VPv[:, :, 0, 1:1 + L], SPL[:, 2560:3584].bitcast(F32R).rearrange("p (c t) -> p c t", c=2)),
                                         (GB.r, SPL[:, 3584:4608].bitcast(F32R))],
                r=[T_SPL], w=[QA.t, UG.t, VP.t, GB.t])

        if g == 1:
            HL = HAL.f[:, 0:64].rearrange("p (c t) -> p c t", c=4)
            HR = HAL.f[:, 64:128].rearrange("p (c t) -> p c t", c=4)
            for ch in range(2):
                cp("pool", UGv[:, ch, 0, 0:15], HL[:, ch, 1:16], r=[HAL.t], w=[UG.t])
                cp("pool", UGv[:, ch, 0, 15 + L:30 + L], HR[:, ch, 0:15], r=[HAL.t], w=[UG.t])
                cp("pool", VPv[:, ch, 0, 0:1], HL[:, 2 + ch, 15:16], r=[HAL.t], w=[VP.t])
                cp("pool", VPv[:, ch, 0, 1 + L:2 + L], HR[:, 2 + ch, 0:1], r=[HAL.t], w=[VP.t])

        for c in range(2):
            T1v = T1.f.rearrange("p (s t) -> p s t", s=nseq)
            ts("dve", T1v, VPf[:, c, :, 0:L], sp(spb + SP_WSC + 0 * 2 + c), None, ALU.mult, None, r=[VP.t, SPR.t], w=[T1.t])
            for k in (1, 2):
                stt("dve", T1v, VPf[:, c, :, k:k + L], sp(spb + SP_WSC + k * 2 + c), T1v, ALU.mult, ALU.add,
                    r=[VP.t, SPR.t, T1.t], w=[T1.t])
            tt("dve", GB.r.rearrange("p (c t) -> p c t", c=2)[:, c, :], GB3[:, c, :], T1.f, ALU.mult, r=[GB.t, T1.t], w=[GB.t])

        U23 = U2.r.rearrange("p (c t) -> p c t", c=2)
        cacc = [T2, T3]
        for c in range(2):
            pc = bank("c")
            for (k0, k1) in ((0, 16), (16, 31)):
                s = RING[ring_i[0] % len(RING)]
                ring_i[0] += 1
                blks = []
                for k in range(k0, k1):
                    bv = V("r", s.off + (k - k0) * 128, 128, "diag")
                    if k % 2:
                        act(bv.r, IDENT.f, AF.Identity, r=[IDENT.t, SPR.t], w=[bv.t], scale=sp(spb + SP_WCDW + k * 2 + c))
                    else:
                        ts("dve", bv.r, IDENT.f, sp(spb + SP_WCDW + k * 2 + c), None, ALU.mult, None, r=[IDENT.t, SPR.t], w=[bv.t])
                    blks.append(bv)

                def fn(e, k0=k0, k1=k1, c=c, pc=pc, blks=blks):
                    ins = None
                    for k in range(k0, k1):
                        ins = e.matmul(pc.f, lhsT=blks[k - k0].r, rhs=UGv[:, c, :, k:k + L], start=(k == 0), stop=(k == 30))
                    return ins
                P.op("pe", fn, r=[s.t, UG.t], w=[pc.t])
            act(cacc[c].f, pc.f, AF.Identity, r=[pc.t, SPR.t], w=[cacc[c].t], bias=sp(spb + SP_BCDW + c))
        ps = bank("a")
        for c in range(2):
            cp("act", U23[:, c, :], cacc[c].f, r=[cacc[c].t], w=[U2.t])
        mm(ps.f, [(ONES_r, U23[:, c, :]) for c in range(2)], r=[CST.t, U2.t], w=[ps.t])
        for c in range(2):
            stt("dve", cacc[c].f, ps.f, -1.0 / 256, cacc[c].f, ALU.mult, ALU.add, r=[ps.t, cacc[c].t], w=[cacc[c].t])
        ps = bank("a")
        for c in range(2):
            act(U23[:, c, :], cacc[c].f, AF.Square, r=[cacc[c].t], w=[U2.t])
        mm(ps.f, [(ONES_r, U23[:, c, :]) for c in range(2)], r=[CST.t, U2.t], w=[ps.t])
        rstd_from_ps(ps, 256)
        for c in range(2):
            tt("dve", cacc[c].f, cacc[c].f, RS.f, ALU.mult, r=[cacc[c].t, RS.t], w=[cacc[c].t])
            act(U23[:, c, :], cacc[c].f, AF.Silu, r=[cacc[c].t, SPR.t], w=[U2.t], bias=sp(spb + SP_BCLN + c), scale=sp(spb + SP_GCLN + c))

        WQ = wload(lambda s: [(s.r[:, 0:2304].rearrange("p (k n) -> p k n", k=3), w_qb[l].bitcast(F32R).rearrange("(k p) n -> p k n", p=128))])
        WKV = wload(lambda s: [(s.r[:, 0:2048].rearrange("p (k n) -> p k n", k=2), w_kvb[l].bitcast(F32R).rearrange("(k p) n -> p k n", p=128))])
        WQ3 = WQ.r[:, 0:2304].rearrange("p (k n) -> p k n", k=3)
        WKV4 = WKV.r[:, 0:2048].rearrange("p (k h n) -> p k h n", k=2, h=8)

        QH3r = QH.r.rearrange("p (h t) -> p h t", h=8)
        QH3f = QH.f.rearrange("p (h t) -> p h t", h=8)
        for h in range(8):
            ps = bank("a")
            mm(ps.f[0:96, :], [(WQ3[:, kc, h * 96:(h + 1) * 96], QA3[:, kc, :]) for kc in range(3)], r=[WQ.t, QA.t], w=[ps.t])
            cp("act", QH3r[0:96, h, :], ps.f[0:96, :], r=[ps.t], w=[QHh[h].t])

        if g == 1:
            for h in range(8):
                rope(QH3f[64:96, h, :], QH3r[64:96, h, :], QHh[h].t)
        pfmap = {}
        if g == 1:
            for i, (pn_, cs_) in enumerate(((0, 0), (0, 1), (1, 0))):
                dma("sp", "pf%d" % i, [(PF[i].r.rearrange("p (l n) -> p l n", l=4),
                                         dftS[cs_, pn_ * 512:(pn_ + 1) * 512, :].bitcast(F32R).rearrange("(l p) n -> p l n", p=128))], w=[PF[i].t])
                pfmap[(pn_, cs_)] = PF[i]
        KH3 = KH.r.rearrange("p (h t) -> p h t", h=8)
        VB4 = VB.r.rearrange("p (c h d) -> p c h d", c=4, h=8)
        VB4f = VB.f.rearrange("p (c h d) -> p c h d", c=4, h=8)
        CKVB3 = CKVB.r.rearrange("p (c t) -> p c t", c=2)
        OH3f = OH.f.rearrange("p (h t) -> p h t", h=8)
        OH3r = OH.r.rearrange("p (h t) -> p h t", h=8)
        for ck_ in range(4):
            cp("dve", VB4[:, ck_, :, 64], CST.f[:, 0:8], r=[CST.t], w=[VBc[ck_].t])
        scale = 96.0 ** -0.5
        et_i = [0]

        def attend_block(ckv_r, ckv_t, kr_r, kr_t, nk, q0, nq, first, half=None, phase="both"):
            kc0 = 0 if half is None else half * 256
            vc0 = 0 if half is None else half * 2

            def kt(h):
                return [KHq[h][0].t, KHq[h][1].t] if half is None else [KHq[h][half].t]
            nck = nk // 128
            if phase in ("both", "prep"):
              for h in range(8):
                ps = bank("a")
                mm(ps.f[0:64, 0:nk], [(WKV4[:, kc, h, 0:64], ckv_r(kc)) for kc in range(2)], r=[WKV.t, ckv_t], w=[ps.t])
                cp("act" if h % 2 else "dve", KH3[0:64, h, kc0:kc0 + nk], ps.f[0:64, 0:nk], r=[ps.t], w=kt(h))
              for h in range(8):
                cp("act" if h % 2 else "dve", KH3[64:96, h, kc0:kc0 + nk], kr_r, r=[kr_t], w=kt(h))
              for ck in range(nck):
                ps = bank("a")
                mm(ps.f, [(ckv_r(kc)[:, ck * 128:(ck + 1) * 128], WKV4[:, kc, :, 64:128]) for kc in range(2)], r=[WKV.t, ckv_t], w=[ps.t])
                cp("act" if ck % 2 else "dve", VB4[:, vc0 + ck, :, 0:64], ps.f.rearrange("p (h d) -> p h d", h=8), r=[ps.t], w=[VBc[vc0 + ck].t])
            if phase == "prep":
                return
            items = [(h, ck) for h in range(8) for ck in range(nck)]
            LOOK = 2
            pos = {}
            ets = {}
            for i in range(len(items) + LOOK):
                if i < len(items):
                    h, ck = items[i]
                    pss = bank("b")
                    mm(pss.f[:, 0:nq], [(KH3[0:96, h, kc0 + ck * 128:kc0 + (ck + 1) * 128], QH3r[0:96, h, q0:q0 + nq])], r=kt(h) + [QHh[h].t], w=[pss.t])
                    et = ET[et_i[0] % 3]
                    et_i[0] += 1
                    act(et.r[:, 0:nq], pss.f[:, 0:nq], AF.Exp, r=[pss.t], w=[et.t], scale=scale)
                    ets[i] = et
                j = i - LOOK
                if j >= 0:
                    h, ck = items[j]
                    if ck == 0:
                        pos[h] = bank("c")
                    po, et = pos[h], ets.pop(j)
                    P.op("pe", lambda e, po=po, ck=ck, h=h, et=et: e.matmul(po.f[0:65, 0:nq], lhsT=VB4[:, vc0 + ck, h, :], rhs=et.r[:, 0:nq],
                                                                            start=(ck == 0), stop=(ck == nck - 1)),
                         r=[VBc[vc0 + ck].t, et.t], w=[po.t])
                    if ck == nck - 1:
                        if first:
                            cp("dve", OH3r[0:65, h, q0:q0 + nq], po.f[0:65, 0:nq], r=[po.t], w=[OHh[h].t])
                        else:
                            tt("dve", OH3r[0:65, h, q0:q0 + nq], OH3f[0:65, h, q0:q0 + nq], po.f[0:65, 0:nq], ALU.add, r=[po.t, OHh[h].t], w=[OHh[h].t])

        if g == 0:
            for ph in ("prep", "loop"):
                for s_ in range(2):
                    q0 = s_ * 256
                    attend_block(lambda kc, q0=q0: CKV3[:, kc, q0:q0 + 256], CKV.t, KR96.r[64:96, q0:q0 + 256], KR96.t, 256, q0, 256, True,
                                 half=s_, phase=ph)
        else:
            dma("pool", "kvb", [(CKVB3[:, :, 0:256], cckvT[l].bitcast(F32R).rearrange("(c p) t -> p c t", p=128)),
                                (KRB.r[64:96, 0:256], ckrT[l].bitcast(F32R))], w=[CKVB.t, KRB.t])
            attend_block(lambda kc: CKVB3[:, kc, 0:256], CKVB.t, KRB.r[64:96, 0:256], KRB.t, 256, 0, 512, True)
            for rk in range(4):
                r0 = rk * 288
                dma("pool", "kvb", [(CKVB3, GAKV[r0:r0 + 256].bitcast(F32R).rearrange("(c p) t -> p c t", p=128)),
                                    (KRB.r[64:96, :], GAKV[r0 + 256:r0 + 288].bitcast(F32R))], r=[T_GAKV], w=[CKVB.t, KRB.t])
                attend_block(lambda kc: CKVB3[:, kc, :], CKVB.t, KRB.r[64:96, :], KRB.t, 512, 0, 512, False)
        for h in range(8):
            ps = bank("a")
            P.op("pe", lambda e, ps=ps, h=h: e.matmul(ps.f[0:64, :], lhsT=SEL_r[0:65, :], rhs=OH3r[0:65, h, :], start=True, stop=True),
                 r=[CST.t, OHh[h].t], w=[ps.t])
            tq = T1 if h % 2 == 0 else T2
            act(tq.f[0:64, :], ps.f[0:64, :], AF.Ln, r=[ps.t], w=[tq.t])
            act(tq.f[0:64, :], tq.f[0:64, :], AF.Exp, r=[tq.t], w=[tq.t], scale=-1.0)
            tt("dve", OH3r[0:64, h, :], OH3f[0:64, h, :], tq.f[0:64, :], ALU.mult, r=[OHh[h].t, tq.t], w=[OHh[h].t])

        F23 = F2.r.rearrange("p (c t) -> p c t", c=2)
        if g == 0:
            ABp = AB.r[:, 0:2048].rearrange("p (s l j n) -> p s l j n", s=2, l=2, j=2)
            for s_ in range(2):
                for lc in range(2):
                    for jc in range(2):
                        ps = bank("a")
                        t0 = s_ * 256 + lc * 128
                        mm(ps.f[:, 0:256], [(FN3[:, jc, t0:t0 + 128], BD_r)], r=[FN.t, CST.t], w=[ps.t])
                        cp("act", ABp[:, s_, lc, jc, :], ps.f[:, 0:256], r=[ps.t], w=[ABt[s_ * 2 + lc].t])
            adanorm(l, g, 0, H)
            for s_ in range(2):
                for jc in range(2):
                    ps = bank("a")
                    pairs = []
                    for lc in range(2):
                        pairs.append((ABp[:, s_, lc, jc, 0:128], DFP[0][:, lc, :]))
                        pairs.append((ABp[:, s_, lc, jc, 128:256], DFP[1][:, lc, :]))
                    mm(ps.f[:, 0:256], pairs, r=[AB.t, DFPT.t], w=[ps.t])
                    cp("act", F23[:, jc, s_ * 256:(s_ + 1) * 256], ps.f[:, 0:256], r=[ps.t], w=[F2.t])
        else:
            FNF3 = FNFULL.r.rearrange("p (c t) -> p c t", c=2)
            dma("pool", "fnf", [(FNF3[:, :, rk * 512:(rk + 1) * 512],
                                 GAFN[rk * 256:(rk + 1) * 256].bitcast(F32R).rearrange("(c p) t -> p c t", p=128))
                                for rk in range(4)], r=[T_GAFN], w=[FNFULL.t])
            ABs = AB.r.rearrange("p (l j n) -> p l j n", l=16, j=2)
            for lc in range(16):
                for jc in range(2):
                    ps = bank("a")
                    mm(ps.f[:, 0:256], [(FNF3[:, jc, lc * 128:(lc + 1) * 128], BD_r)], r=[FNFULL.t, CST.t], w=[ps.t])
                    cp("act" if (lc + jc) % 2 else "dve", ABs[:, lc, jc, :], ps.f[:, 0:256], r=[ps.t], w=[ABt[lc].t])
            adanorm(l, g, 0, H)
            pacc = [bank("c"), bank("c")]
            for pnl in range(4):
                sl = []
                for cs in range(2):
                    if (pnl, cs) in pfmap:
                        s = pfmap[(pnl, cs)]
                    else:
                        s = wload(lambda s, cs=cs, pnl=pnl: [(s.r[:, 0:2048].rearrange("p (l n) -> p l n", l=4),
                                                              dftS[cs, pnl * 512:(pnl + 1) * 512, :].bitcast(F32R).rearrange("(l p) n -> p l n", p=128))])
                    sl.append(s)
                for jc in range(2):
                    def fn(e, jc=jc, pnl=pnl, sl=sl):
                        ins = None
                        for li in range(4):
                            lc = pnl * 4 + li
                            for cs in range(2):
                                ins = e.matmul(pacc[jc].f, lhsT=ABs[:, lc, jc, cs * 128:(cs + 1) * 128],
                                               rhs=sl[cs].r[:, 0:2048].rearrange("p (l n) -> p l n", l=4)[:, li, :],
                                               start=(lc == 0 and cs == 0), stop=(lc == 15 and cs == 1))
                        return ins
                    P.op("pe", fn, r=[AB.t, sl[0].t, sl[1].t], w=[pacc[jc].t])
            for jc in range(2):
                cp("act", F23[:, jc, :], pacc[jc].f, r=[pacc[jc].t], w=[F2.t])

        MG3 = MERGED.r.rearrange("p (c t) -> p c t", c=8)
        for dm in range(8):
            c0 = OFF_GATE + dm * 128
            sg = wload(lambda s, dm=dm: [(s.r[:, 0:3072], wmerge[l, dm, :, 0:3072].bitcast(F32R))])
            so = wload(lambda s, dm=dm: [(s.r[:, 0:2816], wmerge[l, dm, :, 3072:5888].bitcast(F32R))])
            first = True
            for b in range(4):
                src = sg if b < 3 else so
                boff = (b if b < 3 else 0) * 1024
                g3 = src.r[:, boff:boff + 1024].rearrange("p (k n) -> p k n", k=8)
                psg = bank("a")
                mm(psg.f, [(g3[:, kc, :], H3[:, kc, :]) for kc in range(8)], r=[src.t, H.t], w=[psg.t])
                act(T2.f, psg.f, AF.Sigmoid, r=[psg.t], w=[T2.t])
                psy = bank("a")
                if b == 0:
                    wo = so.r[0:64, 1024:2048].rearrange("p (h n) -> p h n", h=8)
                    mm(psy.f, [(wo[:, h, :], OH3r[0:64, h, :]) for h in range(8)], r=[so.t, OH.t], w=[psy.t])
                else:
                    boffs = {1: 2048, 2: 2304, 3: 2560}[b]
                    w3 = so.r[:, boffs:boffs + 256].rearrange("p (k n) -> p k n", k=2)
                    srcv, srct = {1: (U23, U2.t), 2: (GBr3, GB.t), 3: (F23, F2.t)}[b]
                    mm(psy.f, [(w3[:, kc, :], srcv[:, kc, :]) for kc in range(2)], r=[so.t, srct], w=[psy.t])
                if first:
                    tt("dve", MG3[:, dm, :], psy.f, T2.f, ALU.mult, r=[psy.t, T2.t], w=[MERGED.t])
                    first = False
                else:
                    tt("dve", T1.f, psy.f, T2.f, ALU.mult, r=[psy.t, T2.t], w=[T1.t])
                    tt("dve", MG3[:, dm, :], MG3[:, dm, :], T1.f, ALU.add, r=[MERGED.t, T1.t], w=[MERGED.t])
        for pnl in range(3):
            c0 = pnl * 384
            nm = 3 if pnl < 2 else 2
            s = wload(lambda s, c0=c0, nm=nm: [(s.r[:, 0:8 * nm * 128].rearrange("p (k n) -> p k n", k=8),
                                                w_out[l, :, c0:c0 + nm * 128].bitcast(F32R).rearrange("(k p) n -> p k n", p=128))])
            s3 = s.r[:, 0:8 * nm * 128].rearrange("p (k n) -> p k n", k=8)
            for mi in range(nm):
                m = pnl * 3 + mi
                ps = bank("a")
                mm(ps.f, [(s3[:, kc, mi * 128:(mi + 1) * 128], MG3[:, kc, :]) for kc in range(8)], r=[s.t, MERGED.t], w=[ps.t])
                stt("dve", xg(m, g), ps.f, modv(l, 16 + m, lat), xg(m, g), ALU.mult, ALU.add, r=[ps.t, MOD[l].t, XT[m][g]], w=[XT[m][g]])

    DFPT = CST
    DFP = [CST.r[:, 544:1056].rearrange("p (l n) -> p l n", l=2), CST.r[:, 1056:1568].rearrange("p (l n) -> p l n", l=2)]

    def ffn(l):
        H23 = H2.r.rearrange("p (c t) -> p c t", c=8)
        HID3 = HID.r.rearrange("p (c t) -> p c t", c=11)
        norm_both(l, 1, lambda g, c: H23[:, c, g * 512:(g + 1) * 512], lambda g, c: H2T[c][g])
        for half in range(2):
            for jp in range(0, 11, 3):
                nj = min(3, 11 - jp)
                j0 = (half * 11 + jp) * 128
                sg = wload(lambda s, j0=j0, nj=nj: [(s.r[:, 0:8 * nj * 128].rearrange("p (k n) -> p k n", k=8),
                                                     w_ffn_gate[l, :, j0:j0 + nj * 128].bitcast(F32R).rearrange("(k p) n -> p k n", p=128))])
                su = wload(lambda s, j0=j0, nj=nj: [(s.r[:, 0:8 * nj * 128].rearrange("p (k n) -> p k n", k=8),
                                                     w_ffn_up[l, :, j0:j0 + nj * 128].bitcast(F32R).rearrange("(k p) n -> p k n", p=128))])
                sg3 = sg.r[:, 0:8 * nj * 128].rearrange("p (k n) -> p k n", k=8)
                su3 = su.r[:, 0:8 * nj * 128].rearrange("p (k n) -> p k n", k=8)
                for ji in range(nj):
                    for g in range(2):
                        tsl = slice(g * 512, (g + 1) * 512)
                        pg = bank("a")
                        mm(pg.f, [(sg3[:, kc, ji * 128:(ji + 1) * 128], H23[:, kc, tsl]) for kc in range(8)], r=[sg.t, H2.t], w=[pg.t])
                        tmp = T2 if g == 0 else T3
                        act(tmp.f, pg.f, AF.Silu, r=[pg.t], w=[tmp.t])
                        pu = bank("a")
                        mm(pu.f, [(su3[:, kc, ji * 128:(ji + 1) * 128], H23[:, kc, tsl]) for kc in range(8)], r=[su.t, H2.t], w=[pu.t])
                        tt("dve", HID3[:, jp + ji, tsl], pu.f, tmp.f, ALU.mult, r=[pu.t, tmp.t], w=[HID.t])
            for m in range(8):
                pss = [bank("c"), bank("c")]
                s = wload(lambda s, m=m, half=half: [(s.r[:, 0:1408], wdown[l, half, m].bitcast(F32R))])
                s3 = s.r[:, 0:1408].rearrange("p (j n) -> p j n", j=11)
                for g in range(2):
                    def fn(e, s3=s3, g=g, pss=pss):
                        ins = None
                        for j in range(11):
                            ins = e.matmul(pss[g].f, lhsT=s3[:, j, :], rhs=HID3[:, j, g * 512:(g + 1) * 512],
                                           start=(j == 0), stop=(j == 10))
                        return ins
                    P.op("pe", fn, r=[s.t, HID.t], w=[pss[g].t])
                for g in range(2):
                    stt("dve", xg(m, g), pss[g].f, modv(l, 40 + m, g), xg(m, g), ALU.mult, ALU.add, r=[pss[g].t, MOD[l].t, XT[m][g]], w=[XT[m][g]])

    ada_mm()
    H3G = [TSM.H.r.rearrange("p (c t) -> p c t", c=8), TSA.H.r.rearrange("p (c t) -> p c t", c=8)]
    norm_pre(lambda g, c: H3G[g][:, c, :], lambda g, c: (TSM, TSA)[g].H.t)
    ada_fin()
    for l in range(DEPTH):
        scl_prep(l)
        stage1_both(l, pre=(l == 0))
        mixer(l, 0, part=2)
        mixer(l, 1, part=2)
        ffn(l)
    for g in range(2):
        O3r = H2.r.rearrange("p (c t) -> p c t", c=8)
        tsl = slice(g * 512, (g + 1) * 512)
        ps = bank("a")
        for c in range(8):
            act(O3r[:, c, tsl], xg(c, g), AF.Square, r=[XT[c][g]], w=[H2.t])
        mm(ps.f, [(ONES_r, O3r[:, c, tsl]) for c in range(8)], r=[CST.t, H2.t], w=[ps.t])
        rstd_from_ps(ps, D)
        dst = (ypT if g == 0 else ysT).rearrange("(c p) t -> c p t", p=128)
        stg = [T1, T2, T3]
        for c in range(8):
            st_ = stg[c % 3]
            stt("dve", st_.f, xg(c, g), sp(SP_GFIN + c), RS.f, ALU.mult, ALU.mult, r=[XT[c][g], SPR.t, RS.t], w=[st_.t])
            dma("pool", "oy%d" % (c % 3), [(dst[c], st_.f)], r=[st_.t], w=[])

    P.emit(stack)
    stack.close()
    return nc, P


def _consts(j):
    cst = np.zeros((128, NCST), np.float32)
    cst[:, 0:128] = 1.0
    R = np.zeros((32, 32), np.float32)
    for blk in (0, 16):
        for i in range(8):
            R[blk + 8 + i, blk + i] = -1.0
            R[blk + i, blk + 8 + i] = 1.0
    cst[64:96, 128 + 64:128 + 96] = R
    cst[64, 224:288] = 1.0
    pos = np.arange(j * 512, (j + 1) * 512)
    row, col = (pos // 64).astype(np.float32), (pos % 64).astype(np.float32)
    inv = (10000.0 ** (-np.arange(0, 16, 2, dtype=np.float32) / 16)).astype(np.float32)
    ar, ac = row[None, :] * inv[:, None], col[None, :] * inv[:, None]
    cosT = np.concatenate([np.cos(ar), np.cos(ar), np.cos(ac), np.cos(ac)], 0)
    sinT = np.concatenate([np.sin(ar), np.sin(ar), np.sin(ac), np.sin(ac)], 0)
    cst[64:96, 288:800] = cosT
    cst[64:96, 800:1312] = sinT
    k = np.arange(64)
    a64 = 2 * np.pi * np.outer(k, k) / 64
    C64, S64 = np.cos(a64) / 8.0, np.sin(a64) / 8.0
    bd = np.zeros((128, 256))
    for b in range(2):
        bd[b * 64:(b + 1) * 64, b * 64:(b + 1) * 64] = C64
        bd[b * 64:(b + 1) * 64, 128 + b * 64:128 + (b + 1) * 64] = S64
    cst[:, 1312:1568] = bd
    n = np.arange(256)
    a = 2 * np.pi * np.outer(n, n) / 256
    CL, NSL = np.cos(a) / 16.0, -np.sin(a) / 16.0
    cst[:, C_CL:C_CL + 512] = CL.reshape(2, 128, 256).transpose(1, 0, 2).reshape(128, 512)
    cst[:, C_NSL:C_NSL + 512] = NSL.reshape(2, 128, 256).transpose(1, 0, 2).reshape(128, 512)
    cst[:, C_EPS] = EPS
    cst[:, C_ID:C_ID + 128] = np.eye(128, dtype=np.float32)
    return cst


def _dft_s(j):
    n = np.arange(2048, dtype=np.int64)
    m = np.arange(j * 512, (j + 1) * 512, dtype=np.int64)
    a = 2 * np.pi * ((np.outer(n, m) % 2048).astype(np.float64)) / 2048
    s = 1.0 / math.sqrt(2048.0)
    return np.stack([np.cos(a) * s, -np.sin(a) * s]).astype(np.float32)


_CACHE = {}


def kernel(**inputs):
    f = lambda k: np.ascontiguousarray(np.asarray(inputs[k], dtype=np.float32))
    if "nc" not in _CACHE:
        _CACHE["nc"] = build_program()[0]
    nc = _CACHE["nc"]
    xp, xs = f("x_prompt"), f("x_sample")
    cckv, ckr, cc, cctx = f("cache_ckv"), f("cache_krope"), f("c"), f("c_ctx")
    wnames = ["w_in", "w_qb", "w_kvb", "w_out", "w_ffn_gate", "w_ffn_up"]
    W = {k: f(k) for k in wnames}
    win, wo, wpw, wsc, wfn, wdn = f("w_in"), f("w_o_mla"), f("w_conf_pw"), f("w_sc_out"), f("w_fn"), f("w_ffn_down")
    wm = np.zeros((DEPTH, 8, 128, 5888), np.float32)
    for dm in range(8):
        d0 = dm * 128
        for b in range(4):
            blk = win[:, :, OFF_GATE + b * D + d0:OFF_GATE + b * D + d0 + 128].reshape(DEPTH, 8, 128, 128).transpose(0, 2, 1, 3).reshape(DEPTH, 128, 1024)
            off = b * 1024 if b < 3 else 3072
            wm[:, dm, :, off:off + 1024] = blk
        wm[:, dm, 0:64, 4096:5120] = wo[:, :, d0:d0 + 128].reshape(DEPTH, 8, 64, 128).transpose(0, 2, 1, 3).reshape(DEPTH, 64, 1024)
        for i, wsrc in enumerate((wpw, wsc, wfn)):
            wm[:, dm, :, 5120 + i * 256:5120 + (i + 1) * 256] = wsrc[:, :, d0:d0 + 128].reshape(DEPTH, 2, 128, 128).transpose(0, 2, 1, 3).reshape(DEPTH, 128, 256)
    W["wmerge"] = wm
    W["wdown"] = np.ascontiguousarray(wdn.reshape(DEPTH, 2, 11, 128, 8, 128).transpose(0, 1, 4, 3, 2, 5).reshape(DEPTH, 2, 8, 128, 1408))
    wada = f("w_ada")

    def colsT(v):
        return v.reshape(-1, 128).T

    in_maps = []
    for c in range(8):
        b, j = c // 4, c % 4
        spm = np.zeros((128, NSP), np.float32)
        for l in range(DEPTH):
            o = l * NSP_L
            spm[:, o + SP_GN1:o + SP_GN1 + 8] = colsT(f("g_norm1")[l])
            spm[:, o + SP_GN2:o + SP_GN2 + 8] = colsT(f("g_norm2")[l])
            spm[:, o + SP_GQA:o + SP_GQA + 3] = colsT(f("g_qa")[l])
            spm[:, o + SP_GKVA:o + SP_GKVA + 2] = colsT(f("g_kva")[l])
            spm[:, o + SP_BCDW:o + SP_BCDW + 2] = colsT(f("b_conf_dw")[l])
            spm[:, o + SP_GCLN:o + SP_GCLN + 2] = colsT(f("g_conf_ln")[l])
            spm[:, o + SP_BCLN:o + SP_BCLN + 2] = colsT(f("b_conf_ln")[l])
            spm[:, o + SP_WSC:o + SP_WSC + 6] = f("w_sc_conv")[l].reshape(3, 2, 128).transpose(2, 0, 1).reshape(128, 6)
            spm[:, o + SP_WCDW:o + SP_WCDW + 62] = f("w_conf_dw")[l].reshape(31, 2, 128).transpose(2, 0, 1).reshape(128, 62)
            spm[:, o + SP_BADA:o + SP_BADA + 12] = colsT(f("b_ada")[l][j * 1536:(j + 1) * 1536])
        spm[:, SP_GFIN:SP_GFIN + 8] = colsT(f("g_final"))
        spm[:, SP_CCTX:SP_CCTX + 8] = colsT(cctx)
        spm[:, SP_CLAT:SP_CLAT + 8] = colsT(cc[0])
        spm[:, SP_CLAT + 8:SP_CLAT + 16] = colsT(cc[1])
        m = {
            "xpT": np.ascontiguousarray(xp[2 * c:2 * c + 2].reshape(512, D).T),
            "xsT": np.ascontiguousarray(xs[b, j * 512:(j + 1) * 512].T),
            "cckvT": np.ascontiguousarray(cckv[b].transpose(0, 2, 1)),
            "ckrT": np.ascontiguousarray(ckr[b].transpose(0, 2, 1)),
            "smallp": spm,
            "cst": _consts(j),
            "dftS": _dft_s(j),
            "w_ada_s": np.ascontiguousarray(wada[:, :, j * 1536:(j + 1) * 1536]),
        }
        m.update(W)
        in_maps.append(m)
    res = run_bass_kernel_spmd(nc, in_maps, core_ids=list(range(8))).results
    y_prompt = np.zeros((16, 256, D), np.float32)
    y_sample = np.zeros((2, 2048, D), np.float32)
    new_ckv = np.zeros((16, DEPTH, 256, 256), np.float32)
    new_kr = np.zeros((16, DEPTH, 256, 32), np.float32)
    for c in range(8):
        b, j = c // 4, c % 4
        r = res[c]
        y_prompt[2 * c:2 * c + 2] = r["ypT"].T.reshape(2, 256, D)
        y_sample[b, j * 512:(j + 1) * 512] = r["ysT"].T
        new_ckv[2 * c:2 * c + 2] = r["nckvT"].transpose(2, 0, 1).reshape(2, 256, DEPTH, 256).transpose(0, 2, 1, 3)
        new_kr[2 * c:2 * c + 2] = r["nkrT"].transpose(2, 0, 1).reshape(2, 256, DEPTH, 32).transpose(0, 2, 1, 3)
    return (y_prompt, y_sample, new_ckv, new_kr)
```
